# Optimizing a Trainium2 kernel written in Bass

```python
import jax, jax.numpy as jnp
from jax import lax
import numpy as np

D_MODEL = 1024
BATCH = 4
SEQ = 4096
DEPTH = 1

MLA_HEADS = 4
V_HEAD_DIM = D_MODEL // (2 * MLA_HEADS)
QK_NOPE_DIM = V_HEAD_DIM
QK_ROPE_DIM = V_HEAD_DIM // 2
Q_LORA_RANK = D_MODEL // 4
KV_LORA_RANK = D_MODEL // 8
MLA_WIDTH = MLA_HEADS * V_HEAD_DIM
POOL_WIDTH = D_MODEL - MLA_WIDTH
POOL_WINDOWS = (2, 4, 8, 16)
POOL_GROUPS = len(POOL_WINDOWS)
POOL_CH = POOL_WIDTH // POOL_GROUPS
IN_PROJ_DIM = Q_LORA_RANK + KV_LORA_RANK + QK_ROPE_DIM + POOL_WIDTH
D_FF = ((8 * D_MODEL // 3 + 255) // 256) * 256
N_MEM = 256
MEM_HEADS = 4
MEM_HEAD_DIM = D_MODEL // MEM_HEADS
ROPE_BASE = 10000.0
RMS_EPS = 1e-6
BLOCK_Q = 128

kernel_name = "hymba_mla_pool_macaron_memxattn"


def rmsnorm(x, g):
    xf = x.astype(jnp.float32)
    y = xf * lax.rsqrt(jnp.mean(xf * xf, axis=-1, keepdims=True) + RMS_EPS)
    return (y * g.astype(jnp.float32)).astype(x.dtype)


def swiglu(x, w_gate, w_up, w_down):
    return (jax.nn.silu(x @ w_gate) * (x @ w_up)) @ w_down


def rope(x, positions):
    d = x.shape[-1]
    freqs = 1.0 / (ROPE_BASE ** (jnp.arange(0, d, 2, dtype=jnp.float32) / d))
    ang = positions.astype(jnp.float32)[..., None] * freqs
    cos = jnp.cos(ang)[:, :, None, :]
    sin = jnp.sin(ang)[:, :, None, :]
    xf = x.astype(jnp.float32)
    x1, x2 = xf[..., : d // 2], xf[..., d // 2:]
    out = jnp.concatenate([x1 * cos - x2 * sin, x1 * sin + x2 * cos], axis=-1)
    return out.astype(x.dtype)


def causal_block_attention(q, k, v, scale):
    B, S, H, Dqk = q.shape
    nblk = S // BLOCK_Q
    qb = q.reshape(B, nblk, BLOCK_Q, H, Dqk).transpose(1, 0, 2, 3, 4)
    key_pos = jnp.arange(S)

    def one_block(args):
        qi, i = args
        s = jnp.einsum('bqhd,bkhd->bhqk', qi, k).astype(jnp.float32) * scale
        q_pos = i * BLOCK_Q + jnp.arange(BLOCK_Q)
        mask = key_pos[None, :] <= q_pos[:, None]
        s = jnp.where(mask[None, None], s, -jnp.inf)
        p = jax.nn.softmax(s, axis=-1).astype(v.dtype)
        return jnp.einsum('bhqk,bkhd->bqhd', p, v)

    out = lax.map(one_block, (qb, jnp.arange(nblk)))
    return out.transpose(1, 0, 2, 3, 4).reshape(B, S, H, v.shape[-1])


def mla_group(c_q, c_kv, k_rope, positions, q_norm, w_q_up, kv_norm, w_kv_up):
    B, S, _ = c_q.shape
    q = (rmsnorm(c_q, q_norm) @ w_q_up).reshape(B, S, MLA_HEADS, QK_NOPE_DIM + QK_ROPE_DIM)
    q_nope, q_pe = q[..., :QK_NOPE_DIM], q[..., QK_NOPE_DIM:]
    q_pe = rope(q_pe, positions)
    kv = (rmsnorm(c_kv, kv_norm) @ w_kv_up).reshape(B, S, MLA_HEADS, QK_NOPE_DIM + V_HEAD_DIM)
    k_nope, v = kv[..., :QK_NOPE_DIM], kv[..., QK_NOPE_DIM:]
    k_pe = rope(k_rope[:, :, None, :], positions)
    k_pe = jnp.broadcast_to(k_pe, (B, S, MLA_HEADS, QK_ROPE_DIM))
    q_full = jnp.concatenate([q_nope, q_pe], axis=-1)
    k_full = jnp.concatenate([k_nope, k_pe], axis=-1)
    scale = (QK_NOPE_DIM + QK_ROPE_DIM) ** -0.5
    o = causal_block_attention(q_full, k_full, v, scale)
    return o.reshape(B, S, MLA_WIDTH)


def pool_group(z, pool_w, pool_scale):
    B, S, _ = z.shape
    zg = z.reshape(B, S, POOL_GROUPS, POOL_CH)
    cs = jnp.cumsum(zg.astype(jnp.float32), axis=1)
    cs = jnp.pad(cs, ((0, 0), (1, 0), (0, 0), (0, 0)))
    t1 = jnp.arange(1, S + 1, dtype=jnp.float32)
    pooled = []
    for g, w in enumerate(POOL_WINDOWS):
        hi = cs[:, 1:, g]
        lo = jnp.pad(cs[:, : S + 1 - w, g], ((0, 0), (w - 1, 0), (0, 0)))
        count = jnp.minimum(t1, float(w))[None, :, None]
        pooled.append((hi - lo) / count)
    mean = jnp.stack(pooled, axis=2).astype(z.dtype)
    y = jnp.einsum('bsgc,gcd->bsgd', mean - zg, pool_w).reshape(B, S, POOL_WIDTH)
    return y * pool_scale


def memory_cross_attention(hn, memn, w_mq, w_mkv, w_mo):
    B, S, _ = hn.shape
    q = (hn @ w_mq).reshape(B, S, MEM_HEADS, MEM_HEAD_DIM)
    kv = (memn @ w_mkv).reshape(B, N_MEM, 2, MEM_HEADS, MEM_HEAD_DIM)
    k, v = kv[:, :, 0], kv[:, :, 1]
    s = jnp.einsum('bshd,bmhd->bhsm', q, k).astype(jnp.float32) * (MEM_HEAD_DIM ** -0.5)
    p = jax.nn.softmax(s, axis=-1).astype(v.dtype)
    o = jnp.einsum('bhsm,bmhd->bshd', p, v).reshape(B, S, D_MODEL)
    return o @ w_mo


def setup_inputs(seed: int = 0) -> dict:
    key = jax.random.key(seed)
    ks = iter(jax.random.split(key, 32))

    def dense(shape, fan_in, mult=1.0):
        return jax.random.normal(next(ks), shape, jnp.float32) * (mult * fan_in ** -0.5)

    def gain(shape):
        return 1.0 + 0.02 * jax.random.normal(next(ks), shape, jnp.float32)

    L = DEPTH
    x = jax.random.normal(next(ks), (BATCH, SEQ, D_MODEL), jnp.float32)
    mem = jax.random.normal(next(ks), (BATCH, N_MEM, D_MODEL), jnp.float32)
    offset = jax.random.randint(next(ks), (BATCH, 1), 0, 1024, dtype=jnp.int32)
    positions = (jnp.arange(SEQ, dtype=jnp.int32)[None, :] + offset).astype(jnp.int32)
    return {
        "x": x,
        "mem": mem,
        "positions": positions,
        "ffn1_norm": gain((L, D_MODEL)),
        "ffn1_w_gate": dense((L, D_MODEL, D_FF), D_MODEL),
        "ffn1_w_up": dense((L, D_MODEL, D_FF), D_MODEL),
        "ffn1_w_down": dense((L, D_FF, D_MODEL), D_FF),
        "mix_norm": gain((L, D_MODEL)),
        "w_in": dense((L, D_MODEL, IN_PROJ_DIM), D_MODEL),
        "q_norm": gain((L, Q_LORA_RANK)),
        "w_q_up": dense((L, Q_LORA_RANK, MLA_HEADS * (QK_NOPE_DIM + QK_ROPE_DIM)), Q_LORA_RANK),
        "kv_norm": gain((L, KV_LORA_RANK)),
        "w_kv_up": dense((L, KV_LORA_RANK, MLA_HEADS * (QK_NOPE_DIM + V_HEAD_DIM)), KV_LORA_RANK),
        "pool_w": dense((L, POOL_GROUPS, POOL_CH, POOL_CH), POOL_CH),
        "pool_scale": 0.5 + 0.05 * jax.random.normal(next(ks), (L, POOL_WIDTH), jnp.float32),
        "w_out": dense((L, MLA_WIDTH + POOL_WIDTH, D_MODEL), MLA_WIDTH + POOL_WIDTH),
        "xattn_norm": gain((L, D_MODEL)),
        "mem_norm": gain((L, D_MODEL)),
        "w_mq": dense((L, D_MODEL, D_MODEL), D_MODEL),
        "w_mkv": dense((L, D_MODEL, 2 * D_MODEL), D_MODEL),
        "w_mo": dense((L, D_MODEL, D_MODEL), D_MODEL),
        "ffn2_norm": gain((L, D_MODEL)),
        "ffn2_w_gate": dense((L, D_MODEL, D_FF), D_MODEL),
        "ffn2_w_up": dense((L, D_MODEL, D_FF), D_MODEL),
        "ffn2_w_down": dense((L, D_FF, D_MODEL), D_FF),
        "final_norm": gain((D_MODEL,)),
    }


def reference(x, mem, positions, ffn1_norm, ffn1_w_gate, ffn1_w_up, ffn1_w_down, mix_norm, w_in,
              q_norm, w_q_up, kv_norm, w_kv_up, pool_w, pool_scale, w_out, xattn_norm, mem_norm,
              w_mq, w_mkv, w_mo, ffn2_norm, ffn2_w_gate, ffn2_w_up, ffn2_w_down, final_norm):
    h = x
    split_pts = [Q_LORA_RANK, Q_LORA_RANK + KV_LORA_RANK, Q_LORA_RANK + KV_LORA_RANK + QK_ROPE_DIM]
    for l in range(DEPTH):
        h = h + 0.5 * swiglu(rmsnorm(h, ffn1_norm[l]), ffn1_w_gate[l], ffn1_w_up[l], ffn1_w_down[l])
        z = rmsnorm(h, mix_norm[l]) @ w_in[l]
        c_q, c_kv, k_rope, z_pool = jnp.split(z, split_pts, axis=-1)
        a = mla_group(c_q, c_kv, k_rope, positions, q_norm[l], w_q_up[l], kv_norm[l], w_kv_up[l])
        p = pool_group(z_pool, pool_w[l], pool_scale[l])
        h = h + jnp.concatenate([a, p], axis=-1) @ w_out[l]
        h = h + memory_cross_attention(rmsnorm(h, xattn_norm[l]), rmsnorm(mem, mem_norm[l]),
                                       w_mq[l], w_mkv[l], w_mo[l])
        h = h + 0.5 * swiglu(rmsnorm(h, ffn2_norm[l]), ffn2_w_gate[l], ffn2_w_up[l], ffn2_w_down[l])
    return rmsnorm(h, final_norm)
```

```python
import contextlib

import numpy as np
import concourse.bass as bass
import concourse.mybir as mybir
from concourse.bass_utils import run_bass_kernel_spmd

F32 = mybir.dt.float32
BF16 = mybir.dt.bfloat16
I32 = mybir.dt.int32
ALU = mybir.AluOpType
AF = mybir.ActivationFunctionType

T = 2048
NT = 4
D = 1024
DFF = 2816
NF = 22
PASSES = [(0, 6), (6, 12), (12, 17), (17, 22)]
EPS = 1e-6
PI = float(np.float32(np.pi))
NEG = -30000.0

C_FFN1, C_MIX, C_XAT, C_FFN2, C_FIN, C_MEM = 0, 8, 16, 24, 32, 40
C_QN, C_KVN, C_PSC, C_FREQ, C_SGN, C_HALO, C_INVC = 48, 50, 51, 55, 56, 57, 58
C_S2PI, C_SPI = 122, 123
NSC = 124


class Buf:
    __slots__ = ("name", "writer", "readers")

    def __init__(self, name):
        self.name = name
        self.writer = None
        self.readers = []


class Op:
    __slots__ = ("eng", "fn", "deps", "needed", "dma", "slot", "val", "idx")


class Sched:
    ENGS = ("pe", "act", "dve", "pool", "sp")

    def __init__(self):
        self.ops = {e: [] for e in self.ENGS}
        self.nops = 0

    def add(self, eng, fn, reads=(), writes=(), dma=False, slot=None):
        op = Op()
        op.eng, op.fn, op.dma, op.slot = eng, fn, dma, slot
        op.needed = False
        op.val = None
        op.idx = self.nops
        self.nops += 1
        deps = {}
        for b in reads:
            w = b.writer
            if w is not None and not (w.eng == eng and eng == "pe" and not w.dma):
                deps[w.idx] = w
        for b in writes:
            w = b.writer
            if w is not None and not (w.eng == eng and eng == "pe" and not w.dma):
                deps[w.idx] = w
            for r in b.readers:
                if r.eng == eng and eng == "pe" and not r.dma and not dma:
                    continue
                deps[r.idx] = r
        deps.pop(op.idx, None)
        op.deps = list(deps.values())
        for d in op.deps:
            d.needed = True
        for b in reads:
            b.readers.append(op)
        for b in writes:
            b.writer = op
            b.readers = []
        self.ops[eng].append(op)
        return op

    def emit(self, nc, final_waits=()):
        for op in final_waits:
            op.needed = True
        with contextlib.ExitStack() as st:
            esem = {e: st.enter_context(nc.semaphore("sem_" + e)) for e in self.ENGS}
            slot_sem, slot_cnt = {}, {}
            cnt = {e: 0 for e in self.ENGS}
            for e in self.ENGS:
                for op in self.ops[e]:
                    if op.dma:
                        key = op.slot
                        if key not in slot_sem:
                            slot_sem[key] = st.enter_context(nc.semaphore("dsem%d" % len(slot_sem)))
                            slot_cnt[key] = 0
                        slot_cnt[key] += 16
                        op.val = (slot_sem[key], slot_cnt[key])
                    elif op.needed:
                        cnt[e] += 1
                        op.val = (esem[e], cnt[e])
            block = st.enter_context(nc.Block())

            def run(engname, eng, extra=None):
                waited = {}
                for op in self.ops[engname]:
                    for d in op.deps:
                        sem, v = d.val
                        k = id(sem)
                        if waited.get(k, 0) >= v:
                            continue
                        waited[k] = v
                        eng.wait_ge(sem, v)
                    ins = op.fn(eng)
                    if op.dma:
                        ins.then_inc(op.val[0], 16)
                    elif op.needed:
                        ins.then_inc(op.val[0], 1)
                if extra:
                    for op in extra:
                        sem, v = op.val
                        eng.wait_ge(sem, v)

            @block.tensor
            def _(eng):
                run("pe", eng)

            @block.scalar
            def _(eng):
                run("act", eng)

            @block.vector
            def _(eng):
                run("dve", eng)

            @block.gpsimd
            def _(eng):
                run("pool", eng)

            @block.sync
            def _(eng):
                run("sp", eng, extra=final_waits)


def build(stop=None, dumps=()):
    nc = bass.Bass("TRN2", target_bir_lowering=False, dynamic_dma_scratch_size=8192)

    def din(name, shape, dt=F32):
        return nc.dram_tensor(name, list(shape), dt, kind="ExternalInput").ap()

    x_own = din("x_own", [128, 8, T])
    x_oth = din("x_oth", [128, 8, T])
    memT = din("memT", [128, 8, 256])
    pos_own = din("pos_own", [64, T], I32)
    pos_oth = din("pos_oth", [64, T], I32)
    wgu_d = [din("wgu1", [NF, 128, 2048]), din("wgu2", [NF, 128, 2048])]
    wd_d = [din("wd1", [4, 8, 128, 768]), din("wd2", [4, 8, 128, 768])]
    w_in_d = din("w_in", [128, 8192])
    wq_d = din("wq", [128, 2048])
    wkv_d = din("wkv", [128, 1024])
    poolw_d = din("poolw", [128, 512])
    w_out_d = din("w_out", [128, 8192])
    w_mq_d = din("w_mq", [128, 8192])
    w_mo_d = din("w_mo", [128, 8192])
    w_mkv_d = din("w_mkv", [8, 128, 2048])
    tri_d = din("tri", [128, 128])
    smallc_d = din("smallc", [128, NSC])
    kbias_d = din("kbias", [1, 4096])
    outT = nc.dram_tensor("outT", [128, 8, T], F32, kind="ExternalOutput").ap()

    S = Sched()
    bufs = {}

    SCRATCH_KEYS = {"xn", "at", "sq", "rstd", "sg", "un", "zp", "tmp", "dy", "wop", "ct", "st", "posi", "ang", "rs",
                    "t1", "t2", "qnt", "ts16", "kn", "v", "pt", "rc", "oh", "woh", "hn", "qm", "om", "ptm", "rcm",
                    "outb", "memf", "memn", "wm", "accd", "accp", "sumt", "ones32"}
    cur_fence = [None]

    def B(*key):
        b = bufs.get(key)
        if b is None:
            b = bufs[key] = Buf(str(key))
            if key[0] in SCRATCH_KEYS and cur_fence[0] is not None:
                b.writer = cur_fence[0]
        return b

    def fence():
        old = [b for k, b in bufs.items() if k[0] in SCRATCH_KEYS]
        cur_fence[0] = S.add("dve", lambda e: e.engine_nop(), [], old)

    st = contextlib.ExitStack()
    arena = st.enter_context(nc.sbuf_tensor("arena", [128, 208 * 256], F32))
    ps = st.enter_context(nc.psum_tensor("ps", [128, 8, 512], F32))
    A = arena[:]

    def reg(off, nbytes, dt=F32):
        v = A[:, off // 4:(off + nbytes) // 4]
        if dt != F32:
            v = v.bitcast(dt)
        return v

    K = 1024
    H = reg(0, 64 * K).rearrange("p (c t) -> p c t", c=8)
    SCR = 64 * K
    W16 = reg(128 * K, 16 * K, BF16).rearrange("p (c n) -> p c n", c=8)
    KVN = reg(144 * K, 8 * K, BF16)
    KR = reg(152 * K, 8 * K, BF16)
    QNP = reg(160 * K, 16 * K, BF16).rearrange("p (h t) -> p h t", h=4)
    WGUf = [reg(160 * K + i * 4 * K, 4 * K, BF16) for i in range(2)]
    WGU = [w.rearrange("p (g c n) -> p g c n", g=2, c=8) for w in WGUf]
    WDf = [reg(168 * K + i * 1536, 1536, BF16) for i in range(3)]
    WD = [w.rearrange("p (f n) -> p f n", f=6) for w in WDf]
    W16f = reg(128 * K, 16 * K, BF16)
    QPE = reg(176 * K, 16 * K, BF16).rearrange("p (h t) -> p h t", h=4)
    HALO = reg(176 * K - 256, 256).rearrange("p (g t) -> p g t", g=4)
    CB = 192 * K
    WQ = reg(CB, 4096, BF16).rearrange("p (c n) -> p c n", c=2)
    WKV = reg(CB + 4096, 2048, BF16)
    POOLW = reg(CB + 6144, 1024, BF16).rearrange("p (g n) -> p g n", g=4)
    ONES = reg(CB + 7168, 256, BF16)
    TRI = reg(CB + 7424, 256, BF16)
    KM = reg(CB + 7680, 4096, BF16).rearrange("p (c m) -> p c m", c=8)
    VM = reg(CB + 11776, 4096, BF16).rearrange("p (b n) -> p b n", b=2)
    SC = reg(CB + 15872, NSC * 4)

    def scr(off, nbytes, dt=F32):
        assert off + nbytes <= 64 * K, (off, nbytes)
        return reg(SCR + off, nbytes, dt)

    XN = scr(0, 32 * K, BF16).rearrange("p (c t) -> p c t", c=8)
    AT = scr(32 * K, 24 * K, BF16).rearrange("p (f t) -> p f t", f=6)
    SQ = [scr(56 * K + i * K, K, BF16) for i in range(3)]
    RSTD = [scr(59 * K, 2 * K)]
    QNT = scr(61 * K, 2 * K, BF16).rearrange("p (c t) -> p c t", c=2)
    SG = [scr(63 * K, K, BF16)]

    bank_ctr = [0]

    def nb(lo=0, hi=8):
        b = lo + bank_ctr[0] % (hi - lo)
        bank_ctr[0] += 1
        return b

    def mm(out, lhsT, rhs, start, stop, reads, writes):
        return S.add("pe", lambda e: e.matmul(out, lhsT=lhsT, rhs=rhs, start=start, stop=stop), reads, writes)

    def act(out, in_, func, reads, writes, **kw):
        return S.add("act", lambda e: e.activation(out=out, in_=in_, func=func, **kw), reads, writes)

    def tt(out, in0, in1, op, reads, writes, eng="dve"):
        return S.add(eng, lambda e: e.tensor_tensor(out=out, in0=in0, in1=in1, op=op), reads, writes)

    def ts(out, in0, s1, s2, op0, op1, reads, writes, eng="dve"):
        if op1 is None:
            return S.add(eng, lambda e: e.tensor_scalar(out=out, in0=in0, scalar1=s1, scalar2=None, op0=op0), reads, writes)
        return S.add(eng, lambda e: e.tensor_scalar(out=out, in0=in0, scalar1=s1, scalar2=s2, op0=op0, op1=op1), reads, writes)

    def stt(out, in0, scalar, in1, op0, op1, reads, writes, eng="dve"):
        return S.add(eng, lambda e: e.scalar_tensor_tensor(out=out, in0=in0, scalar=scalar, in1=in1, op0=op0, op1=op1), reads, writes)

    def cp(out, in_, reads, writes, eng="dve"):
        if eng == "act":
            return S.add("act", lambda e: e.copy(out=out, in_=in_), reads, writes)
        return S.add(eng, lambda e: e.tensor_copy(out=out, in_=in_), reads, writes)

    def dma(q, out, in_, reads, writes, slot):
        return S.add(q, lambda e: e.dma_start(out=out, in_=in_), reads, writes, dma=True, slot=slot)

    BPS = [B("ps", i) for i in range(8)]
    BC = B("c_sc")
    BONES = B("c_ones")
    BTRI = B("c_tri")

    sq_ctr = [0]

    def emit_norm(srcs, src_bufs, N, gcol, dtot, dsts, dst_bufs):
        C = len(srcs)
        b = nb()
        for c in range(C):
            sl = sq_ctr[0] % 3
            sq_ctr[0] += 1
            act(SQ[sl][:, 0:N], srcs[c], AF.Square, [src_bufs[c]], [B("sq", sl)])
            mm(ps[:, b, 0:N], ONES, SQ[sl][:, 0:N], c == 0, c == C - 1, [B("sq", sl), BONES], [BPS[b]])
        act(RSTD[0][:, 0:N], ps[:, b, 0:N], AF.Ln, [BPS[b]], [B("rstd", 0)], scale=1.0 / dtot, bias=EPS)
        act(RSTD[0][:, 0:N], RSTD[0][:, 0:N], AF.Exp, [B("rstd", 0)], [B("rstd", 0)], scale=-0.5)
        for c in range(C):
            stt(dsts[c], srcs[c], SC[:, gcol + c:gcol + c + 1], RSTD[0][:, 0:N], ALU.mult, ALU.mult,
                [src_bufs[c], B("rstd", 0), BC], [dst_bufs[c]])

    def tsl(t):
        return slice(t * 512, (t + 1) * 512)

    def BH(t, m):
        return B("H", t, m)

    dma("sp", SC, smallc_d, [], [BC], "c_sc")
    S.add("dve", lambda e: e.memset(ONES, 1.0), [], [BONES])
    dma("pool", TRI, tri_d, [], [BTRI], "c_tri")
    dma("pool", reg(CB, 4096, BF16), wq_d, [], [B("wq")], "c_wq")
    dma("pool", WKV, wkv_d, [], [B("wkv")], "c_wkv")
    dma("pool", reg(CB + 6144, 1024, BF16), poolw_d, [], [B("poolw")], "c_pw")
    dma("pool", KR[64:65, 0:2048], kbias_d[:, 0:2048], [], [B("krb")], "c_kb")
    dma("pool", KR[64:65, 2048:4096], kbias_d[:, 2048:4096], [], [B("krb")], "c_kb")
    for h in range(4):
        S.add("dve", lambda e, h=h: e.memset(QPE[64:65, h, :], 1.0), [], [B("qpeb")])

    wgu_ctr = [0]
    wd_ctr = [0]

    def emit_ffn(which, gcol, x_dram=None):
        for t in range(NT):
            if x_dram is not None:
                dma("sp", H[:, :, tsl(t)], x_dram[:, :, tsl(t)], [], [BH(t, m) for m in range(8)], ("H", t))
            emit_norm([H[:, c, tsl(t)] for c in range(8)], [BH(t, c) for c in range(8)], 512, gcol, D,
                      [XN[:, c, tsl(t)] for c in range(8)], [B("xn", t)] * 8)
        for j, (f0, f1) in enumerate(PASSES):
            nf = f1 - f0
            for f in range(f0, f1):
                sl = wgu_ctr[0] % 2
                wgu_ctr[0] += 1
                dma("pool", WGUf[sl], wgu_d[which][f], [], [B("wgu", sl)], ("wgu", sl))
                for t in range(NT):
                    bg = nb()
                    for kc in range(8):
                        mm(ps[:, bg, :], WGU[sl][:, 0, kc, :], XN[:, kc, tsl(t)], kc == 0, kc == 7,
                           [B("wgu", sl), B("xn", t)], [BPS[bg]])
                    bu = nb()
                    for kc in range(8):
                        mm(ps[:, bu, :], WGU[sl][:, 1, kc, :], XN[:, kc, tsl(t)], kc == 0, kc == 7,
                           [B("wgu", sl), B("xn", t)], [BPS[bu]])
                    act(SG[0], ps[:, bg, :], AF.Silu, [BPS[bg]], [B("sg")])
                    tt(AT[:, f - f0, tsl(t)], ps[:, bu, :], SG[0], ALU.mult, [BPS[bu], B("sg")], [B("at", f - f0, t)])
            for m in range(8):
                sl = wd_ctr[0] % 3
                wd_ctr[0] += 1
                dma("pool", WDf[sl][:, 0:nf * 128], wd_d[which][j, m][:, 0:nf * 128], [], [B("wd", sl)], ("wd", sl))
                for t in range(NT):
                    b = nb()
                    for fl in range(nf):
                        mm(ps[:, b, :], WD[sl][:, fl, :], AT[:, fl, tsl(t)], fl == 0, fl == nf - 1,
                           [B("wd", sl), B("at", fl, t)], [BPS[b]])
                    stt(H[:, m, tsl(t)], ps[:, b, :], 0.5, H[:, m, tsl(t)], ALU.mult, ALU.add,
                        [BPS[b], BH(t, m)], [BH(t, m)])

    UN = [scr(i * 8 * K, 8 * K, BF16).rearrange("p (c t) -> p c t", c=8) for i in range(2)]
    ZP = scr(16 * K, 4 * 528 * 4).rearrange("p (g t) -> p g t", g=4)
    o = 16 * K + 8448
    TMP = [scr(o + i * 2112, 2112) for i in range(2)]
    o += 2 * 2112
    DY = scr(o, 4 * K, BF16).rearrange("p (g t) -> p g t", g=4)
    o += 4 * K
    WOPf = scr(o, 8 * K, BF16)
    WOP = WOPf.rearrange("p (c n) -> p c n", c=4)
    o += 8 * K
    CT = scr(o, 2 * K)
    STb = scr(o + 2 * K, 2 * K)
    o += 4 * K
    POSI = scr(o, 2 * K, I32)
    ANG = scr(o + 2 * K, 2 * K)
    RS = scr(o + 4 * K, 2 * K)
    o += 6 * K
    T1 = scr(o, 2 * K)
    T2 = ANG
    o += 2 * K
    RSTD.append(scr(o, 2 * K))
    o += 2 * K
    TS16 = scr(o, 64)
    o += 64
    assert o <= 56 * K, o

    class PA:
        def __init__(self):
            self.free = list(range(8))

        def alloc(self):
            assert self.free, "PSUM banks exhausted"
            return self.free.pop(0)

        def release(self, b):
            self.free.append(b)

    pa = PA()

    def norm_g(srcs, src_bufs, N, gcol, dtot, dsts, dst_bufs, ri, rel=()):
        C = len(srcs)
        b = pa.alloc()
        for c in range(C):
            sl = sq_ctr[0] % 3
            sq_ctr[0] += 1
            act(SQ[sl][:, 0:N], srcs[c], AF.Square, [src_bufs[c]], [B("sq", sl)])
            mm(ps[:, b, 0:N], ONES, SQ[sl][:, 0:N], c == 0, c == C - 1, [B("sq", sl), BONES], [BPS[b]])
            yield
        rb = B("rstd", ri)
        act(RSTD[ri][:, 0:N], ps[:, b, 0:N], AF.Ln, [BPS[b]], [rb], scale=1.0 / dtot, bias=EPS)
        act(RSTD[ri][:, 0:N], RSTD[ri][:, 0:N], AF.Exp, [rb], [rb], scale=-0.5)
        pa.release(b)
        yield
        for c in range(C):
            stt(dsts[c], srcs[c], SC[:, gcol + c:gcol + c + 1], RSTD[ri][:, 0:N], ALU.mult, ALU.mult,
                [src_bufs[c], rb, BC], [dst_bufs[c]])
            yield
        for b_ in rel:
            pa.release(b_)

    def tables_g(pos_d, t):
        dma("sp", POSI[0:64, :], pos_d[:, tsl(t)], [], [B("posi")], "posi")
        cp(ANG[0:64, :], POSI[0:64, :], [B("posi")], [B("ang")])
        ts(ANG[0:64, :], ANG[0:64, :], SC[0:64, C_FREQ:C_FREQ + 1], None, ALU.mult, None, [B("ang"), BC], [B("ang")])
        yield
        KI = POSI
        for (dst, db, off, use_sgn) in ((STb, "st", 0.5, True), (CT, "ct", 0.75, False)):
            ts(RS[0:64, :], ANG[0:64, :], 1.0 / (2.0 * PI), off, ALU.mult, ALU.add, [B("ang")], [B("rs")])
            cp(KI[0:64, :], RS[0:64, :], [B("rs")], [B("posi")])
            yield
            cp(T1[0:64, :], KI[0:64, :], [B("posi")], [B("t1")])
            tt(RS[0:64, :], RS[0:64, :], T1[0:64, :], ALU.subtract, [B("rs"), B("t1")], [B("rs")])
            yield
            stt(RS[0:64, :], RS[0:64, :], 0.0, RS[0:64, :], ALU.is_lt, ALU.add, [B("rs")], [B("rs")])
            if use_sgn:
                act(dst[0:64, :], RS[0:64, :], AF.Sin, [B("rs"), BC], [B(db)],
                    scale=SC[0:64, C_S2PI:C_S2PI + 1], bias=SC[0:64, C_SPI:C_SPI + 1])
            else:
                act(dst[0:64, :], RS[0:64, :], AF.Sin, [B("rs")], [B(db)], scale=2.0 * PI, bias=-PI)
            yield

    def rope_ops(za_b, zb_b, dst, dst_buf):
        tt(T1[0:64, :], ps[0:64, za_b, :], CT[0:64, :], ALU.mult, [BPS[za_b], B("ct")], [B("t1")])
        tt(T2[0:64, :], ps[0:64, zb_b, :], STb[0:64, :], ALU.mult, [BPS[zb_b], B("st")], [B("ang")])
        pa.release(za_b)
        pa.release(zb_b)
        tt(dst, T1[0:64, :], T2[0:64, :], ALU.add, [B("t1"), B("ang")], [dst_buf], eng="pool")

    def proj_t(t, c0, c1, part=128):
        u, ub = UN[t % 2], B("un", t % 2)
        b = pa.alloc()
        for kc in range(8):
            mm(ps[0:part, b, :], W16[:, kc, c0:c1], u[:, kc, :], kc == 0, kc == 7, [B("w16"), ub], [BPS[b]])
        return b

    def p1_g(t):
        u, ub = UN[t % 2], B("un", t % 2)
        yield from norm_g([H[:, c, tsl(t)] for c in range(8)], [BH(t, c) for c in range(8)], 512, C_MIX, D,
                          [u[:, c, :] for c in range(8)], [ub] * 8, 0)

    def p2_g(t, own):
        kt = (4 if own else 0) + t
        ksl = slice(kt * 512, (kt + 1) * 512)
        tg = tables_g(pos_own if own else pos_oth, t)
        next(tg)
        b = proj_t(t, 256, 384)
        yield
        ng = norm_g([ps[:, b, :]], [BPS[b]], 512, C_KVN, 128, [KVN[:, ksl]], [B("kvn", kt)], 1, rel=(b,))
        for _ in ng:
            next(tg, None)
            yield
        for _ in tg:
            yield
        ba = proj_t(t, 384, 448, 64)
        yield
        bb = proj_t(t, 960, 1024, 64)
        yield
        if own:
            bq = [proj_t(t, 0, 128)]
            yield
            bq.append(proj_t(t, 128, 256))
            yield
        rope_ops(ba, bb, KR[0:64, ksl], B("kr", kt))
        yield
        if own:
            yield from norm_g([ps[:, bq[0], :], ps[:, bq[1], :]], [BPS[bq[0]], BPS[bq[1]]], 512, C_QN, 256,
                              [QNT[:, 0, :], QNT[:, 1, :]], [B("qnt")] * 2, 1, rel=tuple(bq))
            for h in range(4):
                b = pa.alloc()
                for kc in range(2):
                    mm(ps[:, b, :], WQ[:, kc, h * 256:h * 256 + 128], QNT[:, kc, :], kc == 0, kc == 1, [B("wq"), B("qnt")], [BPS[b]])
                ba = pa.alloc()
                for kc in range(2):
                    mm(ps[0:64, ba, :], WQ[:, kc, h * 256 + 128:h * 256 + 192], QNT[:, kc, :], kc == 0, kc == 1, [B("wq"), B("qnt")], [BPS[ba]])
                bb = pa.alloc()
                for kc in range(2):
                    mm(ps[0:64, bb, :], WQ[:, kc, h * 256 + 192:h * 256 + 256], QNT[:, kc, :], kc == 0, kc == 1, [B("wq"), B("qnt")], [BPS[bb]])
                yield
                cp(QNP[:, h, tsl(t)], ps[:, b, :], [BPS[b]], [B("qnp", h, t)], eng="act")
                pa.release(b)
                rope_ops(ba, bb, QPE[0:64, h, tsl(t)], B("qpe", h, t))
                yield

    def p3_g(t, own):
        for g in range(4):
            b = proj_t(t, 448 + g * 128, 576 + g * 128)
            yield
            if not own:
                ts(HALO[:, g, :], ps[:, b, 496:512], SC[:, C_HALO:C_HALO + 1], None, ALU.mult, None,
                   [BPS[b], BC], [B("halo")])
                pa.release(b)
                continue
            if t == 0:
                cp(ZP[:, g, 0:16], HALO[:, g, :], [B("halo"), B("qnp", 3, 3)], [B("zp", g)], eng="pool")
            cp(ZP[:, g, 16:528], ps[:, b, :], [BPS[b]], [B("zp", g)], eng="act")
            pa.release(b)
            yield
            cur, curb = ZP[:, g, :], B("zp", g)
            lo = 0
            for l in range(g + 1):
                step = 1 << l
                lo += step
                dst, dstb = TMP[l % 2], B("tmp", l % 2)
                tt(dst[:, lo:528], cur[:, lo:528], cur[:, lo - step:528 - step], ALU.add, [curb], [dstb], eng="pool")
                cur, curb = dst, dstb
                yield
            w = 1 << (g + 1)
            if t == 0:
                tt(TS16[:, 0:16], cur[:, 16:32], SC[:, C_INVC + g * 16:C_INVC + (g + 1) * 16], ALU.mult, [curb, BC], [B("ts16")], eng="pool")
            stt(DY[:, g, :], cur[:, 16:528], 1.0 / w, ZP[:, g, 16:528], ALU.mult, ALU.subtract,
                [curb, B("zp", g)], [B("dy", g)])
            if t == 0:
                tt(DY[:, g, 0:16], TS16[:, 0:16], ZP[:, g, 16:32], ALU.subtract, [B("ts16"), B("zp", g)], [B("dy", g)], eng="pool")
            if t < NT - 1:
                cp(ZP[:, g, 0:16], ZP[:, g, 512:528], [B("zp", g)], [B("zp", g)], eng="pool")
            yield
            b2 = pa.alloc()
            mm(ps[:, b2, :], POOLW[:, g, :], DY[:, g, :], True, True, [B("poolw"), B("dy", g)], [BPS[b2]])
            ts(DY[:, g, :], ps[:, b2, :], SC[:, C_PSC + g:C_PSC + g + 1], None, ALU.mult, None, [BPS[b2], BC], [B("dy", g)])
            pa.release(b2)
            yield
        if own:
            for m in range(8):
                b = pa.alloc()
                for g in range(4):
                    mm(ps[:, b, :], WOP[:, g, m * 128:(m + 1) * 128], DY[:, g, :], g == 0, g == 3, [B("wop"), B("dy", g)], [BPS[b]])
                tt(H[:, m, tsl(t)], ps[:, b, :], H[:, m, tsl(t)], ALU.add, [BPS[b], BH(t, m)], [BH(t, m)])
                pa.release(b)
                yield

    def interleave(*gens):
        gens = [g for g in gens if g is not None]
        while gens:
            for g in list(gens):
                try:
                    next(g)
                except StopIteration:
                    gens.remove(g)

    MEMF = scr(16 * K, 8 * K).rearrange("p (c m) -> p c m", c=8)
    MEMN = scr(24 * K, 4 * K, BF16).rearrange("p (c m) -> p c m", c=8)
    WMf = [scr(28 * K + i * 4 * K, 4 * K, BF16) for i in range(2)]
    WM = [w.rearrange("p (c n) -> p c n", c=8) for w in WMf]
    RSTD.append(scr(36 * K, K))
    RI_P0 = len(RSTD) - 1

    def p0_g():
        dma("sp", MEMF, memT, [], [B("memf")], "memf")
        yield from norm_g([MEMF[:, c, :] for c in range(8)], [B("memf")] * 8, 256, C_MEM, D,
                          [MEMN[:, c, :] for c in range(8)], [B("memn")] * 8, RI_P0)
        for cg in range(8):
            sl = cg % 2
            dma("pool", WMf[sl], w_mkv_d[cg], [], [B("wm", sl)], ("wm", sl))
            for half in range(2):
                b = pa.alloc()
                if cg < 4:
                    c = cg * 2 + half
                    for kc in range(8):
                        mm(ps[:, b, 0:256], WM[sl][:, kc, half * 128:(half + 1) * 128], MEMN[:, kc, :], kc == 0, kc == 7,
                           [B("wm", sl), B("memn")], [BPS[b]])
                    cp(KM[:, c, :], ps[:, b, 0:256], [BPS[b]], [B("km")])
                else:
                    for kc in range(8):
                        mm(ps[:, b, 0:256], MEMN[:, kc, half * 128:(half + 1) * 128], WM[sl][:, kc, :], kc == 0, kc == 7,
                           [B("wm", sl), B("memn")], [BPS[b]])
                    cp(VM[:, half, (cg - 4) * 256:(cg - 3) * 256], ps[:, b, 0:256], [BPS[b]], [B("vm")])
                pa.release(b)
                yield

    def take_g(it, n):
        for _ in range(n):
            try:
                next(it)
            except StopIteration:
                return
            yield

    def emit_inproj_phase(own):
        extra = None if own else p0_g()
        interleave(p1_g(0))
        for t in range(NT):
            streams = [p2_g(t, own)]
            if own or t == NT - 1:
                streams.append(p3_g(t, own))
            if t + 1 < NT:
                streams.append(p1_g(t + 1))
            if extra is not None:
                streams.append(take_g(extra, 12))
            interleave(*streams)
        if extra is not None:
            interleave(extra)

    fence()
    emit_ffn(0, C_FFN1, x_oth)
    fence()
    dma("pool", W16f, w_in_d, [], [B("w16")], "w16")
    emit_inproj_phase(False)
    if stop != "A":
        fence()
        emit_ffn(0, C_FFN1, x_own)
    if stop not in ("A", "B1"):
        fence()
        dma("pool", WOPf, w_out_d[:, 4096:8192], [], [B("wop")], "wop")
        emit_inproj_phase(True)

    if stop not in ("A", "B1", "B2"):
        fence()
        KNb = [scr(i * 8 * K, 8 * K, BF16) for i in range(2)]
        Vb = [scr(16 * K + i * 8 * K, 8 * K, BF16).rearrange("p (b n) -> p b n", b=32) for i in range(2)]
        PT = [scr(32 * K + i * K, K, BF16) for i in range(4)] + [scr(56 * K + i * K, K, BF16) for i in range(4)]
        NPT = len(PT)
        ACCV = {"dve": scr(52 * K, 2 * K), "pool": scr(54 * K, 2 * K)}
        ACCB = {"dve": "accd", "pool": "accp"}
        ONES32 = scr(60 * K, 512)
        SUMT = scr(61 * K, 2 * K)
        S.add("dve", lambda e: e.memset(ONES32, 1.0), [], [B("ones32")])
        RC = [scr(36 * K + i * 2 * K, 2 * K) for i in range(2)]
        OH = [scr(40 * K + i * 4 * K, 4 * K, BF16) for i in range(2)]
        WOH = [scr(48 * K + i * 2 * K, 2 * K, BF16) for i in range(2)]
        sc_attn = float(192.0 ** -0.5)

        def emit_upproj(h):
            hs = h % 2
            dma("pool", WOH[hs], w_out_d[:, h * 1024:(h + 1) * 1024], [], [B("woh", hs)], ("woh", hs))
            for kt in range(8):
                b = nb(0, 4)
                mm(ps[:, b, :], WKV[:, h * 256:h * 256 + 128], KVN[:, kt * 512:(kt + 1) * 512], True, True,
                   [B("wkv"), B("kvn", kt)], [BPS[b]])
                cp(KNb[hs][:, kt * 512:(kt + 1) * 512], ps[:, b, :], [BPS[b]], [B("kn", hs, kt)])
            for vb in range(8):
                b = nb(0, 4)
                for j in range(4):
                    kb = vb * 4 + j
                    mm(ps[:, b, j * 128:(j + 1) * 128], KVN[:, kb * 128:(kb + 1) * 128], WKV[:, h * 256 + 128:h * 256 + 256],
                       True, True, [B("wkv"), B("kvn", kb // 4)], [BPS[b]])
                cp(Vb[hs][:, vb * 4:(vb + 1) * 4, :], ps[:, b, :].rearrange("p (j n) -> p j n", j=4), [BPS[b]], [B("v", hs, vb)])

        pending = []

        def flush():
            while pending:
                pending.pop(0)()

        def emit_wout(h, t):
            hs = h % 2
            for m in range(8):
                b = nb(0, 4)
                mm(ps[:, b, :], WOH[hs][:, m * 128:(m + 1) * 128], OH[hs][:, tsl(t)], True, True,
                   [B("woh", hs), B("oh", hs, t)], [BPS[b]])
                tt(H[:, m, tsl(t)], ps[:, b, :], H[:, m, tsl(t)], ALU.add, [BPS[b], BH(t, m)], [BH(t, m)])

        emit_upproj(0)
        units_all = []
        acc = 0
        for h in range(4):
            for t in range(NT):
                ul = [(kb, 0, False) for kb in range(16)]
                for j in range(4 * t + 4):
                    ul.append((16 + j, max(0, j - 4 * t), j >= 4 * t))
                for idx, (kb, r, diag) in enumerate(ul):
                    units_all.append(dict(h=h, t=t, idx=idx, kb=kb, r=r, diag=diag, first=(idx == 0),
                                          last=(idx == len(ul) - 1), acc=acc))
                acc += 1
        pt_ctr = [0]

        def stage_a(u):
            h, t, kb = u["h"], u["t"], u["kb"]
            hs = h % 2
            if u["idx"] == 4:
                flush()
                if t == 1 and h < 3:
                    emit_upproj(h + 1)
            n0 = u["r"] * 128
            b = nb(0, 4)
            pi = pt_ctr[0] % NPT
            pt_ctr[0] += 1
            u["pi"] = pi
            qs = slice(t * 512 + n0, (t + 1) * 512)
            mm(ps[:, b, n0:512], KNb[hs][:, kb * 128:(kb + 1) * 128], QNP[:, h, qs], True, False,
               [B("kn", hs, kb // 4), B("qnp", h, t)], [BPS[b]])
            mm(ps[:, b, n0:512], KR[0:65, kb * 128:(kb + 1) * 128], QPE[0:65, h, qs], False, True,
               [B("kr", kb // 4), B("krb"), B("qpe", h, t), B("qpeb")], [BPS[b]])
            act(PT[pi][:, n0:512], ps[:, b, n0:512], AF.Exp, [BPS[b]], [B("pt", pi)], scale=sc_attn)
            if u["diag"]:
                tt(PT[pi][:, n0:n0 + 128], PT[pi][:, n0:n0 + 128], TRI, ALU.mult, [B("pt", pi), BTRI], [B("pt", pi)])

        def stage_b(u):
            h, t, kb, pi = u["h"], u["t"], u["kb"], u["pi"]
            hs = h % 2
            n0 = u["r"] * 128
            ob, sb, rcb = 4 + u["acc"] % 2, 6 + u["acc"] % 2, u["acc"] % 2
            mm(ps[:, ob, n0:512], Vb[hs][:, kb, :], PT[pi][:, n0:512], u["first"], u["last"], [B("v", hs, kb // 4), B("pt", pi)], [BPS[ob]])
            en = "dve" if u["idx"] % 2 == 0 else "pool"
            av, ab = ACCV[en], B(ACCB[en])
            if u["idx"] < 2:
                cp(av, PT[pi], [B("pt", pi)], [ab], eng=en)
            else:
                tt(av[:, n0:512], av[:, n0:512], PT[pi][:, n0:512], ALU.add, [ab, B("pt", pi)], [ab], eng=en)
            if u["last"]:
                tt(SUMT, ACCV["dve"], ACCV["pool"], ALU.add, [B("accd"), B("accp")], [B("sumt")])
                mm(ps[:, sb, :], ONES32, SUMT, True, True, [B("ones32"), B("sumt")], [BPS[sb]])
                act(RC[rcb], ps[:, sb, :], AF.Ln, [BPS[sb]], [B("rc", rcb)])
                act(RC[rcb], RC[rcb], AF.Exp, [B("rc", rcb)], [B("rc", rcb)], scale=-1.0)
                tt(OH[hs][:, tsl(t)], ps[:, ob, :], RC[rcb], ALU.mult, [BPS[ob], B("rc", rcb)], [B("oh", hs, t)])
                pending.append(lambda h=h, t=t: emit_wout(h, t))

        LOOK = 2
        for i in range(len(units_all) + LOOK):
            if i < len(units_all):
                stage_a(units_all[i])
            if i >= LOOK:
                stage_b(units_all[i - LOOK])
        flush()

    if stop not in ("A", "B1", "B2", "B3"):
        fence()
        WMOf = reg(144 * K, 16 * K, BF16)
        WMO = WMOf.rearrange("p (c n) -> p c n", c=8)
        kv_dead = [B("kvn", kt) for kt in range(8)] + [B("kr", kt) for kt in range(8)] + [B("krb")]
        dma("pool", W16f, w_mq_d, [], [B("w16")], "w16")
        dma("pool", WMOf, w_mo_d, [], kv_dead, "wmo")
        HN = [scr(i * 8 * K, 8 * K, BF16).rearrange("p (c t) -> p c t", c=8) for i in range(2)]
        QM = [scr(16 * K + i * 8 * K, 8 * K, BF16).rearrange("p (c t) -> p c t", c=8) for i in range(2)]
        OM = [scr(32 * K + i * 8 * K, 8 * K, BF16).rearrange("p (c t) -> p c t", c=8) for i in range(2)]
        PTm = [scr(48 * K + i * K, K, BF16) for i in range(4)]
        RCm = [scr(52 * K + i * 2 * K, 2 * K) for i in range(2)]
        sc_mem = float(256.0 ** -0.5)

        def xa_norm_g(t):
            hn, hb = HN[t % 2], B("hn", t % 2)
            yield from norm_g([H[:, c, tsl(t)] for c in range(8)], [BH(t, c) for c in range(8)], 512, C_XAT, D,
                              [hn[:, c, :] for c in range(8)], [hb] * 8, 0)

        def xa_q_g(t):
            hn, hb = HN[t % 2], B("hn", t % 2)
            for c in range(8):
                b = pa.alloc()
                for kc in range(8):
                    mm(ps[:, b, :], W16[:, kc, c * 128:(c + 1) * 128], hn[:, kc, :], kc == 0, kc == 7, [B("w16"), hb], [BPS[b]])
                cp(QM[t % 2][:, c, :], ps[:, b, :], [BPS[b]], [B("qm", t % 2, c)], eng=("act" if c % 2 else "dve"))
                pa.release(b)
                yield

        def xa_heads_g(t):
            qm, om = QM[t % 2], OM[t % 2]

            def stage_a(h):
                pis = []
                for mb in range(2):
                    b = pa.alloc()
                    for dc in range(2):
                        mm(ps[:, b, :], KM[:, 2 * h + dc, mb * 128:(mb + 1) * 128], qm[:, 2 * h + dc, :], dc == 0, dc == 1,
                           [B("km"), B("qm", t % 2, 2 * h + dc)], [BPS[b]])
                    pi = (2 * h + mb) % 4
                    pis.append(pi)
                    act(PTm[pi], ps[:, b, :], AF.Exp, [BPS[b]], [B("ptm", pi)], scale=sc_mem)
                    pa.release(b)
                return pis

            pis = {0: stage_a(0)}
            yield
            for h in range(4):
                if h < 3:
                    pis[h + 1] = stage_a(h + 1)
                    yield
                p = pis[h]
                b = pa.alloc()
                for mb in range(2):
                    mm(ps[:, b, :], ONES, PTm[p[mb]], mb == 0, mb == 1, [BONES, B("ptm", p[mb])], [BPS[b]])
                rcb = h % 2
                act(RCm[rcb], ps[:, b, :], AF.Ln, [BPS[b]], [B("rcm", rcb)])
                act(RCm[rcb], RCm[rcb], AF.Exp, [B("rcm", rcb)], [B("rcm", rcb)], scale=-1.0)
                pa.release(b)
                for dvc in range(2):
                    b = pa.alloc()
                    for mb in range(2):
                        mm(ps[:, b, :], VM[:, mb, h * 256 + dvc * 128:h * 256 + (dvc + 1) * 128], PTm[p[mb]], mb == 0, mb == 1,
                           [B("vm"), B("ptm", p[mb])], [BPS[b]])
                    tt(om[:, 2 * h + dvc, :], ps[:, b, :], RCm[rcb], ALU.mult, [BPS[b], B("rcm", rcb)], [B("om", t % 2, 2 * h + dvc)])
                    pa.release(b)
                yield
            for m in range(8):
                b = pa.alloc()
                for kc in range(8):
                    mm(ps[:, b, :], WMO[:, kc, m * 128:(m + 1) * 128], om[:, kc, :], kc == 0, kc == 7,
                       [kv_dead[0], B("om", t % 2, kc)], [BPS[b]])
                tt(H[:, m, tsl(t)], ps[:, b, :], H[:, m, tsl(t)], ALU.add, [BPS[b], BH(t, m)], [BH(t, m)])
                pa.release(b)
                yield

        def chain_g(*gs):
            for g_ in gs:
                if g_ is not None:
                    yield from g_

        interleave(xa_norm_g(0))
        interleave(xa_q_g(0), xa_norm_g(1))
        for t in range(NT):
            interleave(xa_heads_g(t),
                       chain_g(xa_norm_g(t + 2) if t + 2 < NT else None, xa_q_g(t + 1) if t + 1 < NT else None))

    finals = []
    if stop is None:
        S.add("dve", lambda e: e.engine_nop(), [],
              [B("qnp", h, t) for h in range(4) for t in range(NT)] + [B("wgu", sl) for sl in range(2)] + [B("wd", sl) for sl in range(3)])
        fence()
        emit_ffn(1, C_FFN2)
        fence()
        OUTB = [scr(i * 16 * K, 16 * K).rearrange("p (c t) -> p c t", c=8) for i in range(2)]
        for t in range(NT):
            ob_, obb = OUTB[t % 2], B("outb", t % 2)
            emit_norm([H[:, c, tsl(t)] for c in range(8)], [BH(t, c) for c in range(8)], 512, C_FIN, D,
                      [ob_[:, c, :] for c in range(8)], [obb] * 8)
            finals.append(dma("sp", outT[:, :, tsl(t)], ob_, [obb], [], ("out", t % 2)))

    dump_specs = {
        "H": (H, [128, 8, T], F32),
        "KVN": (KVN, [128, 4096], BF16),
        "KR": (KR[0:65, :], [65, 4096], BF16),
        "QNP": (QNP, [128, 4, T], BF16),
        "QPE": (QPE[0:65], [65, 4, T], BF16),
        "KM": (KM, [128, 8, 256], BF16),
        "VM": (VM, [128, 2, 1024], BF16),
        "ZPH": (ZP[:, :, 0:16], [128, 4, 16], F32),
    }
    for name in dumps:
        ap, shape, dt = dump_specs[name]
        d = nc.dram_tensor("dbg_" + name, shape, dt, kind="ExternalOutput").ap()
        allb = list(bufs.values())
        finals.append(dma("sp", d, ap, allb, [], ("dbg", name)))

    S.emit(nc, final_waits=finals)
    st.close()
    return nc


def _fm(w):
    k, n = w.shape
    return np.ascontiguousarray(w.reshape(k // 128, 128, n).transpose(1, 0, 2)).reshape(128, -1)


def prepare_inputs(inp):
    f32 = np.float32
    x = np.asarray(inp["x"], f32)
    mem = np.asarray(inp["mem"], f32)
    pos = np.asarray(inp["positions"], np.int32)
    g = lambda k: np.asarray(inp[k], f32)

    def ffn_layout(wg, wu, wd):
        wg_r = wg.reshape(8, 128, NF, 128).transpose(2, 1, 0, 3)
        wu_r = wu.reshape(8, 128, NF, 128).transpose(2, 1, 0, 3)
        wgu = np.ascontiguousarray(np.stack([wg_r, wu_r], axis=2)).reshape(NF, 128, 2048)
        wd_r = wd.reshape(NF, 128, 8, 128)
        wdl = np.zeros((4, 8, 128, 6, 128), f32)
        for j, (f0, f1) in enumerate(PASSES):
            wdl[j, :, :, 0:f1 - f0, :] = wd_r[f0:f1].transpose(2, 1, 0, 3)
        return wgu, wdl.reshape(4, 8, 128, 768)

    wgu1, wd1 = ffn_layout(g("ffn1_w_gate")[0], g("ffn1_w_up")[0], g("ffn1_w_down")[0])
    wgu2, wd2 = ffn_layout(g("ffn2_w_gate")[0], g("ffn2_w_up")[0], g("ffn2_w_down")[0])
    w_in = g("w_in")[0]
    w_in_ext = np.concatenate([w_in, w_in[:, 416:448], w_in[:, 384:416]], axis=1)
    wqu = g("w_q_up")[0]
    wq_ext = np.zeros((256, 1024), f32)
    for h in range(4):
        o = h * 192
        wq_ext[:, h * 256:h * 256 + 192] = wqu[:, o:o + 192]
        wq_ext[:, h * 256 + 192:h * 256 + 224] = wqu[:, o + 160:o + 192]
        wq_ext[:, h * 256 + 224:h * 256 + 256] = wqu[:, o + 128:o + 160]
    w_mkv = g("w_mkv")[0]
    w_mkv_l = np.stack([_fm(w_mkv[:, cg * 256:(cg + 1) * 256]) for cg in range(8)])
    tri = (np.arange(128)[None, :] >= np.arange(128)[:, None]).astype(f32)
    freq = (1.0 / (np.float32(10000.0) ** (np.arange(0, 64, 2, dtype=f32) / np.float32(64)))).astype(f32)

    shared = {
        "wgu1": wgu1, "wd1": wd1, "wgu2": wgu2, "wd2": wd2,
        "w_in": _fm(w_in_ext), "wq": _fm(wq_ext), "wkv": np.ascontiguousarray(g("w_kv_up")[0]),
        "poolw": np.ascontiguousarray(g("pool_w")[0].transpose(1, 0, 2)).reshape(128, 512),
        "w_out": _fm(g("w_out")[0]), "w_mq": _fm(g("w_mq")[0]), "w_mo": _fm(g("w_mo")[0]),
        "w_mkv": w_mkv_l, "tri": tri,
    }
    sc0 = np.zeros((128, NSC), f32)
    for col, key in ((C_FFN1, "ffn1_norm"), (C_MIX, "mix_norm"), (C_XAT, "xattn_norm"), (C_FFN2, "ffn2_norm"), (C_MEM, "mem_norm")):
        sc0[:, col:col + 8] = g(key)[0].reshape(8, 128).T
    sc0[:, C_FIN:C_FIN + 8] = g("final_norm").reshape(8, 128).T
    sc0[:, C_QN:C_QN + 2] = g("q_norm")[0].reshape(2, 128).T
    sc0[:, C_KVN] = g("kv_norm")[0]
    sc0[:, C_PSC:C_PSC + 4] = g("pool_scale")[0].reshape(4, 128).T
    sc0[0:32, C_FREQ] = freq
    sc0[32:64, C_FREQ] = freq
    sc0[0:32, C_SGN] = -1.0
    sc0[32:64, C_SGN] = 1.0
    sc0[:, C_S2PI] = sc0[:, C_SGN] * np.float32(2.0 * PI)
    sc0[:, C_SPI] = -sc0[:, C_SGN] * np.float32(PI)

    in_maps = []
    for c in range(8):
        b, half = c // 2, c % 2
        own = slice(half * T, (half + 1) * T)
        oth = slice((1 - half) * T, (2 - half) * T)
        sc = sc0.copy()
        sc[:, C_HALO] = float(half)
        for gi, w in enumerate((2, 4, 8, 16)):
            for i in range(16):
                cntv = min(i + 1, w) if half == 0 else w
                sc[:, C_INVC + gi * 16 + i] = np.float32(1.0) / np.float32(cntv)
        kb = np.zeros((1, 4096), f32)
        if half == 0:
            kb[0, 0:T] = NEG
        m = dict(shared)
        m.update({
            "x_own": np.ascontiguousarray(x[b, own].T).reshape(8, 128, T).transpose(1, 0, 2).copy(),
            "x_oth": np.ascontiguousarray(x[b, oth].T).reshape(8, 128, T).transpose(1, 0, 2).copy(),
            "memT": np.ascontiguousarray(mem[b].T).reshape(8, 128, 256).transpose(1, 0, 2).copy(),
            "pos_own": np.ascontiguousarray(np.broadcast_to(pos[b, own][None, :], (64, T))),
            "pos_oth": np.ascontiguousarray(np.broadcast_to(pos[b, oth][None, :], (64, T))),
            "smallc": sc, "kbias": kb,
        })
        in_maps.append(m)
    return in_maps


_NC_CACHE = {}


def kernel(**inputs):
    in_maps = prepare_inputs(inputs)
    if "nc" not in _NC_CACHE:
        _NC_CACHE["nc"] = build()
    nc = _NC_CACHE["nc"]
    res = run_bass_kernel_spmd(nc, in_maps, core_ids=list(range(8)))
    out = np.empty((4, 4096, D), np.float32)
    for c in range(8):
        b, half = c // 2, c % 2
        o = np.asarray(res.results[c]["outT"], np.float32)
        out[b, half * T:(half + 1) * T, :] = o.transpose(2, 1, 0).reshape(T, D)
    return out
```

```python
import contextlib

import numpy as np
import concourse.bass as bass
import concourse.mybir as mybir
from concourse.bass_utils import run_bass_kernel_spmd

F32 = mybir.dt.float32
BF16 = mybir.dt.bfloat16
I32 = mybir.dt.int32
ALU = mybir.AluOpType
AF = mybir.ActivationFunctionType

T = 2048
NT = 4
D = 1024
DFF = 2816
NF = 22
PASSES = [(0, 6), (6, 12), (12, 17), (17, 22)]
EPS = 1e-6
PI = float(np.float32(np.pi))
NEG = -30000.0

C_FFN1, C_MIX, C_XAT, C_FFN2, C_FIN, C_MEM = 0, 8, 16, 24, 32, 40
C_QN, C_KVN, C_PSC, C_FREQ, C_SGN, C_HALO, C_INVC = 48, 50, 51, 55, 56, 57, 58
C_S2PI, C_SPI = 122, 123
NSC = 124


class Buf:
    __slots__ = ("name", "writer", "readers")

    def __init__(self, name):
        self.name = name
        self.writer = None
        self.readers = []


class Op:
    __slots__ = ("eng", "fn", "deps", "needed", "dma", "slot", "val", "idx")


class Sched:
    ENGS = ("pe", "act", "dve", "pool", "sp")

    def __init__(self):
        self.ops = {e: [] for e in self.ENGS}
        self.nops = 0

    def add(self, eng, fn, reads=(), writes=(), dma=False, slot=None):
        op = Op()
        op.eng, op.fn, op.dma, op.slot = eng, fn, dma, slot
        op.needed = False
        op.val = None
        op.idx = self.nops
        self.nops += 1
        deps = {}
        for b in reads:
            w = b.writer
            if w is not None and not (w.eng == eng and eng == "pe" and not w.dma):
                deps[w.idx] = w
        for b in writes:
            w = b.writer
            if w is not None and not (w.eng == eng and eng == "pe" and not w.dma):
                deps[w.idx] = w
            for r in b.readers:
                if r.eng == eng and eng == "pe" and not r.dma and not dma:
                    continue
                deps[r.idx] = r
        deps.pop(op.idx, None)
        op.deps = list(deps.values())
        for d in op.deps:
            d.needed = True
        for b in reads:
            b.readers.append(op)
        for b in writes:
            b.writer = op
            b.readers = []
        self.ops[eng].append(op)
        return op

    def emit(self, nc, final_waits=()):
        for op in final_waits:
            op.needed = True
        with contextlib.ExitStack() as st:
            esem = {e: st.enter_context(nc.semaphore("sem_" + e)) for e in self.ENGS}
            slot_sem, slot_cnt = {}, {}
            cnt = {e: 0 for e in self.ENGS}
            for e in self.ENGS:
                for op in self.ops[e]:
                    if op.dma:
                        key = op.slot
                        if key not in slot_sem:
                            slot_sem[key] = st.enter_context(nc.semaphore("dsem%d" % len(slot_sem)))
                            slot_cnt[key] = 0
                        slot_cnt[key] += 16
                        op.val = (slot_sem[key], slot_cnt[key])
                    elif op.needed:
                        cnt[e] += 1
                        op.val = (esem[e], cnt[e])
            block = st.enter_context(nc.Block())

            def run(engname, eng, extra=None):
                waited = {}
                for op in self.ops[engname]:
                    for d in op.deps:
                        sem, v = d.val
                        k = id(sem)
                        if waited.get(k, 0) >= v:
                            continue
                        waited[k] = v
                        eng.wait_ge(sem, v)
                    ins = op.fn(eng)
                    if op.dma:
                        ins.then_inc(op.val[0], 16)
                    elif op.needed:
                        ins.then_inc(op.val[0], 1)
                if extra:
                    for op in extra:
                        sem, v = op.val
                        eng.wait_ge(sem, v)

            @block.tensor
            def _(eng):
                run("pe", eng)

            @block.scalar
            def _(eng):
                run("act", eng)

            @block.vector
            def _(eng):
                run("dve", eng)

            @block.gpsimd
            def _(eng):
                run("pool", eng)

            @block.sync
            def _(eng):
                run("sp", eng, extra=final_waits)


def build(stop=None, dumps=()):
    nc = bass.Bass("TRN2", target_bir_lowering=False, dynamic_dma_scratch_size=8192)

    def din(name, shape, dt=F32):
        return nc.dram_tensor(name, list(shape), dt, kind="ExternalInput").ap()

    x_own = din("x_own", [128, 8, T])
    x_oth = din("x_oth", [128, 8, T])
    memT = din("memT", [128, 8, 256])
    pos_own = din("pos_own", [64, T], I32)
    pos_oth = din("pos_oth", [64, T], I32)
    wgu_d = [din("wgu1", [NF, 128, 2048]), din("wgu2", [NF, 128, 2048])]
    wd_d = [din("wd1", [4, 8, 128, 768]), din("wd2", [4, 8, 128, 768])]
    w_in_d = din("w_in", [128, 8192])
    wq_d = din("wq", [128, 2048])
    wkv_d = din("wkv", [128, 1024])
    poolw_d = din("poolw", [128, 512])
    w_out_d = din("w_out", [128, 8192])
    w_mq_d = din("w_mq", [128, 8192])
    w_mo_d = din("w_mo", [128, 8192])
    w_mkv_d = din("w_mkv", [8, 128, 2048])
    tri_d = din("tri", [128, 128])
    smallc_d = din("smallc", [128, NSC])
    kbias_d = din("kbias", [1, 4096])
    outT = nc.dram_tensor("outT", [128, 8, T], F32, kind="ExternalOutput").ap()

    S = Sched()
    bufs = {}

    SCRATCH_KEYS = {"xn", "at", "sq", "rstd", "sg", "un", "zp", "tmp", "dy", "wop", "ct", "st", "posi", "ang", "rs",
                    "t1", "t2", "qnt", "ts16", "kn", "v", "pt", "rc", "oh", "woh", "hn", "qm", "om", "ptm", "rcm",
                    "outb", "memf", "memn", "wm", "accd", "accp", "sumt", "ones32"}
    cur_fence = [None]

    def B(*key):
        b = bufs.get(key)
        if b is None:
            b = bufs[key] = Buf(str(key))
            if key[0] in SCRATCH_KEYS and cur_fence[0] is not None:
                b.writer = cur_fence[0]
        return b

    def fence():
        old = [b for k, b in bufs.items() if k[0] in SCRATCH_KEYS]
        cur_fence[0] = S.add("dve", lambda e: e.engine_nop(), [], old)

    st = contextlib.ExitStack()
    arena = st.enter_context(nc.sbuf_tensor("arena", [128, 208 * 256], F32))
    ps = st.enter_context(nc.psum_tensor("ps", [128, 8, 512], F32))
    A = arena[:]

    def reg(off, nbytes, dt=F32):
        v = A[:, off // 4:(off + nbytes) // 4]
        if dt != F32:
            v = v.bitcast(dt)
        return v

    K = 1024
    H = reg(0, 64 * K).rearrange("p (c t) -> p c t", c=8)
    SCR = 64 * K
    W16 = reg(128 * K, 16 * K, BF16).rearrange("p (c n) -> p c n", c=8)
    KVN = reg(144 * K, 8 * K, BF16)
    KR = reg(152 * K, 8 * K, BF16)
    QNP = reg(160 * K, 16 * K, BF16).rearrange("p (h t) -> p h t", h=4)
    WGUf = [reg(160 * K + i * 4 * K, 4 * K, BF16) for i in range(2)]
    WGU = [w.rearrange("p (g c n) -> p g c n", g=2, c=8) for w in WGUf]
    WDf = [reg(168 * K + i * 1536, 1536, BF16) for i in range(3)]
    WD = [w.rearrange("p (f n) -> p f n", f=6) for w in WDf]
    W16f = reg(128 * K, 16 * K, BF16)
    QPE = reg(176 * K, 16 * K, BF16).rearrange("p (h t) -> p h t", h=4)
    HALO = reg(176 * K - 256, 256).rearrange("p (g t) -> p g t", g=4)
    CB = 192 * K
    WQ = reg(CB, 4096, BF16).rearrange("p (c n) -> p c n", c=2)
    WKV = reg(CB + 4096, 2048, BF16)
    POOLW = reg(CB + 6144, 1024, BF16).rearrange("p (g n) -> p g n", g=4)
    ONES = reg(CB + 7168, 256, BF16)
    TRI = reg(CB + 7424, 256, BF16)
    KM = reg(CB + 7680, 4096, BF16).rearrange("p (c m) -> p c m", c=8)
    VM = reg(CB + 11776, 4096, BF16).rearrange("p (b n) -> p b n", b=2)
    SC = reg(CB + 15872, NSC * 4)

    def scr(off, nbytes, dt=F32):
        assert off + nbytes <= 64 * K, (off, nbytes)
        return reg(SCR + off, nbytes, dt)

    XN = scr(0, 32 * K, BF16).rearrange("p (c t) -> p c t", c=8)
    AT = scr(32 * K, 24 * K, BF16).rearrange("p (f t) -> p f t", f=6)
    SQ = [scr(56 * K + i * K, K, BF16) for i in range(3)]
    RSTD = [scr(59 * K, 2 * K)]
    QNT = scr(61 * K, 2 * K, BF16).rearrange("p (c t) -> p c t", c=2)
    SG = [scr(63 * K, K, BF16)]

    bank_ctr = [0]

    def nb(lo=0, hi=8):
        b = lo + bank_ctr[0] % (hi - lo)
        bank_ctr[0] += 1
        return b

    def mm(out, lhsT, rhs, start, stop, reads, writes):
        return S.add("pe", lambda e: e.matmul(out, lhsT=lhsT, rhs=rhs, start=start, stop=stop), reads, writes)

    def act(out, in_, func, reads, writes, **kw):
        return S.add("act", lambda e: e.activation(out=out, in_=in_, func=func, **kw), reads, writes)

    def tt(out, in0, in1, op, reads, writes, eng="dve"):
        return S.add(eng, lambda e: e.tensor_tensor(out=out, in0=in0, in1=in1, op=op), reads, writes)

    def ts(out, in0, s1, s2, op0, op1, reads, writes, eng="dve"):
        if op1 is None:
            return S.add(eng, lambda e: e.tensor_scalar(out=out, in0=in0, scalar1=s1, scalar2=None, op0=op0), reads, writes)
        return S.add(eng, lambda e: e.tensor_scalar(out=out, in0=in0, scalar1=s1, scalar2=s2, op0=op0, op1=op1), reads, writes)

    def stt(out, in0, scalar, in1, op0, op1, reads, writes, eng="dve"):
        return S.add(eng, lambda e: e.scalar_tensor_tensor(out=out, in0=in0, scalar=scalar, in1=in1, op0=op0, op1=op1), reads, writes)

    def cp(out, in_, reads, writes, eng="dve"):
        if eng == "act":
            return S.add("act", lambda e: e.copy(out=out, in_=in_), reads, writes)
        return S.add(eng, lambda e: e.tensor_copy(out=out, in_=in_), reads, writes)

    def dma(q, out, in_, reads, writes, slot):
        return S.add(q, lambda e: e.dma_start(out=out, in_=in_), reads, writes, dma=True, slot=slot)

    BPS = [B("ps", i) for i in range(8)]
    BC = B("c_sc")
    BONES = B("c_ones")
    BTRI = B("c_tri")

    sq_ctr = [0]

    def emit_norm(srcs, src_bufs, N, gcol, dtot, dsts, dst_bufs):
        C = len(srcs)
        b = nb()
        for c in range(C):
            sl = sq_ctr[0] % 3
            sq_ctr[0] += 1
            act(SQ[sl][:, 0:N], srcs[c], AF.Square, [src_bufs[c]], [B("sq", sl)])
            mm(ps[:, b, 0:N], ONES, SQ[sl][:, 0:N], c == 0, c == C - 1, [B("sq", sl), BONES], [BPS[b]])
        act(RSTD[0][:, 0:N], ps[:, b, 0:N], AF.Ln, [BPS[b]], [B("rstd", 0)], scale=1.0 / dtot, bias=EPS)
        act(RSTD[0][:, 0:N], RSTD[0][:, 0:N], AF.Exp, [B("rstd", 0)], [B("rstd", 0)], scale=-0.5)
        for c in range(C):
            stt(dsts[c], srcs[c], SC[:, gcol + c:gcol + c + 1], RSTD[0][:, 0:N], ALU.mult, ALU.mult,
                [src_bufs[c], B("rstd", 0), BC], [dst_bufs[c]])

    def tsl(t):
        return slice(t * 512, (t + 1) * 512)

    def BH(t, m):
        return B("H", t, m)

    dma("sp", SC, smallc_d, [], [BC], "c_sc")
    S.add("dve", lambda e: e.memset(ONES, 1.0), [], [BONES])
    dma("pool", TRI, tri_d, [], [BTRI], "c_tri")
    dma("pool", reg(CB, 4096, BF16), wq_d, [], [B("wq")], "c_wq")
    dma("pool", WKV, wkv_d, [], [B("wkv")], "c_wkv")
    dma("pool", reg(CB + 6144, 1024, BF16), poolw_d, [], [B("poolw")], "c_pw")
    dma("pool", KR[64:65, 0:2048], kbias_d[:, 0:2048], [], [B("krb")], "c_kb")
    dma("pool", KR[64:65, 2048:4096], kbias_d[:, 2048:4096], [], [B("krb")], "c_kb")
    for h in range(4):
        S.add("dve", lambda e, h=h: e.memset(QPE[64:65, h, :], 1.0), [], [B("qpeb")])

    wgu_ctr = [0]
    wd_ctr = [0]

    def emit_ffn(which, gcol, x_dram=None):
        for t in range(NT):
            if x_dram is not None:
                dma("sp", H[:, :, tsl(t)], x_dram[:, :, tsl(t)], [], [BH(t, m) for m in range(8)], ("H", t))
            emit_norm([H[:, c, tsl(t)] for c in range(8)], [BH(t, c) for c in range(8)], 512, gcol, D,
                      [XN[:, c, tsl(t)] for c in range(8)], [B("xn", t)] * 8)
        for j, (f0, f1) in enumerate(PASSES):
            nf = f1 - f0
            for f in range(f0, f1):
                sl = wgu_ctr[0] % 2
                wgu_ctr[0] += 1
                dma("pool", WGUf[sl], wgu_d[which][f], [], [B("wgu", sl)], ("wgu", sl))
                for t in range(NT):
                    bg = nb()
                    for kc in range(8):
                        mm(ps[:, bg, :], WGU[sl][:, 0, kc, :], XN[:, kc, tsl(t)], kc == 0, kc == 7,
                           [B("wgu", sl), B("xn", t)], [BPS[bg]])
                    bu = nb()
                    for kc in range(8):
                        mm(ps[:, bu, :], WGU[sl][:, 1, kc, :], XN[:, kc, tsl(t)], kc == 0, kc == 7,
                           [B("wgu", sl), B("xn", t)], [BPS[bu]])
                    act(SG[0], ps[:, bg, :], AF.Silu, [BPS[bg]], [B("sg")])
                    tt(AT[:, f - f0, tsl(t)], ps[:, bu, :], SG[0], ALU.mult, [BPS[bu], B("sg")], [B("at", f - f0, t)])
            for m in range(8):
                sl = wd_ctr[0] % 3
                wd_ctr[0] += 1
                dma("pool", WDf[sl][:, 0:nf * 128], wd_d[which][j, m][:, 0:nf * 128], [], [B("wd", sl)], ("wd", sl))
                for t in range(NT):
                    b = nb()
                    for fl in range(nf):
                        mm(ps[:, b, :], WD[sl][:, fl, :], AT[:, fl, tsl(t)], fl == 0, fl == nf - 1,
                           [B("wd", sl), B("at", fl, t)], [BPS[b]])
                    stt(H[:, m, tsl(t)], ps[:, b, :], 0.5, H[:, m, tsl(t)], ALU.mult, ALU.add,
                        [BPS[b], BH(t, m)], [BH(t, m)])

    UN = [scr(i * 8 * K, 8 * K, BF16).rearrange("p (c t) -> p c t", c=8) for i in range(2)]
    ZP = scr(16 * K, 4 * 528 * 4).rearrange("p (g t) -> p g t", g=4)
    o = 16 * K + 8448
    TMP = [scr(o + i * 2112, 2112) for i in range(2)]
    o += 2 * 2112
    DY = scr(o, 4 * K, BF16).rearrange("p (g t) -> p g t", g=4)
    o += 4 * K
    WOPf = scr(o, 8 * K, BF16)
    WOP = WOPf.rearrange("p (c n) -> p c n", c=4)
    o += 8 * K
    CT = scr(o, 2 * K)
    STb = scr(o + 2 * K, 2 * K)
    o += 4 * K
    POSI = scr(o, 2 * K, I32)
    ANG = scr(o + 2 * K, 2 * K)
    RS = scr(o + 4 * K, 2 * K)
    o += 6 * K
    T1 = scr(o, 2 * K)
    T2 = ANG
    o += 2 * K
    RSTD.append(scr(o, 2 * K))
    o += 2 * K
    TS16 = scr(o, 64)
    o += 64
    assert o <= 56 * K, o

    class PA:
        def __init__(self):
            self.free = list(range(8))

        def alloc(self):
            assert self.free, "PSUM banks exhausted"
            return self.free.pop(0)

        def release(self, b):
            self.free.append(b)

    pa = PA()

    def norm_g(srcs, src_bufs, N, gcol, dtot, dsts, dst_bufs, ri, rel=()):
        C = len(srcs)
        b = pa.alloc()
        for c in range(C):
            sl = sq_ctr[0] % 3
            sq_ctr[0] += 1
            act(SQ[sl][:, 0:N], srcs[c], AF.Square, [src_bufs[c]], [B("sq", sl)])
            mm(ps[:, b, 0:N], ONES, SQ[sl][:, 0:N], c == 0, c == C - 1, [B("sq", sl), BONES], [BPS[b]])
            yield
        rb = B("rstd", ri)
        act(RSTD[ri][:, 0:N], ps[:, b, 0:N], AF.Ln, [BPS[b]], [rb], scale=1.0 / dtot, bias=EPS)
        act(RSTD[ri][:, 0:N], RSTD[ri][:, 0:N], AF.Exp, [rb], [rb], scale=-0.5)
        pa.release(b)
        yield
        for c in range(C):
            stt(dsts[c], srcs[c], SC[:, gcol + c:gcol + c + 1], RSTD[ri][:, 0:N], ALU.mult, ALU.mult,
                [src_bufs[c], rb, BC], [dst_bufs[c]])
            yield
        for b_ in rel:
            pa.release(b_)

    def tables_g(pos_d, t):
        dma("sp", POSI[0:64, :], pos_d[:, tsl(t)], [], [B("posi")], "posi")
        cp(ANG[0:64, :], POSI[0:64, :], [B("posi")], [B("ang")])
        ts(ANG[0:64, :], ANG[0:64, :], SC[0:64, C_FREQ:C_FREQ + 1], None, ALU.mult, None, [B("ang"), BC], [B("ang")])
        yield
        KI = POSI
        for (dst, db, off, use_sgn) in ((STb, "st", 0.5, True), (CT, "ct", 0.75, False)):
            ts(RS[0:64, :], ANG[0:64, :], 1.0 / (2.0 * PI), off, ALU.mult, ALU.add, [B("ang")], [B("rs")])
            cp(KI[0:64, :], RS[0:64, :], [B("rs")], [B("posi")])
            yield
            cp(T1[0:64, :], KI[0:64, :], [B("posi")], [B("t1")])
            tt(RS[0:64, :], RS[0:64, :], T1[0:64, :], ALU.subtract, [B("rs"), B("t1")], [B("rs")])
            yield
            stt(RS[0:64, :], RS[0:64, :], 0.0, RS[0:64, :], ALU.is_lt, ALU.add, [B("rs")], [B("rs")])
            if use_sgn:
                act(dst[0:64, :], RS[0:64, :], AF.Sin, [B("rs"), BC], [B(db)],
                    scale=SC[0:64, C_S2PI:C_S2PI + 1], bias=SC[0:64, C_SPI:C_SPI + 1])
            else:
                act(dst[0:64, :], RS[0:64, :], AF.Sin, [B("rs")], [B(db)], scale=2.0 * PI, bias=-PI)
            yield

    def rope_ops(za_b, zb_b, dst, dst_buf):
        tt(T1[0:64, :], ps[0:64, za_b, :], CT[0:64, :], ALU.mult, [BPS[za_b], B("ct")], [B("t1")])
        tt(T2[0:64, :], ps[0:64, zb_b, :], STb[0:64, :], ALU.mult, [BPS[zb_b], B("st")], [B("ang")])
        pa.release(za_b)
        pa.release(zb_b)
        tt(dst, T1[0:64, :], T2[0:64, :], ALU.add, [B("t1"), B("ang")], [dst_buf], eng="pool")

    def proj_t(t, c0, c1, part=128):
        u, ub = UN[t % 2], B("un", t % 2)
        b = pa.alloc()
        for kc in range(8):
            mm(ps[0:part, b, :], W16[:, kc, c0:c1], u[:, kc, :], kc == 0, kc == 7, [B("w16"), ub], [BPS[b]])
        return b

    def p1_g(t):
        u, ub = UN[t % 2], B("un", t % 2)
        yield from norm_g([H[:, c, tsl(t)] for c in range(8)], [BH(t, c) for c in range(8)], 512, C_MIX, D,
                          [u[:, c, :] for c in range(8)], [ub] * 8, 0)

    def p2_g(t, own):
        kt = (4 if own else 0) + t
        ksl = slice(kt * 512, (kt + 1) * 512)
        tg = tables_g(pos_own if own else pos_oth, t)
        next(tg)
        b = proj_t(t, 256, 384)
        yield
        ng = norm_g([ps[:, b, :]], [BPS[b]], 512, C_KVN, 128, [KVN[:, ksl]], [B("kvn", kt)], 1, rel=(b,))
        for _ in ng:
            next(tg, None)
            yield
        for _ in tg:
            yield
        ba = proj_t(t, 384, 448, 64)
        yield
        bb = proj_t(t, 960, 1024, 64)
        yield
        if own:
            bq = [proj_t(t, 0, 128)]
            yield
            bq.append(proj_t(t, 128, 256))
            yield
        rope_ops(ba, bb, KR[0:64, ksl], B("kr", kt))
        yield
        if own:
            yield from norm_g([ps[:, bq[0], :], ps[:, bq[1], :]], [BPS[bq[0]], BPS[bq[1]]], 512, C_QN, 256,
                              [QNT[:, 0, :], QNT[:, 1, :]], [B("qnt")] * 2, 1, rel=tuple(bq))
            for h in range(4):
                b = pa.alloc()
                for kc in range(2):
                    mm(ps[:, b, :], WQ[:, kc, h * 256:h * 256 + 128], QNT[:, kc, :], kc == 0, kc == 1, [B("wq"), B("qnt")], [BPS[b]])
                ba = pa.alloc()
                for kc in range(2):
                    mm(ps[0:64, ba, :], WQ[:, kc, h * 256 + 128:h * 256 + 192], QNT[:, kc, :], kc == 0, kc == 1, [B("wq"), B("qnt")], [BPS[ba]])
                bb = pa.alloc()
                for kc in range(2):
                    mm(ps[0:64, bb, :], WQ[:, kc, h * 256 + 192:h * 256 + 256], QNT[:, kc, :], kc == 0, kc == 1, [B("wq"), B("qnt")], [BPS[bb]])
                yield
                cp(QNP[:, h, tsl(t)], ps[:, b, :], [BPS[b]], [B("qnp", h, t)], eng="act")
                pa.release(b)
                rope_ops(ba, bb, QPE[0:64, h, tsl(t)], B("qpe", h, t))
                yield

    def p3_g(t, own):
        for g in range(4):
            b = proj_t(t, 448 + g * 128, 576 + g * 128)
            yield
            if not own:
                ts(HALO[:, g, :], ps[:, b, 496:512], SC[:, C_HALO:C_HALO + 1], None, ALU.mult, None,
                   [BPS[b], BC], [B("halo")])
                pa.release(b)
                continue
            if t == 0:
                cp(ZP[:, g, 0:16], HALO[:, g, :], [B("halo"), B("qnp", 3, 3)], [B("zp", g)], eng="pool")
            cp(ZP[:, g, 16:528], ps[:, b, :], [BPS[b]], [B("zp", g)], eng="act")
            pa.release(b)
            yield
            cur, curb = ZP[:, g, :], B("zp", g)
            lo = 0
            for l in range(g + 1):
                step = 1 << l
                lo += step
                dst, dstb = TMP[l % 2], B("tmp", l % 2)
                tt(dst[:, lo:528], cur[:, lo:528], cur[:, lo - step:528 - step], ALU.add, [curb], [dstb], eng="pool")
                cur, curb = dst, dstb
                yield
            w = 1 << (g + 1)
            if t == 0:
                tt(TS16[:, 0:16], cur[:, 16:32], SC[:, C_INVC + g * 16:C_INVC + (g + 1) * 16], ALU.mult, [curb, BC], [B("ts16")], eng="pool")
            stt(DY[:, g, :], cur[:, 16:528], 1.0 / w, ZP[:, g, 16:528], ALU.mult, ALU.subtract,
                [curb, B("zp", g)], [B("dy", g)])
            if t == 0:
                tt(DY[:, g, 0:16], TS16[:, 0:16], ZP[:, g, 16:32], ALU.subtract, [B("ts16"), B("zp", g)], [B("dy", g)], eng="pool")
            if t < NT - 1:
                cp(ZP[:, g, 0:16], ZP[:, g, 512:528], [B("zp", g)], [B("zp", g)], eng="pool")
            yield
            b2 = pa.alloc()
            mm(ps[:, b2, :], POOLW[:, g, :], DY[:, g, :], True, True, [B("poolw"), B("dy", g)], [BPS[b2]])
            ts(DY[:, g, :], ps[:, b2, :], SC[:, C_PSC + g:C_PSC + g + 1], None, ALU.mult, None, [BPS[b2], BC], [B("dy", g)])
            pa.release(b2)
            yield
        if own:
            for m in range(8):
                b = pa.alloc()
                for g in range(4):
                    mm(ps[:, b, :], WOP[:, g, m * 128:(m + 1) * 128], DY[:, g, :], g == 0, g == 3, [B("wop"), B("dy", g)], [BPS[b]])
                tt(H[:, m, tsl(t)], ps[:, b, :], H[:, m, tsl(t)], ALU.add, [BPS[b], BH(t, m)], [BH(t, m)])
                pa.release(b)
                yield

    def interleave(*gens):
        gens = [g for g in gens if g is not None]
        while gens:
            for g in list(gens):
                try:
                    next(g)
                except StopIteration:
                    gens.remove(g)

    MEMF = scr(16 * K, 8 * K).rearrange("p (c m) -> p c m", c=8)
    MEMN = scr(24 * K, 4 * K, BF16).rearrange("p (c m) -> p c m", c=8)
    WMf = [scr(28 * K + i * 4 * K, 4 * K, BF16) for i in range(2)]
    WM = [w.rearrange("p (c n) -> p c n", c=8) for w in WMf]
    RSTD.append(scr(36 * K, K))
    RI_P0 = len(RSTD) - 1

    def p0_g():
        dma("sp", MEMF, memT, [], [B("memf")], "memf")
        yield from norm_g([MEMF[:, c, :] for c in range(8)], [B("memf")] * 8, 256, C_MEM, D,
                          [MEMN[:, c, :] for c in range(8)], [B("memn")] * 8, RI_P0)
        for cg in range(8):
            sl = cg % 2
            dma("pool", WMf[sl], w_mkv_d[cg], [], [B("wm", sl)], ("wm", sl))
            for half in range(2):
                b = pa.alloc()
                if cg < 4:
                    c = cg * 2 + half
                    for kc in range(8):
                        mm(ps[:, b, 0:256], WM[sl][:, kc, half * 128:(half + 1) * 128], MEMN[:, kc, :], kc == 0, kc == 7,
                           [B("wm", sl), B("memn")], [BPS[b]])
                    cp(KM[:, c, :], ps[:, b, 0:256], [BPS[b]], [B("km")])
                else:
                    for kc in range(8):
                        mm(ps[:, b, 0:256], MEMN[:, kc, half * 128:(half + 1) * 128], WM[sl][:, kc, :], kc == 0, kc == 7,
                           [B("wm", sl), B("memn")], [BPS[b]])
                    cp(VM[:, half, (cg - 4) * 256:(cg - 3) * 256], ps[:, b, 0:256], [BPS[b]], [B("vm")])
                pa.release(b)
                yield

    def take_g(it, n):
        for _ in range(n):
            try:
                next(it)
            except StopIteration:
                return
            yield

    def emit_inproj_phase(own):
        extra = None if own else p0_g()
        interleave(p1_g(0))
        for t in range(NT):
            streams = [p2_g(t, own)]
            if own or t == NT - 1:
                streams.append(p3_g(t, own))
            if t + 1 < NT:
                streams.append(p1_g(t + 1))
            if extra is not None:
                streams.append(take_g(extra, 12))
            interleave(*streams)
        if extra is not None:
            interleave(extra)

    fence()
    emit_ffn(0, C_FFN1, x_oth)
    fence()
    dma("pool", W16f, w_in_d, [], [B("w16")], "w16")
    emit_inproj_phase(False)
    if stop != "A":
        fence()
        emit_ffn(0, C_FFN1, x_own)
    if stop not in ("A", "B1"):
        fence()
        dma("pool", WOPf, w_out_d[:, 4096:8192], [], [B("wop")], "wop")
        emit_inproj_phase(True)

    if stop not in ("A", "B1", "B2"):
        fence()
        KNb = [scr(i * 8 * K, 8 * K, BF16) for i in range(2)]
        Vb = [scr(16 * K + i * 8 * K, 8 * K, BF16).rearrange("p (b n) -> p b n", b=32) for i in range(2)]
        PT = [scr(32 * K + i * K, K, BF16) for i in range(4)] + [scr(56 * K + i * K, K, BF16) for i in range(4)]
        NPT = len(PT)
        ACCV = {"dve": scr(52 * K, 2 * K), "pool": scr(54 * K, 2 * K)}
        ACCB = {"dve": "accd", "pool": "accp"}
        ONES32 = scr(60 * K, 512)
        SUMT = scr(61 * K, 2 * K)
        S.add("dve", lambda e: e.memset(ONES32, 1.0), [], [B("ones32")])
        RC = [scr(36 * K + i * 2 * K, 2 * K) for i in range(2)]
        OH = [scr(40 * K + i * 4 * K, 4 * K, BF16) for i in range(2)]
        WOH = [scr(48 * K + i * 2 * K, 2 * K, BF16) for i in range(2)]
        sc_attn = float(192.0 ** -0.5)

        def emit_upproj(h):
            hs = h % 2
            dma("pool", WOH[hs], w_out_d[:, h * 1024:(h + 1) * 1024], [], [B("woh", hs)], ("woh", hs))
            for kt in range(8):
                b = nb(0, 4)
                mm(ps[:, b, :], WKV[:, h * 256:h * 256 + 128], KVN[:, kt * 512:(kt + 1) * 512], True, True,
                   [B("wkv"), B("kvn", kt)], [BPS[b]])
                cp(KNb[hs][:, kt * 512:(kt + 1) * 512], ps[:, b, :], [BPS[b]], [B("kn", hs, kt)])
            for vb in range(8):
                b = nb(0, 4)
                for j in range(4):
                    kb = vb * 4 + j
                    mm(ps[:, b, j * 128:(j + 1) * 128], KVN[:, kb * 128:(kb + 1) * 128], WKV[:, h * 256 + 128:h * 256 + 256],
                       True, True, [B("wkv"), B("kvn", kb // 4)], [BPS[b]])
                cp(Vb[hs][:, vb * 4:(vb + 1) * 4, :], ps[:, b, :].rearrange("p (j n) -> p j n", j=4), [BPS[b]], [B("v", hs, vb)])

        pending = []

        def flush():
            while pending:
                pending.pop(0)()

        def emit_wout(h, t):
            hs = h % 2
            for m in range(8):
                b = nb(0, 4)
                mm(ps[:, b, :], WOH[hs][:, m * 128:(m + 1) * 128], OH[hs][:, tsl(t)], True, True,
                   [B("woh", hs), B("oh", hs, t)], [BPS[b]])
                tt(H[:, m, tsl(t)], ps[:, b, :], H[:, m, tsl(t)], ALU.add, [BPS[b], BH(t, m)], [BH(t, m)])

        emit_upproj(0)
        units_all = []
        acc = 0
        for h in range(4):
            for t in range(NT):
                ul = [(kb, 0, False) for kb in range(16)]
                for j in range(4 * t + 4):
                    ul.append((16 + j, max(0, j - 4 * t), j >= 4 * t))
                for idx, (kb, r, diag) in enumerate(ul):
                    units_all.append(dict(h=h, t=t, idx=idx, kb=kb, r=r, diag=diag, first=(idx == 0),
                                          last=(idx == len(ul) - 1), acc=acc))
                acc += 1
        pt_ctr = [0]

        def stage_a(u):
            h, t, kb = u["h"], u["t"], u["kb"]
            hs = h % 2
            if u["idx"] == 4:
                flush()
                if t == 1 and h < 3:
                    emit_upproj(h + 1)
            n0 = u["r"] * 128
            b = nb(0, 4)
            pi = pt_ctr[0] % NPT
            pt_ctr[0] += 1
            u["pi"] = pi
            qs = slice(t * 512 + n0, (t + 1) * 512)
            mm(ps[:, b, n0:512], KNb[hs][:, kb * 128:(kb + 1) * 128], QNP[:, h, qs], True, False,
               [B("kn", hs, kb // 4), B("qnp", h, t)], [BPS[b]])
            mm(ps[:, b, n0:512], KR[0:65, kb * 128:(kb + 1) * 128], QPE[0:65, h, qs], False, True,
               [B("kr", kb // 4), B("krb"), B("qpe", h, t), B("qpeb")], [BPS[b]])
            act(PT[pi][:, n0:512], ps[:, b, n0:512], AF.Exp, [BPS[b]], [B("pt", pi)], scale=sc_attn)
            if u["diag"]:
                tt(PT[pi][:, n0:n0 + 128], PT[pi][:, n0:n0 + 128], TRI, ALU.mult, [B("pt", pi), BTRI], [B("pt", pi)])

        def stage_b(u):
            h, t, kb, pi = u["h"], u["t"], u["kb"], u["pi"]
            hs = h % 2
            n0 = u["r"] * 128
            ob, sb, rcb = 4 + u["acc"] % 2, 6 + u["acc"] % 2, u["acc"] % 2
            mm(ps[:, ob, n0:512], Vb[hs][:, kb, :], PT[pi][:, n0:512], u["first"], u["last"], [B("v", hs, kb // 4), B("pt", pi)], [BPS[ob]])
            role = ("pool", "dve", "pe", "pool")[u["idx"] % 4]
            if role == "pe":
                mm(ps[:, sb, n0:512], ONES, PT[pi][:, n0:512], u["idx"] == 2, False, [BONES, B("pt", pi)], [BPS[sb]])
            else:
                av, ab = ACCV[role], B(ACCB[role])
                if u["idx"] < 2:
                    cp(av, PT[pi], [B("pt", pi)], [ab], eng=role)
                else:
                    tt(av[:, n0:512], av[:, n0:512], PT[pi][:, n0:512], ALU.add, [ab, B("pt", pi)], [ab], eng=role)
            if u["last"]:
                tt(SUMT, ACCV["dve"], ACCV["pool"], ALU.add, [B("accd"), B("accp")], [B("sumt")])
                mm(ps[:, sb, :], ONES32, SUMT, False, True, [B("ones32"), B("sumt")], [BPS[sb]])
                act(RC[rcb], ps[:, sb, :], AF.Ln, [BPS[sb]], [B("rc", rcb)])
                act(RC[rcb], RC[rcb], AF.Exp, [B("rc", rcb)], [B("rc", rcb)], scale=-1.0)
                tt(OH[hs][:, tsl(t)], ps[:, ob, :], RC[rcb], ALU.mult, [BPS[ob], B("rc", rcb)], [B("oh", hs, t)])
                pending.append(lambda h=h, t=t: emit_wout(h, t))

        LOOK = 2
        for i in range(len(units_all) + LOOK):
            if i < len(units_all):
                stage_a(units_all[i])
            if i >= LOOK:
                stage_b(units_all[i - LOOK])
        flush()

    if stop not in ("A", "B1", "B2", "B3"):
        fence()
        WMOf = reg(144 * K, 16 * K, BF16)
        WMO = WMOf.rearrange("p (c n) -> p c n", c=8)
        kv_dead = [B("kvn", kt) for kt in range(8)] + [B("kr", kt) for kt in range(8)] + [B("krb")]
        dma("pool", W16f, w_mq_d, [], [B("w16")], "w16")
        dma("pool", WMOf, w_mo_d, [], kv_dead, "wmo")
        HN = [scr(i * 8 * K, 8 * K, BF16).rearrange("p (c t) -> p c t", c=8) for i in range(2)]
        QM = [scr(16 * K + i * 8 * K, 8 * K, BF16).rearrange("p (c t) -> p c t", c=8) for i in range(2)]
        OM = [scr(32 * K + i * 8 * K, 8 * K, BF16).rearrange("p (c t) -> p c t", c=8) for i in range(2)]
        PTm = [scr(48 * K + i * K, K, BF16) for i in range(4)]
        RCm = [scr(52 * K + i * 2 * K, 2 * K) for i in range(2)]
        sc_mem = float(256.0 ** -0.5)

        def xa_norm_g(t):
            hn, hb = HN[t % 2], B("hn", t % 2)
            yield from norm_g([H[:, c, tsl(t)] for c in range(8)], [BH(t, c) for c in range(8)], 512, C_XAT, D,
                              [hn[:, c, :] for c in range(8)], [hb] * 8, 0)

        def xa_q_g(t):
            hn, hb = HN[t % 2], B("hn", t % 2)
            for c in range(8):
                b = pa.alloc()
                for kc in range(8):
                    mm(ps[:, b, :], W16[:, kc, c * 128:(c + 1) * 128], hn[:, kc, :], kc == 0, kc == 7, [B("w16"), hb], [BPS[b]])
                cp(QM[t % 2][:, c, :], ps[:, b, :], [BPS[b]], [B("qm", t % 2, c)], eng=("act" if c % 2 else "dve"))
                pa.release(b)
                yield

        def xa_heads_g(t):
            qm, om = QM[t % 2], OM[t % 2]

            def stage_a(h):
                pis = []
                for mb in range(2):
                    b = pa.alloc()
                    for dc in range(2):
                        mm(ps[:, b, :], KM[:, 2 * h + dc, mb * 128:(mb + 1) * 128], qm[:, 2 * h + dc, :], dc == 0, dc == 1,
                           [B("km"), B("qm", t % 2, 2 * h + dc)], [BPS[b]])
                    pi = (2 * h + mb) % 4
                    pis.append(pi)
                    act(PTm[pi], ps[:, b, :], AF.Exp, [BPS[b]], [B("ptm", pi)], scale=sc_mem)
                    pa.release(b)
                return pis

            pis = {0: stage_a(0)}
            yield
            for h in range(4):
                if h < 3:
                    pis[h + 1] = stage_a(h + 1)
                    yield
                p = pis[h]
                b = pa.alloc()
                for mb in range(2):
                    mm(ps[:, b, :], ONES, PTm[p[mb]], mb == 0, mb == 1, [BONES, B("ptm", p[mb])], [BPS[b]])
                rcb = h % 2
                act(RCm[rcb], ps[:, b, :], AF.Ln, [BPS[b]], [B("rcm", rcb)])
                act(RCm[rcb], RCm[rcb], AF.Exp, [B("rcm", rcb)], [B("rcm", rcb)], scale=-1.0)
                pa.release(b)
                for dvc in range(2):
                    b = pa.alloc()
                    for mb in range(2):
                        mm(ps[:, b, :], VM[:, mb, h * 256 + dvc * 128:h * 256 + (dvc + 1) * 128], PTm[p[mb]], mb == 0, mb == 1,
                           [B("vm"), B("ptm", p[mb])], [BPS[b]])
                    tt(om[:, 2 * h + dvc, :], ps[:, b, :], RCm[rcb], ALU.mult, [BPS[b], B("rcm", rcb)], [B("om", t % 2, 2 * h + dvc)])
                    pa.release(b)
                yield
            for m in range(8):
                b = pa.alloc()
                for kc in range(8):
                    mm(ps[:, b, :], WMO[:, kc, m * 128:(m + 1) * 128], om[:, kc, :], kc == 0, kc == 7,
                       [kv_dead[0], B("om", t % 2, kc)], [BPS[b]])
                tt(H[:, m, tsl(t)], ps[:, b, :], H[:, m, tsl(t)], ALU.add, [BPS[b], BH(t, m)], [BH(t, m)])
                pa.release(b)
                yield

        def chain_g(*gs):
            for g_ in gs:
                if g_ is not None:
                    yield from g_

        interleave(xa_norm_g(0))
        interleave(xa_q_g(0), xa_norm_g(1))
        for t in range(NT):
            interleave(xa_heads_g(t),
                       chain_g(xa_norm_g(t + 2) if t + 2 < NT else None, xa_q_g(t + 1) if t + 1 < NT else None))

    finals = []
    if stop is None:
        S.add("dve", lambda e: e.engine_nop(), [],
              [B("qnp", h, t) for h in range(4) for t in range(NT)] + [B("wgu", sl) for sl in range(2)] + [B("wd", sl) for sl in range(3)])
        fence()
        emit_ffn(1, C_FFN2)
        fence()
        OUTB = [scr(i * 16 * K, 16 * K).rearrange("p (c t) -> p c t", c=8) for i in range(2)]
        for t in range(NT):
            ob_, obb = OUTB[t % 2], B("outb", t % 2)
            emit_norm([H[:, c, tsl(t)] for c in range(8)], [BH(t, c) for c in range(8)], 512, C_FIN, D,
                      [ob_[:, c, :] for c in range(8)], [obb] * 8)
            finals.append(dma("sp", outT[:, :, tsl(t)], ob_, [obb], [], ("out", t % 2)))

    dump_specs = {
        "H": (H, [128, 8, T], F32),
        "KVN": (KVN, [128, 4096], BF16),
        "KR": (KR[0:65, :], [65, 4096], BF16),
        "QNP": (QNP, [128, 4, T], BF16),
        "QPE": (QPE[0:65], [65, 4, T], BF16),
        "KM": (KM, [128, 8, 256], BF16),
        "VM": (VM, [128, 2, 1024], BF16),
        "ZPH": (ZP[:, :, 0:16], [128, 4, 16], F32),
    }
    for name in dumps:
        ap, shape, dt = dump_specs[name]
        d = nc.dram_tensor("dbg_" + name, shape, dt, kind="ExternalOutput").ap()
        allb = list(bufs.values())
        finals.append(dma("sp", d, ap, allb, [], ("dbg", name)))

    S.emit(nc, final_waits=finals)
    st.close()
    return nc


def _fm(w):
    k, n = w.shape
    return np.ascontiguousarray(w.reshape(k // 128, 128, n).transpose(1, 0, 2)).reshape(128, -1)


def prepare_inputs(inp):
    f32 = np.float32
    x = np.asarray(inp["x"], f32)
    mem = np.asarray(inp["mem"], f32)
    pos = np.asarray(inp["positions"], np.int32)
    g = lambda k: np.asarray(inp[k], f32)

    def ffn_layout(wg, wu, wd):
        wg_r = wg.reshape(8, 128, NF, 128).transpose(2, 1, 0, 3)
        wu_r = wu.reshape(8, 128, NF, 128).transpose(2, 1, 0, 3)
        wgu = np.ascontiguousarray(np.stack([wg_r, wu_r], axis=2)).reshape(NF, 128, 2048)
        wd_r = wd.reshape(NF, 128, 8, 128)
        wdl = np.zeros((4, 8, 128, 6, 128), f32)
        for j, (f0, f1) in enumerate(PASSES):
            wdl[j, :, :, 0:f1 - f0, :] = wd_r[f0:f1].transpose(2, 1, 0, 3)
        return wgu, wdl.reshape(4, 8, 128, 768)

    wgu1, wd1 = ffn_layout(g("ffn1_w_gate")[0], g("ffn1_w_up")[0], g("ffn1_w_down")[0])
    wgu2, wd2 = ffn_layout(g("ffn2_w_gate")[0], g("ffn2_w_up")[0], g("ffn2_w_down")[0])
    w_in = g("w_in")[0]
    w_in_ext = np.concatenate([w_in, w_in[:, 416:448], w_in[:, 384:416]], axis=1)
    wqu = g("w_q_up")[0]
    wq_ext = np.zeros((256, 1024), f32)
    for h in range(4):
        o = h * 192
        wq_ext[:, h * 256:h * 256 + 192] = wqu[:, o:o + 192]
        wq_ext[:, h * 256 + 192:h * 256 + 224] = wqu[:, o + 160:o + 192]
        wq_ext[:, h * 256 + 224:h * 256 + 256] = wqu[:, o + 128:o + 160]
    w_mkv = g("w_mkv")[0]
    w_mkv_l = np.stack([_fm(w_mkv[:, cg * 256:(cg + 1) * 256]) for cg in range(8)])
    tri = (np.arange(128)[None, :] >= np.arange(128)[:, None]).astype(f32)
    freq = (1.0 / (np.float32(10000.0) ** (np.arange(0, 64, 2, dtype=f32) / np.float32(64)))).astype(f32)

    shared = {
        "wgu1": wgu1, "wd1": wd1, "wgu2": wgu2, "wd2": wd2,
        "w_in": _fm(w_in_ext), "wq": _fm(wq_ext), "wkv": np.ascontiguousarray(g("w_kv_up")[0]),
        "poolw": np.ascontiguousarray(g("pool_w")[0].transpose(1, 0, 2)).reshape(128, 512),
        "w_out": _fm(g("w_out")[0]), "w_mq": _fm(g("w_mq")[0]), "w_mo": _fm(g("w_mo")[0]),
        "w_mkv": w_mkv_l, "tri": tri,
    }
    sc0 = np.zeros((128, NSC), f32)
    for col, key in ((C_FFN1, "ffn1_norm"), (C_MIX, "mix_norm"), (C_XAT, "xattn_norm"), (C_FFN2, "ffn2_norm"), (C_MEM, "mem_norm")):
        sc0[:, col:col + 8] = g(key)[0].reshape(8, 128).T
    sc0[:, C_FIN:C_FIN + 8] = g("final_norm").reshape(8, 128).T
    sc0[:, C_QN:C_QN + 2] = g("q_norm")[0].reshape(2, 128).T
    sc0[:, C_KVN] = g("kv_norm")[0]
    sc0[:, C_PSC:C_PSC + 4] = g("pool_scale")[0].reshape(4, 128).T
    sc0[0:32, C_FREQ] = freq
    sc0[32:64, C_FREQ] = freq
    sc0[0:32, C_SGN] = -1.0
    sc0[32:64, C_SGN] = 1.0
    sc0[:, C_S2PI] = sc0[:, C_SGN] * np.float32(2.0 * PI)
    sc0[:, C_SPI] = -sc0[:, C_SGN] * np.float32(PI)

    in_maps = []
    for c in range(8):
        b, half = c // 2, c % 2
        own = slice(half * T, (half + 1) * T)
        oth = slice((1 - half) * T, (2 - half) * T)
        sc = sc0.copy()
        sc[:, C_HALO] = float(half)
        for gi, w in enumerate((2, 4, 8, 16)):
            for i in range(16):
                cntv = min(i + 1, w) if half == 0 else w
                sc[:, C_INVC + gi * 16 + i] = np.float32(1.0) / np.float32(cntv)
        kb = np.zeros((1, 4096), f32)
        if half == 0:
            kb[0, 0:T] = NEG
        m = dict(shared)
        m.update({
            "x_own": np.ascontiguousarray(x[b, own].T).reshape(8, 128, T).transpose(1, 0, 2).copy(),
            "x_oth": np.ascontiguousarray(x[b, oth].T).reshape(8, 128, T).transpose(1, 0, 2).copy(),
            "memT": np.ascontiguousarray(mem[b].T).reshape(8, 128, 256).transpose(1, 0, 2).copy(),
            "pos_own": np.ascontiguousarray(np.broadcast_to(pos[b, own][None, :], (64, T))),
            "pos_oth": np.ascontiguousarray(np.broadcast_to(pos[b, oth][None, :], (64, T))),
            "smallc": sc, "kbias": kb,
        })
        in_maps.append(m)
    return in_maps


_NC_CACHE = {}


def kernel(**inputs):
    in_maps = prepare_inputs(inputs)
    if "nc" not in _NC_CACHE:
        _NC_CACHE["nc"] = build()
    nc = _NC_CACHE["nc"]
    res = run_bass_kernel_spmd(nc, in_maps, core_ids=list(range(8)))
    out = np.empty((4, 4096, D), np.float32)
    for c in range(8):
        b, half = c // 2, c % 2
        o = np.asarray(res.results[c]["outT"], np.float32)
        out[b, half * T:(half + 1) * T, :] = o.transpose(2, 1, 0).reshape(T, D)
    return out
```

```python
import contextlib

import numpy as np
import concourse.bass as bass
import concourse.mybir as mybir
from concourse.bass_utils import run_bass_kernel_spmd

F32 = mybir.dt.float32
BF16 = mybir.dt.bfloat16
I32 = mybir.dt.int32
ALU = mybir.AluOpType
AF = mybir.ActivationFunctionType

T = 2048
NT = 4
D = 1024
DFF = 2816
NF = 22
PASSES = [(0, 6), (6, 12), (12, 17), (17, 22)]
EPS = 1e-6
PI = float(np.float32(np.pi))
NEG = -30000.0

C_FFN1, C_MIX, C_XAT, C_FFN2, C_FIN, C_MEM = 0, 8, 16, 24, 32, 40
C_QN, C_KVN, C_PSC, C_FREQ, C_SGN, C_HALO, C_INVC = 48, 50, 51, 55, 56, 57, 58
C_S2PI, C_SPI = 122, 123
NSC = 124


class Buf:
    __slots__ = ("name", "writer", "readers")

    def __init__(self, name):
        self.name = name
        self.writer = None
        self.readers = []


class Op:
    __slots__ = ("eng", "fn", "deps", "needed", "dma", "slot", "val", "idx")


class Sched:
    ENGS = ("pe", "act", "dve", "pool", "sp")

    def __init__(self):
        self.ops = {e: [] for e in self.ENGS}
        self.nops = 0

    def add(self, eng, fn, reads=(), writes=(), dma=False, slot=None):
        op = Op()
        op.eng, op.fn, op.dma, op.slot = eng, fn, dma, slot
        op.needed = False
        op.val = None
        op.idx = self.nops
        self.nops += 1
        deps = {}
        for b in reads:
            w = b.writer
            if w is not None and not (w.eng == eng and eng == "pe" and not w.dma):
                deps[w.idx] = w
        for b in writes:
            w = b.writer
            if w is not None and not (w.eng == eng and eng == "pe" and not w.dma):
                deps[w.idx] = w
            for r in b.readers:
                if r.eng == eng and eng == "pe" and not r.dma and not dma:
                    continue
                deps[r.idx] = r
        deps.pop(op.idx, None)
        op.deps = list(deps.values())
        for d in op.deps:
            d.needed = True
        for b in reads:
            b.readers.append(op)
        for b in writes:
            b.writer = op
            b.readers = []
        self.ops[eng].append(op)
        return op

    def emit(self, nc, final_waits=()):
        for op in final_waits:
            op.needed = True
        with contextlib.ExitStack() as st:
            esem = {e: st.enter_context(nc.semaphore("sem_" + e)) for e in self.ENGS}
            slot_sem, slot_cnt = {}, {}
            cnt = {e: 0 for e in self.ENGS}
            for e in self.ENGS:
                for op in self.ops[e]:
                    if op.dma:
                        key = op.slot
                        if key not in slot_sem:
                            slot_sem[key] = st.enter_context(nc.semaphore("dsem%d" % len(slot_sem)))
                            slot_cnt[key] = 0
                        slot_cnt[key] += 16
                        op.val = (slot_sem[key], slot_cnt[key])
                    elif op.needed:
                        cnt[e] += 1
                        op.val = (esem[e], cnt[e])
            block = st.enter_context(nc.Block())

            def run(engname, eng, extra=None):
                waited = {}
                for op in self.ops[engname]:
                    for d in op.deps:
                        sem, v = d.val
                        k = id(sem)
                        if waited.get(k, 0) >= v:
                            continue
                        waited[k] = v
                        eng.wait_ge(sem, v)
                    ins = op.fn(eng)
                    if op.dma:
                        ins.then_inc(op.val[0], 16)
                    elif op.needed:
                        ins.then_inc(op.val[0], 1)
                if extra:
                    for op in extra:
                        sem, v = op.val
                        eng.wait_ge(sem, v)

            @block.tensor
            def _(eng):
                run("pe", eng)

            @block.scalar
            def _(eng):
                run("act", eng)

            @block.vector
            def _(eng):
                run("dve", eng)

            @block.gpsimd
            def _(eng):
                run("pool", eng)

            @block.sync
            def _(eng):
                run("sp", eng, extra=final_waits)


def build(stop=None, dumps=()):
    nc = bass.Bass("TRN2", target_bir_lowering=False, dynamic_dma_scratch_size=8192)

    def din(name, shape, dt=F32):
        return nc.dram_tensor(name, list(shape), dt, kind="ExternalInput").ap()

    x_own = din("x_own", [128, 8, T])
    x_oth = din("x_oth", [128, 8, T])
    memT = din("memT", [128, 8, 256])
    pos_own = din("pos_own", [64, T], I32)
    pos_oth = din("pos_oth", [64, T], I32)
    wgu_d = [din("wgu1", [NF, 128, 2048]), din("wgu2", [NF, 128, 2048])]
    wd_d = [din("wd1", [4, 8, 128, 768]), din("wd2", [4, 8, 128, 768])]
    w_in_d = din("w_in", [128, 8192])
    wq_d = din("wq", [128, 2048])
    wkv_d = din("wkv", [128, 1024])
    poolw_d = din("poolw", [128, 512])
    w_out_d = din("w_out", [128, 8192])
    w_mq_d = din("w_mq", [128, 8192])
    w_mo_d = din("w_mo", [128, 8192])
    w_mkv_d = din("w_mkv", [8, 128, 2048])
    tri_d = din("tri", [128, 128])
    smallc_d = din("smallc", [128, NSC])
    kbias_d = din("kbias", [1, 4096])
    outT = nc.dram_tensor("outT", [128, 8, T], F32, kind="ExternalOutput").ap()

    S = Sched()
    bufs = {}

    SCRATCH_KEYS = {"xn", "at", "sq", "rstd", "sg", "un", "zp", "tmp", "dy", "wop", "ct", "st", "posi", "ang", "rs",
                    "t1", "t2", "qnt", "ts16", "kn", "v", "pt", "rc", "oh", "woh", "hn", "qm", "om", "ptm", "rcm",
                    "outb", "memf", "memn", "wm", "accd", "accp", "sumt", "ones32"}
    cur_fence = [None]

    def B(*key):
        b = bufs.get(key)
        if b is None:
            b = bufs[key] = Buf(str(key))
            if key[0] in SCRATCH_KEYS and cur_fence[0] is not None:
                b.writer = cur_fence[0]
        return b

    def fence():
        old = [b for k, b in bufs.items() if k[0] in SCRATCH_KEYS]
        cur_fence[0] = S.add("dve", lambda e: e.engine_nop(), [], old)

    st = contextlib.ExitStack()
    arena = st.enter_context(nc.sbuf_tensor("arena", [128, 208 * 256], F32))
    ps = st.enter_context(nc.psum_tensor("ps", [128, 8, 512], F32))
    A = arena[:]

    def reg(off, nbytes, dt=F32):
        v = A[:, off // 4:(off + nbytes) // 4]
        if dt != F32:
            v = v.bitcast(dt)
        return v

    K = 1024
    H = reg(0, 64 * K).rearrange("p (c t) -> p c t", c=8)
    SCR = 64 * K
    W16 = reg(128 * K, 16 * K, BF16).rearrange("p (c n) -> p c n", c=8)
    KVN = reg(144 * K, 8 * K, BF16)
    KR = reg(152 * K, 8 * K, BF16)
    QNP = reg(160 * K, 16 * K, BF16).rearrange("p (h t) -> p h t", h=4)
    WGUf = [reg(160 * K + i * 4 * K, 4 * K, BF16) for i in range(2)]
    WGU = [w.rearrange("p (g c n) -> p g c n", g=2, c=8) for w in WGUf]
    WDf = [reg(168 * K + i * 1536, 1536, BF16) for i in range(3)]
    WD = [w.rearrange("p (f n) -> p f n", f=6) for w in WDf]
    W16f = reg(128 * K, 16 * K, BF16)
    QPE = reg(176 * K, 16 * K, BF16).rearrange("p (h t) -> p h t", h=4)
    HALO = reg(176 * K - 256, 256).rearrange("p (g t) -> p g t", g=4)
    CB = 192 * K
    WQ = reg(CB, 4096, BF16).rearrange("p (c n) -> p c n", c=2)
    WKV = reg(CB + 4096, 2048, BF16)
    POOLW = reg(CB + 6144, 1024, BF16).rearrange("p (g n) -> p g n", g=4)
    ONES = reg(CB + 7168, 256, BF16)
    TRI = reg(CB + 7424, 256, BF16)
    KM = reg(CB + 7680, 4096, BF16).rearrange("p (c m) -> p c m", c=8)
    VM = reg(CB + 11776, 4096, BF16).rearrange("p (b n) -> p b n", b=2)
    SC = reg(CB + 15872, NSC * 4)

    def scr(off, nbytes, dt=F32):
        assert off + nbytes <= 64 * K, (off, nbytes)
        return reg(SCR + off, nbytes, dt)

    XN = scr(0, 32 * K, BF16).rearrange("p (c t) -> p c t", c=8)
    AT = scr(32 * K, 24 * K, BF16).rearrange("p (f t) -> p f t", f=6)
    SQ = [scr(56 * K + i * K, K, BF16) for i in range(3)]
    RSTD = [scr(59 * K, 2 * K)]
    QNT = scr(61 * K, 2 * K, BF16).rearrange("p (c t) -> p c t", c=2)
    SG = [scr(63 * K, K, BF16)]

    bank_ctr = [0]

    def nb(lo=0, hi=8):
        b = lo + bank_ctr[0] % (hi - lo)
        bank_ctr[0] += 1
        return b

    def mm(out, lhsT, rhs, start, stop, reads, writes):
        return S.add("pe", lambda e: e.matmul(out, lhsT=lhsT, rhs=rhs, start=start, stop=stop), reads, writes)

    def act(out, in_, func, reads, writes, **kw):
        return S.add("act", lambda e: e.activation(out=out, in_=in_, func=func, **kw), reads, writes)

    def tt(out, in0, in1, op, reads, writes, eng="dve"):
        return S.add(eng, lambda e: e.tensor_tensor(out=out, in0=in0, in1=in1, op=op), reads, writes)

    def ts(out, in0, s1, s2, op0, op1, reads, writes, eng="dve"):
        if op1 is None:
            return S.add(eng, lambda e: e.tensor_scalar(out=out, in0=in0, scalar1=s1, scalar2=None, op0=op0), reads, writes)
        return S.add(eng, lambda e: e.tensor_scalar(out=out, in0=in0, scalar1=s1, scalar2=s2, op0=op0, op1=op1), reads, writes)

    def stt(out, in0, scalar, in1, op0, op1, reads, writes, eng="dve"):
        return S.add(eng, lambda e: e.scalar_tensor_tensor(out=out, in0=in0, scalar=scalar, in1=in1, op0=op0, op1=op1), reads, writes)

    def cp(out, in_, reads, writes, eng="dve"):
        if eng == "act":
            return S.add("act", lambda e: e.copy(out=out, in_=in_), reads, writes)
        return S.add(eng, lambda e: e.tensor_copy(out=out, in_=in_), reads, writes)

    def dma(q, out, in_, reads, writes, slot):
        return S.add(q, lambda e: e.dma_start(out=out, in_=in_), reads, writes, dma=True, slot=slot)

    BPS = [B("ps", i) for i in range(8)]
    BC = B("c_sc")
    BONES = B("c_ones")
    BTRI = B("c_tri")

    sq_ctr = [0]

    def emit_norm(srcs, src_bufs, N, gcol, dtot, dsts, dst_bufs):
        C = len(srcs)
        b = nb()
        for c in range(C):
            sl = sq_ctr[0] % 3
            sq_ctr[0] += 1
            act(SQ[sl][:, 0:N], srcs[c], AF.Square, [src_bufs[c]], [B("sq", sl)])
            mm(ps[:, b, 0:N], ONES, SQ[sl][:, 0:N], c == 0, c == C - 1, [B("sq", sl), BONES], [BPS[b]])
        act(RSTD[0][:, 0:N], ps[:, b, 0:N], AF.Ln, [BPS[b]], [B("rstd", 0)], scale=1.0 / dtot, bias=EPS)
        act(RSTD[0][:, 0:N], RSTD[0][:, 0:N], AF.Exp, [B("rstd", 0)], [B("rstd", 0)], scale=-0.5)
        for c in range(C):
            stt(dsts[c], srcs[c], SC[:, gcol + c:gcol + c + 1], RSTD[0][:, 0:N], ALU.mult, ALU.mult,
                [src_bufs[c], B("rstd", 0), BC], [dst_bufs[c]])

    def tsl(t):
        return slice(t * 512, (t + 1) * 512)

    def BH(t, m):
        return B("H", t, m)

    dma("sp", SC, smallc_d, [], [BC], "c_sc")
    S.add("dve", lambda e: e.memset(ONES, 1.0), [], [BONES])
    dma("pool", TRI, tri_d, [], [BTRI], "c_tri")
    dma("pool", reg(CB, 4096, BF16), wq_d, [], [B("wq")], "c_wq")
    dma("pool", WKV, wkv_d, [], [B("wkv")], "c_wkv")
    dma("pool", reg(CB + 6144, 1024, BF16), poolw_d, [], [B("poolw")], "c_pw")
    dma("pool", KR[64:65, 0:2048], kbias_d[:, 0:2048], [], [B("krb")], "c_kb")
    dma("pool", KR[64:65, 2048:4096], kbias_d[:, 2048:4096], [], [B("krb")], "c_kb")
    for h in range(4):
        S.add("dve", lambda e, h=h: e.memset(QPE[64:65, h, :], 1.0), [], [B("qpeb")])

    wgu_ctr = [0]
    wd_ctr = [0]

    def emit_ffn(which, gcol, x_dram=None):
        for t in range(NT):
            if x_dram is not None:
                dma("sp", H[:, :, tsl(t)], x_dram[:, :, tsl(t)], [], [BH(t, m) for m in range(8)], ("H", t))
            emit_norm([H[:, c, tsl(t)] for c in range(8)], [BH(t, c) for c in range(8)], 512, gcol, D,
                      [XN[:, c, tsl(t)] for c in range(8)], [B("xn", t)] * 8)
        for j, (f0, f1) in enumerate(PASSES):
            nf = f1 - f0
            for f in range(f0, f1):
                sl = wgu_ctr[0] % 2
                wgu_ctr[0] += 1
                dma("pool", WGUf[sl], wgu_d[which][f], [], [B("wgu", sl)], ("wgu", sl))
                for t in range(NT):
                    bg = nb()
                    for kc in range(8):
                        mm(ps[:, bg, :], WGU[sl][:, 0, kc, :], XN[:, kc, tsl(t)], kc == 0, kc == 7,
                           [B("wgu", sl), B("xn", t)], [BPS[bg]])
                    bu = nb()
                    for kc in range(8):
                        mm(ps[:, bu, :], WGU[sl][:, 1, kc, :], XN[:, kc, tsl(t)], kc == 0, kc == 7,
                           [B("wgu", sl), B("xn", t)], [BPS[bu]])
                    act(SG[0], ps[:, bg, :], AF.Silu, [BPS[bg]], [B("sg")])
                    tt(AT[:, f - f0, tsl(t)], ps[:, bu, :], SG[0], ALU.mult, [BPS[bu], B("sg")], [B("at", f - f0, t)])
            for m in range(8):
                sl = wd_ctr[0] % 3
                wd_ctr[0] += 1
                dma("pool", WDf[sl][:, 0:nf * 128], wd_d[which][j, m][:, 0:nf * 128], [], [B("wd", sl)], ("wd", sl))
                for t in range(NT):
                    b = nb()
                    for fl in range(nf):
                        mm(ps[:, b, :], WD[sl][:, fl, :], AT[:, fl, tsl(t)], fl == 0, fl == nf - 1,
                           [B("wd", sl), B("at", fl, t)], [BPS[b]])
                    stt(H[:, m, tsl(t)], ps[:, b, :], 0.5, H[:, m, tsl(t)], ALU.mult, ALU.add,
                        [BPS[b], BH(t, m)], [BH(t, m)])

    UN = [scr(i * 8 * K, 8 * K, BF16).rearrange("p (c t) -> p c t", c=8) for i in range(2)]
    ZP = scr(16 * K, 4 * 528 * 4).rearrange("p (g t) -> p g t", g=4)
    o = 16 * K + 8448
    TMP = [scr(o + i * 2112, 2112) for i in range(2)]
    o += 2 * 2112
    DY = scr(o, 4 * K, BF16).rearrange("p (g t) -> p g t", g=4)
    o += 4 * K
    WOPf = scr(o, 8 * K, BF16)
    WOP = WOPf.rearrange("p (c n) -> p c n", c=4)
    o += 8 * K
    CT = scr(o, 2 * K)
    STb = scr(o + 2 * K, 2 * K)
    o += 4 * K
    POSI = scr(o, 2 * K, I32)
    ANG = scr(o + 2 * K, 2 * K)
    RS = scr(o + 4 * K, 2 * K)
    o += 6 * K
    T1 = scr(o, 2 * K)
    T2 = ANG
    o += 2 * K
    RSTD.append(scr(o, 2 * K))
    o += 2 * K
    TS16 = scr(o, 64)
    o += 64
    assert o <= 56 * K, o

    class PA:
        def __init__(self):
            self.free = list(range(8))

        def alloc(self):
            assert self.free, "PSUM banks exhausted"
            return self.free.pop(0)

        def release(self, b):
            self.free.append(b)

    pa = PA()

    def norm_g(srcs, src_bufs, N, gcol, dtot, dsts, dst_bufs, ri, rel=()):
        C = len(srcs)
        b = pa.alloc()
        for c in range(C):
            sl = sq_ctr[0] % 3
            sq_ctr[0] += 1
            act(SQ[sl][:, 0:N], srcs[c], AF.Square, [src_bufs[c]], [B("sq", sl)])
            mm(ps[:, b, 0:N], ONES, SQ[sl][:, 0:N], c == 0, c == C - 1, [B("sq", sl), BONES], [BPS[b]])
            yield
        rb = B("rstd", ri)
        act(RSTD[ri][:, 0:N], ps[:, b, 0:N], AF.Ln, [BPS[b]], [rb], scale=1.0 / dtot, bias=EPS)
        act(RSTD[ri][:, 0:N], RSTD[ri][:, 0:N], AF.Exp, [rb], [rb], scale=-0.5)
        pa.release(b)
        yield
        for c in range(C):
            stt(dsts[c], srcs[c], SC[:, gcol + c:gcol + c + 1], RSTD[ri][:, 0:N], ALU.mult, ALU.mult,
                [src_bufs[c], rb, BC], [dst_bufs[c]])
            yield
        for b_ in rel:
            pa.release(b_)

    def tables_g(pos_d, t):
        dma("sp", POSI[0:64, :], pos_d[:, tsl(t)], [], [B("posi")], "posi")
        cp(ANG[0:64, :], POSI[0:64, :], [B("posi")], [B("ang")])
        ts(ANG[0:64, :], ANG[0:64, :], SC[0:64, C_FREQ:C_FREQ + 1], None, ALU.mult, None, [B("ang"), BC], [B("ang")])
        yield
        KI = POSI
        for (dst, db, off, use_sgn) in ((STb, "st", 0.5, True), (CT, "ct", 0.75, False)):
            ts(RS[0:64, :], ANG[0:64, :], 1.0 / (2.0 * PI), off, ALU.mult, ALU.add, [B("ang")], [B("rs")])
            cp(KI[0:64, :], RS[0:64, :], [B("rs")], [B("posi")])
            yield
            cp(T1[0:64, :], KI[0:64, :], [B("posi")], [B("t1")])
            tt(RS[0:64, :], RS[0:64, :], T1[0:64, :], ALU.subtract, [B("rs"), B("t1")], [B("rs")])
            yield
            stt(RS[0:64, :], RS[0:64, :], 0.0, RS[0:64, :], ALU.is_lt, ALU.add, [B("rs")], [B("rs")])
            if use_sgn:
                act(dst[0:64, :], RS[0:64, :], AF.Sin, [B("rs"), BC], [B(db)],
                    scale=SC[0:64, C_S2PI:C_S2PI + 1], bias=SC[0:64, C_SPI:C_SPI + 1])
            else:
                act(dst[0:64, :], RS[0:64, :], AF.Sin, [B("rs")], [B(db)], scale=2.0 * PI, bias=-PI)
            yield

    def rope_ops(za_b, zb_b, dst, dst_buf):
        tt(T1[0:64, :], ps[0:64, za_b, :], CT[0:64, :], ALU.mult, [BPS[za_b], B("ct")], [B("t1")])
        tt(T2[0:64, :], ps[0:64, zb_b, :], STb[0:64, :], ALU.mult, [BPS[zb_b], B("st")], [B("ang")])
        pa.release(za_b)
        pa.release(zb_b)
        tt(dst, T1[0:64, :], T2[0:64, :], ALU.add, [B("t1"), B("ang")], [dst_buf], eng="pool")

    def proj_t(t, c0, c1, part=128):
        u, ub = UN[t % 2], B("un", t % 2)
        b = pa.alloc()
        for kc in range(8):
            mm(ps[0:part, b, :], W16[:, kc, c0:c1], u[:, kc, :], kc == 0, kc == 7, [B("w16"), ub], [BPS[b]])
        return b

    def p1_g(t):
        u, ub = UN[t % 2], B("un", t % 2)
        yield from norm_g([H[:, c, tsl(t)] for c in range(8)], [BH(t, c) for c in range(8)], 512, C_MIX, D,
                          [u[:, c, :] for c in range(8)], [ub] * 8, 0)

    def p2_g(t, own):
        kt = (4 if own else 0) + t
        ksl = slice(kt * 512, (kt + 1) * 512)
        tg = tables_g(pos_own if own else pos_oth, t)
        next(tg)
        b = proj_t(t, 256, 384)
        yield
        ng = norm_g([ps[:, b, :]], [BPS[b]], 512, C_KVN, 128, [KVN[:, ksl]], [B("kvn", kt)], 1, rel=(b,))
        for _ in ng:
            next(tg, None)
            yield
        for _ in tg:
            yield
        ba = proj_t(t, 384, 448, 64)
        yield
        bb = proj_t(t, 960, 1024, 64)
        yield
        if own:
            bq = [proj_t(t, 0, 128)]
            yield
            bq.append(proj_t(t, 128, 256))
            yield
        rope_ops(ba, bb, KR[0:64, ksl], B("kr", kt))
        yield
        if own:
            yield from norm_g([ps[:, bq[0], :], ps[:, bq[1], :]], [BPS[bq[0]], BPS[bq[1]]], 512, C_QN, 256,
                              [QNT[:, 0, :], QNT[:, 1, :]], [B("qnt")] * 2, 1, rel=tuple(bq))
            for h in range(4):
                b = pa.alloc()
                for kc in range(2):
                    mm(ps[:, b, :], WQ[:, kc, h * 256:h * 256 + 128], QNT[:, kc, :], kc == 0, kc == 1, [B("wq"), B("qnt")], [BPS[b]])
                ba = pa.alloc()
                for kc in range(2):
                    mm(ps[0:64, ba, :], WQ[:, kc, h * 256 + 128:h * 256 + 192], QNT[:, kc, :], kc == 0, kc == 1, [B("wq"), B("qnt")], [BPS[ba]])
                bb = pa.alloc()
                for kc in range(2):
                    mm(ps[0:64, bb, :], WQ[:, kc, h * 256 + 192:h * 256 + 256], QNT[:, kc, :], kc == 0, kc == 1, [B("wq"), B("qnt")], [BPS[bb]])
                yield
                cp(QNP[:, h, tsl(t)], ps[:, b, :], [BPS[b]], [B("qnp", h, t)], eng="act")
                pa.release(b)
                rope_ops(ba, bb, QPE[0:64, h, tsl(t)], B("qpe", h, t))
                yield

    def p3_g(t, own):
        for g in range(4):
            b = proj_t(t, 448 + g * 128, 576 + g * 128)
            yield
            if not own:
                ts(HALO[:, g, :], ps[:, b, 496:512], SC[:, C_HALO:C_HALO + 1], None, ALU.mult, None,
                   [BPS[b], BC], [B("halo")])
                pa.release(b)
                continue
            if t == 0:
                cp(ZP[:, g, 0:16], HALO[:, g, :], [B("halo"), B("qnp", 3, 3)], [B("zp", g)], eng="pool")
            cp(ZP[:, g, 16:528], ps[:, b, :], [BPS[b]], [B("zp", g)], eng="act")
            pa.release(b)
            yield
            cur, curb = ZP[:, g, :], B("zp", g)
            lo = 0
            for l in range(g + 1):
                step = 1 << l
                lo += step
                dst, dstb = TMP[l % 2], B("tmp", l % 2)
                tt(dst[:, lo:528], cur[:, lo:528], cur[:, lo - step:528 - step], ALU.add, [curb], [dstb], eng="pool")
                cur, curb = dst, dstb
                yield
            w = 1 << (g + 1)
            if t == 0:
                tt(TS16[:, 0:16], cur[:, 16:32], SC[:, C_INVC + g * 16:C_INVC + (g + 1) * 16], ALU.mult, [curb, BC], [B("ts16")], eng="pool")
            stt(DY[:, g, :], cur[:, 16:528], 1.0 / w, ZP[:, g, 16:528], ALU.mult, ALU.subtract,
                [curb, B("zp", g)], [B("dy", g)])
            if t == 0:
                tt(DY[:, g, 0:16], TS16[:, 0:16], ZP[:, g, 16:32], ALU.subtract, [B("ts16"), B("zp", g)], [B("dy", g)], eng="pool")
            if t < NT - 1:
                cp(ZP[:, g, 0:16], ZP[:, g, 512:528], [B("zp", g)], [B("zp", g)], eng="pool")
            yield
            b2 = pa.alloc()
            mm(ps[:, b2, :], POOLW[:, g, :], DY[:, g, :], True, True, [B("poolw"), B("dy", g)], [BPS[b2]])
            ts(DY[:, g, :], ps[:, b2, :], SC[:, C_PSC + g:C_PSC + g + 1], None, ALU.mult, None, [BPS[b2], BC], [B("dy", g)])
            pa.release(b2)
            yield
        if own:
            for m in range(8):
                b = pa.alloc()
                for g in range(4):
                    mm(ps[:, b, :], WOP[:, g, m * 128:(m + 1) * 128], DY[:, g, :], g == 0, g == 3, [B("wop"), B("dy", g)], [BPS[b]])
                tt(H[:, m, tsl(t)], ps[:, b, :], H[:, m, tsl(t)], ALU.add, [BPS[b], BH(t, m)], [BH(t, m)])
                pa.release(b)
                yield

    def interleave(*gens):
        gens = [g for g in gens if g is not None]
        while gens:
            for g in list(gens):
                try:
                    next(g)
                except StopIteration:
                    gens.remove(g)

    MEMF = scr(16 * K, 8 * K).rearrange("p (c m) -> p c m", c=8)
    MEMN = scr(24 * K, 4 * K, BF16).rearrange("p (c m) -> p c m", c=8)
    WMf = [scr(28 * K + i * 4 * K, 4 * K, BF16) for i in range(2)]
    WM = [w.rearrange("p (c n) -> p c n", c=8) for w in WMf]
    RSTD.append(scr(36 * K, K))
    RI_P0 = len(RSTD) - 1

    def p0_g():
        dma("sp", MEMF, memT, [], [B("memf")], "memf")
        yield from norm_g([MEMF[:, c, :] for c in range(8)], [B("memf")] * 8, 256, C_MEM, D,
                          [MEMN[:, c, :] for c in range(8)], [B("memn")] * 8, RI_P0)
        for cg in range(8):
            sl = cg % 2
            dma("pool", WMf[sl], w_mkv_d[cg], [], [B("wm", sl)], ("wm", sl))
            for half in range(2):
                b = pa.alloc()
                if cg < 4:
                    c = cg * 2 + half
                    for kc in range(8):
                        mm(ps[:, b, 0:256], WM[sl][:, kc, half * 128:(half + 1) * 128], MEMN[:, kc, :], kc == 0, kc == 7,
                           [B("wm", sl), B("memn")], [BPS[b]])
                    cp(KM[:, c, :], ps[:, b, 0:256], [BPS[b]], [B("km")])
                else:
                    for kc in range(8):
                        mm(ps[:, b, 0:256], MEMN[:, kc, half * 128:(half + 1) * 128], WM[sl][:, kc, :], kc == 0, kc == 7,
                           [B("wm", sl), B("memn")], [BPS[b]])
                    cp(VM[:, half, (cg - 4) * 256:(cg - 3) * 256], ps[:, b, 0:256], [BPS[b]], [B("vm")])
                pa.release(b)
                yield

    def take_g(it, n):
        for _ in range(n):
            try:
                next(it)
            except StopIteration:
                return
            yield

    def emit_inproj_phase(own):
        extra = None if own else p0_g()
        interleave(p1_g(0))
        for t in range(NT):
            streams = [p2_g(t, own)]
            if own or t == NT - 1:
                streams.append(p3_g(t, own))
            if t + 1 < NT:
                streams.append(p1_g(t + 1))
            if extra is not None:
                streams.append(take_g(extra, 12))
            interleave(*streams)
        if extra is not None:
            interleave(extra)

    fence()
    emit_ffn(0, C_FFN1, x_oth)
    fence()
    dma("pool", W16f, w_in_d, [], [B("w16")], "w16")
    emit_inproj_phase(False)
    if stop != "A":
        fence()
        emit_ffn(0, C_FFN1, x_own)
    if stop not in ("A", "B1"):
        fence()
        dma("pool", WOPf, w_out_d[:, 4096:8192], [], [B("wop")], "wop")
        emit_inproj_phase(True)

    if stop not in ("A", "B1", "B2"):
        fence()
        KNb = [scr(i * 8 * K, 8 * K, BF16) for i in range(2)]
        Vb = [scr(16 * K + i * 8 * K, 8 * K, BF16).rearrange("p (b n) -> p b n", b=32) for i in range(2)]
        PT = [scr(32 * K + i * K, K, BF16) for i in range(4)] + [scr(56 * K + i * K, K, BF16) for i in range(4)]
        NPT = len(PT)
        ACCV = {"dve": scr(52 * K, 2 * K), "pool": scr(54 * K, 2 * K)}
        ACCB = {"dve": "accd", "pool": "accp"}
        ONES32 = scr(60 * K, 512)
        SUMT = scr(61 * K, 2 * K)
        S.add("dve", lambda e: e.memset(ONES32, 1.0), [], [B("ones32")])
        RC = [scr(36 * K + i * 2 * K, 2 * K) for i in range(2)]
        OH = [scr(40 * K + i * 4 * K, 4 * K, BF16) for i in range(2)]
        WOH = [scr(48 * K + i * 2 * K, 2 * K, BF16) for i in range(2)]
        sc_attn = float(192.0 ** -0.5)

        pending = []

        def flush():
            while pending:
                pending.pop(0)()

        def emit_upproj(h, now=False):
            hs = h % 2
            items = []
            items.append(lambda: dma("pool", WOH[hs], w_out_d[:, h * 1024:(h + 1) * 1024], [], [B("woh", hs)], ("woh", hs)))

            def kgrp(kt):
                b = nb(0, 4)
                mm(ps[:, b, :], WKV[:, h * 256:h * 256 + 128], KVN[:, kt * 512:(kt + 1) * 512], True, True,
                   [B("wkv"), B("kvn", kt)], [BPS[b]])
                cp(KNb[hs][:, kt * 512:(kt + 1) * 512], ps[:, b, :], [BPS[b]], [B("kn", hs, kt)])

            def vgrp(vb):
                b = nb(0, 4)
                for j in range(4):
                    kb = vb * 4 + j
                    mm(ps[:, b, j * 128:(j + 1) * 128], KVN[:, kb * 128:(kb + 1) * 128], WKV[:, h * 256 + 128:h * 256 + 256],
                       True, True, [B("wkv"), B("kvn", kb // 4)], [BPS[b]])
                cp(Vb[hs][:, vb * 4:(vb + 1) * 4, :], ps[:, b, :].rearrange("p (j n) -> p j n", j=4), [BPS[b]], [B("v", hs, vb)])

            for kt in range(8):
                items.append(lambda kt=kt: kgrp(kt))
            for vb in range(8):
                items.append(lambda vb=vb: vgrp(vb))
            if now:
                for it in items:
                    it()
            else:
                pending.extend(items)

        def emit_wout(h, t):
            hs = h % 2

            def grp(m):
                b = nb(0, 4)
                mm(ps[:, b, :], WOH[hs][:, m * 128:(m + 1) * 128], OH[hs][:, tsl(t)], True, True,
                   [B("woh", hs), B("oh", hs, t)], [BPS[b]])
                tt(H[:, m, tsl(t)], ps[:, b, :], H[:, m, tsl(t)], ALU.add, [BPS[b], BH(t, m)], [BH(t, m)])

            for m in range(8):
                pending.append(lambda m=m: grp(m))

        emit_upproj(0, now=True)
        units_all = []
        acc = 0
        for h in range(4):
            for t in range(NT):
                ul = [(kb, 0, False) for kb in range(16)]
                for j in range(4 * t + 4):
                    ul.append((16 + j, max(0, j - 4 * t), j >= 4 * t))
                for idx, (kb, r, diag) in enumerate(ul):
                    units_all.append(dict(h=h, t=t, idx=idx, kb=kb, r=r, diag=diag, first=(idx == 0),
                                          last=(idx == len(ul) - 1), acc=acc))
                acc += 1
        pt_ctr = [0]

        def stage_a(u):
            h, t, kb = u["h"], u["t"], u["kb"]
            hs = h % 2
            if u["idx"] == 0 and t == 0:
                flush()
            if u["idx"] == 4 and t == 1 and h < 3:
                emit_upproj(h + 1)
            if u["idx"] >= 3 and pending:
                pending.pop(0)()
            n0 = u["r"] * 128
            b = nb(0, 4)
            pi = pt_ctr[0] % NPT
            pt_ctr[0] += 1
            u["pi"] = pi
            qs = slice(t * 512 + n0, (t + 1) * 512)
            mm(ps[:, b, n0:512], KNb[hs][:, kb * 128:(kb + 1) * 128], QNP[:, h, qs], True, False,
               [B("kn", hs, kb // 4), B("qnp", h, t)], [BPS[b]])
            mm(ps[:, b, n0:512], KR[0:65, kb * 128:(kb + 1) * 128], QPE[0:65, h, qs], False, True,
               [B("kr", kb // 4), B("krb"), B("qpe", h, t), B("qpeb")], [BPS[b]])
            act(PT[pi][:, n0:512], ps[:, b, n0:512], AF.Exp, [BPS[b]], [B("pt", pi)], scale=sc_attn)
            if u["diag"]:
                tt(PT[pi][:, n0:n0 + 128], PT[pi][:, n0:n0 + 128], TRI, ALU.mult, [B("pt", pi), BTRI], [B("pt", pi)])

        def stage_b(u):
            h, t, kb, pi = u["h"], u["t"], u["kb"], u["pi"]
            hs = h % 2
            n0 = u["r"] * 128
            ob, sb, rcb = 4 + u["acc"] % 2, 6 + u["acc"] % 2, u["acc"] % 2
            mm(ps[:, ob, n0:512], Vb[hs][:, kb, :], PT[pi][:, n0:512], u["first"], u["last"], [B("v", hs, kb // 4), B("pt", pi)], [BPS[ob]])
            role = ("pool", "dve", "pe", "pool")[u["idx"] % 4]
            if role == "pe":
                mm(ps[:, sb, n0:512], ONES, PT[pi][:, n0:512], u["idx"] == 2, False, [BONES, B("pt", pi)], [BPS[sb]])
            else:
                av, ab = ACCV[role], B(ACCB[role])
                if u["idx"] < 2:
                    cp(av, PT[pi], [B("pt", pi)], [ab], eng=role)
                else:
                    tt(av[:, n0:512], av[:, n0:512], PT[pi][:, n0:512], ALU.add, [ab, B("pt", pi)], [ab], eng=role)
            if u["last"]:
                tt(SUMT, ACCV["dve"], ACCV["pool"], ALU.add, [B("accd"), B("accp")], [B("sumt")])
                mm(ps[:, sb, :], ONES32, SUMT, False, True, [B("ones32"), B("sumt")], [BPS[sb]])
                act(RC[rcb], ps[:, sb, :], AF.Ln, [BPS[sb]], [B("rc", rcb)])
                act(RC[rcb], RC[rcb], AF.Exp, [B("rc", rcb)], [B("rc", rcb)], scale=-1.0)
                tt(OH[hs][:, tsl(t)], ps[:, ob, :], RC[rcb], ALU.mult, [BPS[ob], B("rc", rcb)], [B("oh", hs, t)])
                emit_wout(h, t)

        LOOK = 2
        for i in range(len(units_all) + LOOK):
            if i < len(units_all):
                stage_a(units_all[i])
            if i >= LOOK:
                stage_b(units_all[i - LOOK])
        flush()

    if stop not in ("A", "B1", "B2", "B3"):
        fence()
        WMOf = reg(144 * K, 16 * K, BF16)
        WMO = WMOf.rearrange("p (c n) -> p c n", c=8)
        kv_dead = [B("kvn", kt) for kt in range(8)] + [B("kr", kt) for kt in range(8)] + [B("krb")]
        dma("pool", W16f, w_mq_d, [], [B("w16")], "w16")
        dma("pool", WMOf, w_mo_d, [], kv_dead, "wmo")
        HN = [scr(i * 8 * K, 8 * K, BF16).rearrange("p (c t) -> p c t", c=8) for i in range(2)]
        QM = [scr(16 * K + i * 8 * K, 8 * K, BF16).rearrange("p (c t) -> p c t", c=8) for i in range(2)]
        OM = [scr(32 * K + i * 8 * K, 8 * K, BF16).rearrange("p (c t) -> p c t", c=8) for i in range(2)]
        PTm = [scr(48 * K + i * K, K, BF16) for i in range(4)]
        RCm = [scr(52 * K + i * 2 * K, 2 * K) for i in range(2)]
        sc_mem = float(256.0 ** -0.5)

        def xa_norm_g(t):
            hn, hb = HN[t % 2], B("hn", t % 2)
            yield from norm_g([H[:, c, tsl(t)] for c in range(8)], [BH(t, c) for c in range(8)], 512, C_XAT, D,
                              [hn[:, c, :] for c in range(8)], [hb] * 8, 0)

        def xa_q_g(t):
            hn, hb = HN[t % 2], B("hn", t % 2)
            for c in range(8):
                b = pa.alloc()
                for kc in range(8):
                    mm(ps[:, b, :], W16[:, kc, c * 128:(c + 1) * 128], hn[:, kc, :], kc == 0, kc == 7, [B("w16"), hb], [BPS[b]])
                cp(QM[t % 2][:, c, :], ps[:, b, :], [BPS[b]], [B("qm", t % 2, c)], eng=("act" if c % 2 else "dve"))
                pa.release(b)
                yield

        def xa_heads_g(t):
            qm, om = QM[t % 2], OM[t % 2]

            def stage_a(h):
                pis = []
                for mb in range(2):
                    b = pa.alloc()
                    for dc in range(2):
                        mm(ps[:, b, :], KM[:, 2 * h + dc, mb * 128:(mb + 1) * 128], qm[:, 2 * h + dc, :], dc == 0, dc == 1,
                           [B("km"), B("qm", t % 2, 2 * h + dc)], [BPS[b]])
                    pi = (2 * h + mb) % 4
                    pis.append(pi)
                    act(PTm[pi], ps[:, b, :], AF.Exp, [BPS[b]], [B("ptm", pi)], scale=sc_mem)
                    pa.release(b)
                return pis

            pis = {0: stage_a(0)}
            yield
            for h in range(4):
                if h < 3:
                    pis[h + 1] = stage_a(h + 1)
                    yield
                p = pis[h]
                b = pa.alloc()
                for mb in range(2):
                    mm(ps[:, b, :], ONES, PTm[p[mb]], mb == 0, mb == 1, [BONES, B("ptm", p[mb])], [BPS[b]])
                rcb = h % 2
                act(RCm[rcb], ps[:, b, :], AF.Ln, [BPS[b]], [B("rcm", rcb)])
                act(RCm[rcb], RCm[rcb], AF.Exp, [B("rcm", rcb)], [B("rcm", rcb)], scale=-1.0)
                pa.release(b)
                for dvc in range(2):
                    b = pa.alloc()
                    for mb in range(2):
                        mm(ps[:, b, :], VM[:, mb, h * 256 + dvc * 128:h * 256 + (dvc + 1) * 128], PTm[p[mb]], mb == 0, mb == 1,
                           [B("vm"), B("ptm", p[mb])], [BPS[b]])
                    tt(om[:, 2 * h + dvc, :], ps[:, b, :], RCm[rcb], ALU.mult, [BPS[b], B("rcm", rcb)], [B("om", t % 2, 2 * h + dvc)])
                    pa.release(b)
                yield
            for m in range(8):
                b = pa.alloc()
                for kc in range(8):
                    mm(ps[:, b, :], WMO[:, kc, m * 128:(m + 1) * 128], om[:, kc, :], kc == 0, kc == 7,
                       [kv_dead[0], B("om", t % 2, kc)], [BPS[b]])
                tt(H[:, m, tsl(t)], ps[:, b, :], H[:, m, tsl(t)], ALU.add, [BPS[b], BH(t, m)], [BH(t, m)])
                pa.release(b)
                yield

        def chain_g(*gs):
            for g_ in gs:
                if g_ is not None:
                    yield from g_

        interleave(xa_norm_g(0))
        interleave(xa_q_g(0), xa_norm_g(1))
        for t in range(NT):
            interleave(xa_heads_g(t),
                       chain_g(xa_norm_g(t + 2) if t + 2 < NT else None, xa_q_g(t + 1) if t + 1 < NT else None))

    finals = []
    if stop is None:
        S.add("dve", lambda e: e.engine_nop(), [],
              [B("qnp", h, t) for h in range(4) for t in range(NT)] + [B("wgu", sl) for sl in range(2)] + [B("wd", sl) for sl in range(3)])
        fence()
        emit_ffn(1, C_FFN2)
        fence()
        OUTB = [scr(i * 16 * K, 16 * K).rearrange("p (c t) -> p c t", c=8) for i in range(2)]
        for t in range(NT):
            ob_, obb = OUTB[t % 2], B("outb", t % 2)
            emit_norm([H[:, c, tsl(t)] for c in range(8)], [BH(t, c) for c in range(8)], 512, C_FIN, D,
                      [ob_[:, c, :] for c in range(8)], [obb] * 8)
            finals.append(dma("sp", outT[:, :, tsl(t)], ob_, [obb], [], ("out", t % 2)))

    dump_specs = {
        "H": (H, [128, 8, T], F32),
        "KVN": (KVN, [128, 4096], BF16),
        "KR": (KR[0:65, :], [65, 4096], BF16),
        "QNP": (QNP, [128, 4, T], BF16),
        "QPE": (QPE[0:65], [65, 4, T], BF16),
        "KM": (KM, [128, 8, 256], BF16),
        "VM": (VM, [128, 2, 1024], BF16),
        "ZPH": (ZP[:, :, 0:16], [128, 4, 16], F32),
    }
    for name in dumps:
        ap, shape, dt = dump_specs[name]
        d = nc.dram_tensor("dbg_" + name, shape, dt, kind="ExternalOutput").ap()
        allb = list(bufs.values())
        finals.append(dma("sp", d, ap, allb, [], ("dbg", name)))

    S.emit(nc, final_waits=finals)
    st.close()
    return nc


def _fm(w):
    k, n = w.shape
    return np.ascontiguousarray(w.reshape(k // 128, 128, n).transpose(1, 0, 2)).reshape(128, -1)


def prepare_inputs(inp):
    f32 = np.float32
    x = np.asarray(inp["x"], f32)
    mem = np.asarray(inp["mem"], f32)
    pos = np.asarray(inp["positions"], np.int32)
    g = lambda k: np.asarray(inp[k], f32)

    def ffn_layout(wg, wu, wd):
        wg_r = wg.reshape(8, 128, NF, 128).transpose(2, 1, 0, 3)
        wu_r = wu.reshape(8, 128, NF, 128).transpose(2, 1, 0, 3)
        wgu = np.ascontiguousarray(np.stack([wg_r, wu_r], axis=2)).reshape(NF, 128, 2048)
        wd_r = wd.reshape(NF, 128, 8, 128)
        wdl = np.zeros((4, 8, 128, 6, 128), f32)
        for j, (f0, f1) in enumerate(PASSES):
            wdl[j, :, :, 0:f1 - f0, :] = wd_r[f0:f1].transpose(2, 1, 0, 3)
        return wgu, wdl.reshape(4, 8, 128, 768)

    wgu1, wd1 = ffn_layout(g("ffn1_w_gate")[0], g("ffn1_w_up")[0], g("ffn1_w_down")[0])
    wgu2, wd2 = ffn_layout(g("ffn2_w_gate")[0], g("ffn2_w_up")[0], g("ffn2_w_down")[0])
    w_in = g("w_in")[0]
    w_in_ext = np.concatenate([w_in, w_in[:, 416:448], w_in[:, 384:416]], axis=1)
    wqu = g("w_q_up")[0]
    wq_ext = np.zeros((256, 1024), f32)
    for h in range(4):
        o = h * 192
        wq_ext[:, h * 256:h * 256 + 192] = wqu[:, o:o + 192]
        wq_ext[:, h * 256 + 192:h * 256 + 224] = wqu[:, o + 160:o + 192]
        wq_ext[:, h * 256 + 224:h * 256 + 256] = wqu[:, o + 128:o + 160]
    w_mkv = g("w_mkv")[0]
    w_mkv_l = np.stack([_fm(w_mkv[:, cg * 256:(cg + 1) * 256]) for cg in range(8)])
    tri = (np.arange(128)[None, :] >= np.arange(128)[:, None]).astype(f32)
    freq = (1.0 / (np.float32(10000.0) ** (np.arange(0, 64, 2, dtype=f32) / np.float32(64)))).astype(f32)

    shared = {
        "wgu1": wgu1, "wd1": wd1, "wgu2": wgu2, "wd2": wd2,
        "w_in": _fm(w_in_ext), "wq": _fm(wq_ext), "wkv": np.ascontiguousarray(g("w_kv_up")[0]),
        "poolw": np.ascontiguousarray(g("pool_w")[0].transpose(1, 0, 2)).reshape(128, 512),
        "w_out": _fm(g("w_out")[0]), "w_mq": _fm(g("w_mq")[0]), "w_mo": _fm(g("w_mo")[0]),
        "w_mkv": w_mkv_l, "tri": tri,
    }
    sc0 = np.zeros((128, NSC), f32)
    for col, key in ((C_FFN1, "ffn1_norm"), (C_MIX, "mix_norm"), (C_XAT, "xattn_norm"), (C_FFN2, "ffn2_norm"), (C_MEM, "mem_norm")):
        sc0[:, col:col + 8] = g(key)[0].reshape(8, 128).T
    sc0[:, C_FIN:C_FIN + 8] = g("final_norm").reshape(8, 128).T
    sc0[:, C_QN:C_QN + 2] = g("q_norm")[0].reshape(2, 128).T
    sc0[:, C_KVN] = g("kv_norm")[0]
    sc0[:, C_PSC:C_PSC + 4] = g("pool_scale")[0].reshape(4, 128).T
    sc0[0:32, C_FREQ] = freq
    sc0[32:64, C_FREQ] = freq
    sc0[0:32, C_SGN] = -1.0
    sc0[32:64, C_SGN] = 1.0
    sc0[:, C_S2PI] = sc0[:, C_SGN] * np.float32(2.0 * PI)
    sc0[:, C_SPI] = -sc0[:, C_SGN] * np.float32(PI)

    in_maps = []
    for c in range(8):
        b, half = c // 2, c % 2
        own = slice(half * T, (half + 1) * T)
        oth = slice((1 - half) * T, (2 - half) * T)
        sc = sc0.copy()
        sc[:, C_HALO] = float(half)
        for gi, w in enumerate((2, 4, 8, 16)):
            for i in range(16):
                cntv = min(i + 1, w) if half == 0 else w
                sc[:, C_INVC + gi * 16 + i] = np.float32(1.0) / np.float32(cntv)
        kb = np.zeros((1, 4096), f32)
        if half == 0:
            kb[0, 0:T] = NEG
        m = dict(shared)
        m.update({
            "x_own": np.ascontiguousarray(x[b, own].T).reshape(8, 128, T).transpose(1, 0, 2).copy(),
            "x_oth": np.ascontiguousarray(x[b, oth].T).reshape(8, 128, T).transpose(1, 0, 2).copy(),
            "memT": np.ascontiguousarray(mem[b].T).reshape(8, 128, 256).transpose(1, 0, 2).copy(),
            "pos_own": np.ascontiguousarray(np.broadcast_to(pos[b, own][None, :], (64, T))),
            "pos_oth": np.ascontiguousarray(np.broadcast_to(pos[b, oth][None, :], (64, T))),
            "smallc": sc, "kbias": kb,
        })
        in_maps.append(m)
    return in_maps


_NC_CACHE = {}


def kernel(**inputs):
    in_maps = prepare_inputs(inputs)
    if "nc" not in _NC_CACHE:
        _NC_CACHE["nc"] = build()
    nc = _NC_CACHE["nc"]
    res = run_bass_kernel_spmd(nc, in_maps, core_ids=list(range(8)))
    out = np.empty((4, 4096, D), np.float32)
    for c in range(8):
        b, half = c // 2, c % 2
        o = np.asarray(res.results[c]["outT"], np.float32)
        out[b, half * T:(half + 1) * T, :] = o.transpose(2, 1, 0).reshape(T, D)
    return out
```

```python
import contextlib

import numpy as np
import concourse.bass as bass
import concourse.mybir as mybir
from concourse.bass_utils import run_bass_kernel_spmd

F32 = mybir.dt.float32
BF16 = mybir.dt.bfloat16
I32 = mybir.dt.int32
ALU = mybir.AluOpType
AF = mybir.ActivationFunctionType

T = 2048
NT = 4
D = 1024
DFF = 2816
NF = 22
PASSES = [(0, 6), (6, 12), (12, 17), (17, 22)]
EPS = 1e-6
PI = float(np.float32(np.pi))
NEG = -30000.0

C_FFN1, C_MIX, C_XAT, C_FFN2, C_FIN, C_MEM = 0, 8, 16, 24, 32, 40
C_QN, C_KVN, C_PSC, C_FREQ, C_SGN, C_HALO, C_INVC = 48, 50, 51, 55, 56, 57, 58
C_S2PI, C_SPI = 122, 123
NSC = 124


class Buf:
    __slots__ = ("name", "writer", "readers")

    def __init__(self, name):
        self.name = name
        self.writer = None
        self.readers = []


class Op:
    __slots__ = ("eng", "fn", "deps", "needed", "dma", "slot", "val", "idx")


class Sched:
    ENGS = ("pe", "act", "dve", "pool", "sp")

    def __init__(self):
        self.ops = {e: [] for e in self.ENGS}
        self.nops = 0

    def add(self, eng, fn, reads=(), writes=(), dma=False, slot=None):
        op = Op()
        op.eng, op.fn, op.dma, op.slot = eng, fn, dma, slot
        op.needed = False
        op.val = None
        op.idx = self.nops
        self.nops += 1
        deps = {}
        for b in reads:
            w = b.writer
            if w is not None and not (w.eng == eng and eng == "pe" and not w.dma):
                deps[w.idx] = w
        for b in writes:
            w = b.writer
            if w is not None and not (w.eng == eng and eng == "pe" and not w.dma):
                deps[w.idx] = w
            for r in b.readers:
                if r.eng == eng and eng == "pe" and not r.dma and not dma:
                    continue
                deps[r.idx] = r
        deps.pop(op.idx, None)
        op.deps = list(deps.values())
        for d in op.deps:
            d.needed = True
        for b in reads:
            b.readers.append(op)
        for b in writes:
            b.writer = op
            b.readers = []
        self.ops[eng].append(op)
        return op

    def emit(self, nc, final_waits=()):
        for op in final_waits:
            op.needed = True
        with contextlib.ExitStack() as st:
            esem = {e: st.enter_context(nc.semaphore("sem_" + e)) for e in self.ENGS}
            slot_sem, slot_cnt = {}, {}
            cnt = {e: 0 for e in self.ENGS}
            for e in self.ENGS:
                for op in self.ops[e]:
                    if op.dma:
                        key = op.slot
                        if key not in slot_sem:
                            slot_sem[key] = st.enter_context(nc.semaphore("dsem%d" % len(slot_sem)))
                            slot_cnt[key] = 0
                        slot_cnt[key] += 16
                        op.val = (slot_sem[key], slot_cnt[key])
                    elif op.needed:
                        cnt[e] += 1
                        op.val = (esem[e], cnt[e])
            block = st.enter_context(nc.Block())

            def run(engname, eng, extra=None):
                waited = {}
                for op in self.ops[engname]:
                    for d in op.deps:
                        sem, v = d.val
                        k = id(sem)
                        if waited.get(k, 0) >= v:
                            continue
                        waited[k] = v
                        eng.wait_ge(sem, v)
                    ins = op.fn(eng)
                    if op.dma:
                        ins.then_inc(op.val[0], 16)
                    elif op.needed:
                        ins.then_inc(op.val[0], 1)
                if extra:
                    for op in extra:
                        sem, v = op.val
                        eng.wait_ge(sem, v)

            @block.tensor
            def _(eng):
                run("pe", eng)

            @block.scalar
            def _(eng):
                run("act", eng)

            @block.vector
            def _(eng):
                run("dve", eng)

            @block.gpsimd
            def _(eng):
                run("pool", eng)

            @block.sync
            def _(eng):
                run("sp", eng, extra=final_waits)


def build(stop=None, dumps=()):
    nc = bass.Bass("TRN2", target_bir_lowering=False, dynamic_dma_scratch_size=8192)

    def din(name, shape, dt=F32):
        return nc.dram_tensor(name, list(shape), dt, kind="ExternalInput").ap()

    x_own = din("x_own", [128, 8, T])
    x_oth = din("x_oth", [128, 8, T])
    memT = din("memT", [128, 8, 256])
    pos_own = din("pos_own", [64, T], I32)
    pos_oth = din("pos_oth", [64, T], I32)
    wgu_d = [din("wgu1", [NF, 128, 2048]), din("wgu2", [NF, 128, 2048])]
    wd_d = [din("wd1", [4, 8, 128, 768]), din("wd2", [4, 8, 128, 768])]
    w_in_d = din("w_in", [128, 8192])
    wq_d = din("wq", [128, 2048])
    wkv_d = din("wkv", [128, 1024])
    poolw_d = din("poolw", [128, 512])
    w_out_d = din("w_out", [128, 8192])
    w_mq_d = din("w_mq", [128, 8192])
    w_mo_d = din("w_mo", [128, 8192])
    w_mkv_d = din("w_mkv", [8, 128, 2048])
    tri_d = din("tri", [128, 128])
    smallc_d = din("smallc", [128, NSC])
    kbias_d = din("kbias", [1, 4096])
    outT = nc.dram_tensor("outT", [128, 8, T], F32, kind="ExternalOutput").ap()

    S = Sched()
    bufs = {}

    SCRATCH_KEYS = {"xn", "at", "sq", "rstd", "sg", "un", "zp", "tmp", "dy", "wop", "ct", "st", "posi", "ang", "rs",
                    "t1", "t2", "qnt", "ts16", "kn", "v", "pt", "rc", "oh", "woh", "hn", "qm", "om", "ptm", "rcm",
                    "outb", "memf", "memn", "wm", "accd", "accp", "sumt", "ones32"}
    cur_fence = [None]

    def B(*key):
        b = bufs.get(key)
        if b is None:
            b = bufs[key] = Buf(str(key))
            if key[0] in SCRATCH_KEYS and cur_fence[0] is not None:
                b.writer = cur_fence[0]
        return b

    def fence():
        old = [b for k, b in bufs.items() if k[0] in SCRATCH_KEYS]
        cur_fence[0] = S.add("dve", lambda e: e.engine_nop(), [], old)

    st = contextlib.ExitStack()
    arena = st.enter_context(nc.sbuf_tensor("arena", [128, 208 * 256], F32))
    ps = st.enter_context(nc.psum_tensor("ps", [128, 8, 512], F32))
    A = arena[:]

    def reg(off, nbytes, dt=F32):
        v = A[:, off // 4:(off + nbytes) // 4]
        if dt != F32:
            v = v.bitcast(dt)
        return v

    K = 1024
    H = reg(0, 64 * K).rearrange("p (c t) -> p c t", c=8)
    SCR = 64 * K
    W16 = reg(128 * K, 16 * K, BF16).rearrange("p (c n) -> p c n", c=8)
    KVN = reg(144 * K, 8 * K, BF16)
    KR = reg(152 * K, 8 * K, BF16)
    QNP = reg(160 * K, 16 * K, BF16).rearrange("p (h t) -> p h t", h=4)
    WGUf = [reg(160 * K + i * 4 * K, 4 * K, BF16) for i in range(2)]
    WGU = [w.rearrange("p (g c n) -> p g c n", g=2, c=8) for w in WGUf]
    WDf = [reg(168 * K + i * 1536, 1536, BF16) for i in range(3)]
    WD = [w.rearrange("p (f n) -> p f n", f=6) for w in WDf]
    W16f = reg(128 * K, 16 * K, BF16)
    QPE = reg(176 * K, 16 * K, BF16).rearrange("p (h t) -> p h t", h=4)
    HALO = reg(176 * K - 256, 256).rearrange("p (g t) -> p g t", g=4)
    CB = 192 * K
    WQ = reg(CB, 4096, BF16).rearrange("p (c n) -> p c n", c=2)
    WKV = reg(CB + 4096, 2048, BF16)
    POOLW = reg(CB + 6144, 1024, BF16).rearrange("p (g n) -> p g n", g=4)
    ONES = reg(CB + 7168, 256, BF16)
    TRI = reg(CB + 7424, 256, BF16)
    KM = reg(CB + 7680, 4096, BF16).rearrange("p (c m) -> p c m", c=8)
    VM = reg(CB + 11776, 4096, BF16).rearrange("p (b n) -> p b n", b=2)
    SC = reg(CB + 15872, NSC * 4)

    def scr(off, nbytes, dt=F32):
        assert off + nbytes <= 64 * K, (off, nbytes)
        return reg(SCR + off, nbytes, dt)

    XN = scr(0, 32 * K, BF16).rearrange("p (c t) -> p c t", c=8)
    AT = scr(32 * K, 24 * K, BF16).rearrange("p (f t) -> p f t", f=6)
    SQ = [scr(56 * K + i * K, K, BF16) for i in range(3)]
    RSTD = [scr(59 * K, 2 * K)]
    QNT = scr(61 * K, 2 * K, BF16).rearrange("p (c t) -> p c t", c=2)
    SG = [scr(63 * K, K, BF16)]

    bank_ctr = [0]

    def nb(lo=0, hi=8):
        b = lo + bank_ctr[0] % (hi - lo)
        bank_ctr[0] += 1
        return b

    def mm(out, lhsT, rhs, start, stop, reads, writes):
        return S.add("pe", lambda e: e.matmul(out, lhsT=lhsT, rhs=rhs, start=start, stop=stop), reads, writes)

    def act(out, in_, func, reads, writes, **kw):
        return S.add("act", lambda e: e.activation(out=out, in_=in_, func=func, **kw), reads, writes)

    def tt(out, in0, in1, op, reads, writes, eng="dve"):
        return S.add(eng, lambda e: e.tensor_tensor(out=out, in0=in0, in1=in1, op=op), reads, writes)

    def ts(out, in0, s1, s2, op0, op1, reads, writes, eng="dve"):
        if op1 is None:
            return S.add(eng, lambda e: e.tensor_scalar(out=out, in0=in0, scalar1=s1, scalar2=None, op0=op0), reads, writes)
        return S.add(eng, lambda e: e.tensor_scalar(out=out, in0=in0, scalar1=s1, scalar2=s2, op0=op0, op1=op1), reads, writes)

    def stt(out, in0, scalar, in1, op0, op1, reads, writes, eng="dve"):
        return S.add(eng, lambda e: e.scalar_tensor_tensor(out=out, in0=in0, scalar=scalar, in1=in1, op0=op0, op1=op1), reads, writes)

    def cp(out, in_, reads, writes, eng="dve"):
        if eng == "act":
            return S.add("act", lambda e: e.copy(out=out, in_=in_), reads, writes)
        return S.add(eng, lambda e: e.tensor_copy(out=out, in_=in_), reads, writes)

    def dma(q, out, in_, reads, writes, slot):
        return S.add(q, lambda e: e.dma_start(out=out, in_=in_), reads, writes, dma=True, slot=slot)

    BPS = [B("ps", i) for i in range(8)]
    BC = B("c_sc")
    BONES = B("c_ones")
    BTRI = B("c_tri")

    sq_ctr = [0]

    def emit_norm(srcs, src_bufs, N, gcol, dtot, dsts, dst_bufs):
        C = len(srcs)
        b = nb()
        for c in range(C):
            sl = sq_ctr[0] % 3
            sq_ctr[0] += 1
            act(SQ[sl][:, 0:N], srcs[c], AF.Square, [src_bufs[c]], [B("sq", sl)])
            mm(ps[:, b, 0:N], ONES, SQ[sl][:, 0:N], c == 0, c == C - 1, [B("sq", sl), BONES], [BPS[b]])
        act(RSTD[0][:, 0:N], ps[:, b, 0:N], AF.Ln, [BPS[b]], [B("rstd", 0)], scale=1.0 / dtot, bias=EPS)
        act(RSTD[0][:, 0:N], RSTD[0][:, 0:N], AF.Exp, [B("rstd", 0)], [B("rstd", 0)], scale=-0.5)
        for c in range(C):
            stt(dsts[c], srcs[c], SC[:, gcol + c:gcol + c + 1], RSTD[0][:, 0:N], ALU.mult, ALU.mult,
                [src_bufs[c], B("rstd", 0), BC], [dst_bufs[c]])

    def tsl(t):
        return slice(t * 512, (t + 1) * 512)

    def BH(t, m):
        return B("H", t, m)

    dma("sp", SC, smallc_d, [], [BC], "c_sc")
    S.add("dve", lambda e: e.memset(ONES, 1.0), [], [BONES])
    dma("pool", TRI, tri_d, [], [BTRI], "c_tri")
    dma("pool", reg(CB, 4096, BF16), wq_d, [], [B("wq")], "c_wq")
    dma("pool", WKV, wkv_d, [], [B("wkv")], "c_wkv")
    dma("pool", reg(CB + 6144, 1024, BF16), poolw_d, [], [B("poolw")], "c_pw")
    dma("pool", KR[64:65, 0:2048], kbias_d[:, 0:2048], [], [B("krb")], "c_kb")
    dma("pool", KR[64:65, 2048:4096], kbias_d[:, 2048:4096], [], [B("krb")], "c_kb")
    for h in range(4):
        S.add("dve", lambda e, h=h: e.memset(QPE[64:65, h, :], 1.0), [], [B("qpeb")])

    wgu_ctr = [0]
    wd_ctr = [0]

    def emit_ffn(which, gcol, x_dram=None):
        for t in range(NT):
            if x_dram is not None:
                dma("sp", H[:, :, tsl(t)], x_dram[:, :, tsl(t)], [], [BH(t, m) for m in range(8)], ("H", t))
            emit_norm([H[:, c, tsl(t)] for c in range(8)], [BH(t, c) for c in range(8)], 512, gcol, D,
                      [XN[:, c, tsl(t)] for c in range(8)], [B("xn", t)] * 8)
        for j, (f0, f1) in enumerate(PASSES):
            nf = f1 - f0
            for f in range(f0, f1):
                sl = wgu_ctr[0] % 2
                wgu_ctr[0] += 1
                dma("pool", WGUf[sl], wgu_d[which][f], [], [B("wgu", sl)], ("wgu", sl))
                for t in range(NT):
                    bg = nb()
                    for kc in range(8):
                        mm(ps[:, bg, :], WGU[sl][:, 0, kc, :], XN[:, kc, tsl(t)], kc == 0, kc == 7,
                           [B("wgu", sl), B("xn", t)], [BPS[bg]])
                    bu = nb()
                    for kc in range(8):
                        mm(ps[:, bu, :], WGU[sl][:, 1, kc, :], XN[:, kc, tsl(t)], kc == 0, kc == 7,
                           [B("wgu", sl), B("xn", t)], [BPS[bu]])
                    act(SG[0], ps[:, bg, :], AF.Silu, [BPS[bg]], [B("sg")])
                    tt(AT[:, f - f0, tsl(t)], ps[:, bu, :], SG[0], ALU.mult, [BPS[bu], B("sg")], [B("at", f - f0, t)])
            for m in range(8):
                sl = wd_ctr[0] % 3
                wd_ctr[0] += 1
                dma("pool", WDf[sl][:, 0:nf * 128], wd_d[which][j, m][:, 0:nf * 128], [], [B("wd", sl)], ("wd", sl))
                for t in range(NT):
                    b = nb()
                    for fl in range(nf):
                        mm(ps[:, b, :], WD[sl][:, fl, :], AT[:, fl, tsl(t)], fl == 0, fl == nf - 1,
                           [B("wd", sl), B("at", fl, t)], [BPS[b]])
                    stt(H[:, m, tsl(t)], ps[:, b, :], 0.5, H[:, m, tsl(t)], ALU.mult, ALU.add,
                        [BPS[b], BH(t, m)], [BH(t, m)])

    UN = [scr(i * 8 * K, 8 * K, BF16).rearrange("p (c t) -> p c t", c=8) for i in range(2)]
    ZP = scr(16 * K, 4 * 528 * 4).rearrange("p (g t) -> p g t", g=4)
    o = 16 * K + 8448
    TMP = [scr(o + i * 2112, 2112) for i in range(2)]
    o += 2 * 2112
    DY = scr(o, 4 * K, BF16).rearrange("p (g t) -> p g t", g=4)
    o += 4 * K
    WOPf = scr(o, 8 * K, BF16)
    WOP = WOPf.rearrange("p (c n) -> p c n", c=4)
    o += 8 * K
    CT = scr(o, 2 * K)
    STb = scr(o + 2 * K, 2 * K)
    o += 4 * K
    POSI = scr(o, 2 * K, I32)
    ANG = scr(o + 2 * K, 2 * K)
    RS = scr(o + 4 * K, 2 * K)
    o += 6 * K
    T1 = scr(o, 2 * K)
    T2 = ANG
    o += 2 * K
    RSTD.append(scr(o, 2 * K))
    o += 2 * K
    TS16 = scr(o, 64)
    o += 64
    assert o <= 56 * K, o

    class PA:
        def __init__(self):
            self.free = list(range(8))

        def alloc(self):
            assert self.free, "PSUM banks exhausted"
            return self.free.pop(0)

        def release(self, b):
            self.free.append(b)

    pa = PA()

    def norm_g(srcs, src_bufs, N, gcol, dtot, dsts, dst_bufs, ri, rel=()):
        C = len(srcs)
        b = pa.alloc()
        for c in range(C):
            sl = sq_ctr[0] % 3
            sq_ctr[0] += 1
            act(SQ[sl][:, 0:N], srcs[c], AF.Square, [src_bufs[c]], [B("sq", sl)])
            mm(ps[:, b, 0:N], ONES, SQ[sl][:, 0:N], c == 0, c == C - 1, [B("sq", sl), BONES], [BPS[b]])
            yield
        rb = B("rstd", ri)
        act(RSTD[ri][:, 0:N], ps[:, b, 0:N], AF.Ln, [BPS[b]], [rb], scale=1.0 / dtot, bias=EPS)
        act(RSTD[ri][:, 0:N], RSTD[ri][:, 0:N], AF.Exp, [rb], [rb], scale=-0.5)
        pa.release(b)
        yield
        for c in range(C):
            stt(dsts[c], srcs[c], SC[:, gcol + c:gcol + c + 1], RSTD[ri][:, 0:N], ALU.mult, ALU.mult,
                [src_bufs[c], rb, BC], [dst_bufs[c]])
            yield
        for b_ in rel:
            pa.release(b_)

    def tables_g(pos_d, t):
        dma("sp", POSI[0:64, :], pos_d[:, tsl(t)], [], [B("posi")], "posi")
        cp(ANG[0:64, :], POSI[0:64, :], [B("posi")], [B("ang")])
        ts(ANG[0:64, :], ANG[0:64, :], SC[0:64, C_FREQ:C_FREQ + 1], None, ALU.mult, None, [B("ang"), BC], [B("ang")])
        yield
        KI = POSI
        for (dst, db, off, use_sgn) in ((STb, "st", 0.5, True), (CT, "ct", 0.75, False)):
            ts(RS[0:64, :], ANG[0:64, :], 1.0 / (2.0 * PI), off, ALU.mult, ALU.add, [B("ang")], [B("rs")])
            cp(KI[0:64, :], RS[0:64, :], [B("rs")], [B("posi")])
            yield
            cp(T1[0:64, :], KI[0:64, :], [B("posi")], [B("t1")])
            tt(RS[0:64, :], RS[0:64, :], T1[0:64, :], ALU.subtract, [B("rs"), B("t1")], [B("rs")])
            yield
            stt(RS[0:64, :], RS[0:64, :], 0.0, RS[0:64, :], ALU.is_lt, ALU.add, [B("rs")], [B("rs")])
            if use_sgn:
                act(dst[0:64, :], RS[0:64, :], AF.Sin, [B("rs"), BC], [B(db)],
                    scale=SC[0:64, C_S2PI:C_S2PI + 1], bias=SC[0:64, C_SPI:C_SPI + 1])
            else:
                act(dst[0:64, :], RS[0:64, :], AF.Sin, [B("rs")], [B(db)], scale=2.0 * PI, bias=-PI)
            yield

    def rope_ops(za_b, zb_b, dst, dst_buf):
        tt(T1[0:64, :], ps[0:64, za_b, :], CT[0:64, :], ALU.mult, [BPS[za_b], B("ct")], [B("t1")])
        tt(T2[0:64, :], ps[0:64, zb_b, :], STb[0:64, :], ALU.mult, [BPS[zb_b], B("st")], [B("ang")])
        pa.release(za_b)
        pa.release(zb_b)
        tt(dst, T1[0:64, :], T2[0:64, :], ALU.add, [B("t1"), B("ang")], [dst_buf], eng="pool")

    def proj_t(t, c0, c1, part=128):
        u, ub = UN[t % 2], B("un", t % 2)
        b = pa.alloc()
        for kc in range(8):
            mm(ps[0:part, b, :], W16[:, kc, c0:c1], u[:, kc, :], kc == 0, kc == 7, [B("w16"), ub], [BPS[b]])
        return b

    def p1_g(t):
        u, ub = UN[t % 2], B("un", t % 2)
        yield from norm_g([H[:, c, tsl(t)] for c in range(8)], [BH(t, c) for c in range(8)], 512, C_MIX, D,
                          [u[:, c, :] for c in range(8)], [ub] * 8, 0)

    def p2_g(t, own):
        kt = (4 if own else 0) + t
        ksl = slice(kt * 512, (kt + 1) * 512)
        tg = tables_g(pos_own if own else pos_oth, t)
        next(tg)
        b = proj_t(t, 256, 384)
        yield
        ng = norm_g([ps[:, b, :]], [BPS[b]], 512, C_KVN, 128, [KVN[:, ksl]], [B("kvn", kt)], 1, rel=(b,))
        for _ in ng:
            next(tg, None)
            yield
        for _ in tg:
            yield
        ba = proj_t(t, 384, 448, 64)
        yield
        bb = proj_t(t, 960, 1024, 64)
        yield
        if own:
            bq = [proj_t(t, 0, 128)]
            yield
            bq.append(proj_t(t, 128, 256))
            yield
        rope_ops(ba, bb, KR[0:64, ksl], B("kr", kt))
        yield
        if own:
            yield from norm_g([ps[:, bq[0], :], ps[:, bq[1], :]], [BPS[bq[0]], BPS[bq[1]]], 512, C_QN, 256,
                              [QNT[:, 0, :], QNT[:, 1, :]], [B("qnt")] * 2, 1, rel=tuple(bq))
            for h in range(4):
                b = pa.alloc()
                for kc in range(2):
                    mm(ps[:, b, :], WQ[:, kc, h * 256:h * 256 + 128], QNT[:, kc, :], kc == 0, kc == 1, [B("wq"), B("qnt")], [BPS[b]])
                ba = pa.alloc()
                for kc in range(2):
                    mm(ps[0:64, ba, :], WQ[:, kc, h * 256 + 128:h * 256 + 192], QNT[:, kc, :], kc == 0, kc == 1, [B("wq"), B("qnt")], [BPS[ba]])
                bb = pa.alloc()
                for kc in range(2):
                    mm(ps[0:64, bb, :], WQ[:, kc, h * 256 + 192:h * 256 + 256], QNT[:, kc, :], kc == 0, kc == 1, [B("wq"), B("qnt")], [BPS[bb]])
                yield
                cp(QNP[:, h, tsl(t)], ps[:, b, :], [BPS[b]], [B("qnp", h, t)], eng="act")
                pa.release(b)
                rope_ops(ba, bb, QPE[0:64, h, tsl(t)], B("qpe", h, t))
                yield

    def p3_g(t, own):
        for g in range(4):
            b = proj_t(t, 448 + g * 128, 576 + g * 128)
            yield
            if not own:
                ts(HALO[:, g, :], ps[:, b, 496:512], SC[:, C_HALO:C_HALO + 1], None, ALU.mult, None,
                   [BPS[b], BC], [B("halo")])
                pa.release(b)
                continue
            if t == 0:
                cp(ZP[:, g, 0:16], HALO[:, g, :], [B("halo"), B("qnp", 3, 3)], [B("zp", g)], eng="pool")
            cp(ZP[:, g, 16:528], ps[:, b, :], [BPS[b]], [B("zp", g)], eng="act")
            pa.release(b)
            yield
            cur, curb = ZP[:, g, :], B("zp", g)
            lo = 0
            for l in range(g + 1):
                step = 1 << l
                lo += step
                dst, dstb = TMP[l % 2], B("tmp", l % 2)
                tt(dst[:, lo:528], cur[:, lo:528], cur[:, lo - step:528 - step], ALU.add, [curb], [dstb], eng="pool")
                cur, curb = dst, dstb
                yield
            w = 1 << (g + 1)
            if t == 0:
                tt(TS16[:, 0:16], cur[:, 16:32], SC[:, C_INVC + g * 16:C_INVC + (g + 1) * 16], ALU.mult, [curb, BC], [B("ts16")], eng="pool")
            stt(DY[:, g, :], cur[:, 16:528], 1.0 / w, ZP[:, g, 16:528], ALU.mult, ALU.subtract,
                [curb, B("zp", g)], [B("dy", g)])
            if t == 0:
                tt(DY[:, g, 0:16], TS16[:, 0:16], ZP[:, g, 16:32], ALU.subtract, [B("ts16"), B("zp", g)], [B("dy", g)], eng="pool")
            if t < NT - 1:
                cp(ZP[:, g, 0:16], ZP[:, g, 512:528], [B("zp", g)], [B("zp", g)], eng="pool")
            yield
            b2 = pa.alloc()
            mm(ps[:, b2, :], POOLW[:, g, :], DY[:, g, :], True, True, [B("poolw"), B("dy", g)], [BPS[b2]])
            ts(DY[:, g, :], ps[:, b2, :], SC[:, C_PSC + g:C_PSC + g + 1], None, ALU.mult, None, [BPS[b2], BC], [B("dy", g)])
            pa.release(b2)
            yield
        if own:
            for m in range(8):
                b = pa.alloc()
                for g in range(4):
                    mm(ps[:, b, :], WOP[:, g, m * 128:(m + 1) * 128], DY[:, g, :], g == 0, g == 3, [B("wop"), B("dy", g)], [BPS[b]])
                tt(H[:, m, tsl(t)], ps[:, b, :], H[:, m, tsl(t)], ALU.add, [BPS[b], BH(t, m)], [BH(t, m)])
                pa.release(b)
                yield

    def interleave(*gens):
        gens = [g for g in gens if g is not None]
        while gens:
            for g in list(gens):
                try:
                    next(g)
                except StopIteration:
                    gens.remove(g)

    MEMF = scr(16 * K, 8 * K).rearrange("p (c m) -> p c m", c=8)
    MEMN = scr(24 * K, 4 * K, BF16).rearrange("p (c m) -> p c m", c=8)
    WMf = [scr(28 * K + i * 4 * K, 4 * K, BF16) for i in range(2)]
    WM = [w.rearrange("p (c n) -> p c n", c=8) for w in WMf]
    RSTD.append(scr(36 * K, K))
    RI_P0 = len(RSTD) - 1

    def p0_g():
        dma("sp", MEMF, memT, [], [B("memf")], "memf")
        yield from norm_g([MEMF[:, c, :] for c in range(8)], [B("memf")] * 8, 256, C_MEM, D,
                          [MEMN[:, c, :] for c in range(8)], [B("memn")] * 8, RI_P0)
        for cg in range(8):
            sl = cg % 2
            dma("pool", WMf[sl], w_mkv_d[cg], [], [B("wm", sl)], ("wm", sl))
            for half in range(2):
                b = pa.alloc()
                if cg < 4:
                    c = cg * 2 + half
                    for kc in range(8):
                        mm(ps[:, b, 0:256], WM[sl][:, kc, half * 128:(half + 1) * 128], MEMN[:, kc, :], kc == 0, kc == 7,
                           [B("wm", sl), B("memn")], [BPS[b]])
                    cp(KM[:, c, :], ps[:, b, 0:256], [BPS[b]], [B("km")])
                else:
                    for kc in range(8):
                        mm(ps[:, b, 0:256], MEMN[:, kc, half * 128:(half + 1) * 128], WM[sl][:, kc, :], kc == 0, kc == 7,
                           [B("wm", sl), B("memn")], [BPS[b]])
                    cp(VM[:, half, (cg - 4) * 256:(cg - 3) * 256], ps[:, b, 0:256], [BPS[b]], [B("vm")])
                pa.release(b)
                yield

    def take_g(it, n):
        for _ in range(n):
            try:
                next(it)
            except StopIteration:
                return
            yield

    def emit_inproj_phase(own):
        extra = None if own else p0_g()
        interleave(p1_g(0))
        for t in range(NT):
            streams = [p2_g(t, own)]
            if own or t == NT - 1:
                streams.append(p3_g(t, own))
            if t + 1 < NT:
                streams.append(p1_g(t + 1))
            if extra is not None:
                streams.append(take_g(extra, 12))
            interleave(*streams)
        if extra is not None:
            interleave(extra)

    fence()
    emit_ffn(0, C_FFN1, x_oth)
    fence()
    dma("pool", W16f, w_in_d, [], [B("w16")], "w16")
    emit_inproj_phase(False)
    if stop != "A":
        fence()
        emit_ffn(0, C_FFN1, x_own)
    if stop not in ("A", "B1"):
        fence()
        dma("pool", WOPf, w_out_d[:, 4096:8192], [], [B("wop")], "wop")
        emit_inproj_phase(True)

    if stop not in ("A", "B1", "B2"):
        fence()
        KNb = [scr(i * 8 * K, 8 * K, BF16) for i in range(2)]
        Vb = [scr(16 * K + i * 8 * K, 8 * K, BF16).rearrange("p (b n) -> p b n", b=32) for i in range(2)]
        PT = [scr(32 * K + i * K, K, BF16) for i in range(4)] + [scr(56 * K + i * K, K, BF16) for i in range(4)]
        NPT = len(PT)
        ACCV = {"dve": scr(52 * K, 2 * K), "pool": scr(54 * K, 2 * K)}
        ACCB = {"dve": "accd", "pool": "accp"}
        ONES32 = scr(60 * K, 512)
        SUMT = scr(61 * K, 2 * K)
        S.add("dve", lambda e: e.memset(ONES32, 1.0), [], [B("ones32")])
        RC = [scr(36 * K + i * 2 * K, 2 * K) for i in range(2)]
        OH = [scr(40 * K + i * 4 * K, 4 * K, BF16) for i in range(2)]
        WOH = [scr(48 * K + i * 2 * K, 2 * K, BF16) for i in range(2)]
        sc_attn = float(192.0 ** -0.5)

        pending = []

        def flush():
            while pending:
                pending.pop(0)()

        def emit_upproj(h, now=False):
            hs = h % 2
            items = []
            items.append(lambda: dma("pool", WOH[hs], w_out_d[:, h * 1024:(h + 1) * 1024], [], [B("woh", hs)], ("woh", hs)))

            def kgrp(kt):
                b = nb(0, 4)
                mm(ps[:, b, :], WKV[:, h * 256:h * 256 + 128], KVN[:, kt * 512:(kt + 1) * 512], True, True,
                   [B("wkv"), B("kvn", kt)], [BPS[b]])
                cp(KNb[hs][:, kt * 512:(kt + 1) * 512], ps[:, b, :], [BPS[b]], [B("kn", hs, kt)])

            def vgrp(vb):
                b = nb(0, 4)
                for j in range(4):
                    kb = vb * 4 + j
                    mm(ps[:, b, j * 128:(j + 1) * 128], KVN[:, kb * 128:(kb + 1) * 128], WKV[:, h * 256 + 128:h * 256 + 256],
                       True, True, [B("wkv"), B("kvn", kb // 4)], [BPS[b]])
                cp(Vb[hs][:, vb * 4:(vb + 1) * 4, :], ps[:, b, :].rearrange("p (j n) -> p j n", j=4), [BPS[b]], [B("v", hs, vb)])

            for kt in range(8):
                items.append(lambda kt=kt: kgrp(kt))
            for vb in range(8):
                items.append(lambda vb=vb: vgrp(vb))
            if now:
                for it in items:
                    it()
            else:
                pending.extend(items)

        def emit_wout(h, t):
            hs = h % 2

            def grp(m):
                b = nb(0, 4)
                mm(ps[:, b, :], WOH[hs][:, m * 128:(m + 1) * 128], OH[hs][:, tsl(t)], True, True,
                   [B("woh", hs), B("oh", hs, t)], [BPS[b]])
                tt(H[:, m, tsl(t)], ps[:, b, :], H[:, m, tsl(t)], ALU.add, [BPS[b], BH(t, m)], [BH(t, m)])

            for m in range(8):
                pending.append(lambda m=m: grp(m))

        emit_upproj(0, now=True)
        units_all = []
        acc = 0
        for h in range(4):
            for t in range(NT):
                ul = [(kb, 0, False) for kb in range(16)]
                for j in range(4 * t + 4):
                    ul.append((16 + j, max(0, j - 4 * t), j >= 4 * t))
                for idx, (kb, r, diag) in enumerate(ul):
                    units_all.append(dict(h=h, t=t, idx=idx, kb=kb, r=r, diag=diag, first=(idx == 0),
                                          last=(idx == len(ul) - 1), acc=acc))
                acc += 1
        pt_ctr = [0]

        def stage_a(u):
            h, t, kb = u["h"], u["t"], u["kb"]
            hs = h % 2
            if u["idx"] == 0 and t == 0:
                flush()
            if u["idx"] == 4 and t == 1 and h < 3:
                emit_upproj(h + 1)
            if u["idx"] >= 3 and pending:
                pending.pop(0)()
            n0 = u["r"] * 128
            b = nb(0, 4)
            pi = pt_ctr[0] % NPT
            pt_ctr[0] += 1
            u["pi"] = pi
            qs = slice(t * 512 + n0, (t + 1) * 512)
            mm(ps[:, b, n0:512], KNb[hs][:, kb * 128:(kb + 1) * 128], QNP[:, h, qs], True, False,
               [B("kn", hs, kb // 4), B("qnp", h, t)], [BPS[b]])
            mm(ps[:, b, n0:512], KR[0:65, kb * 128:(kb + 1) * 128], QPE[0:65, h, qs], False, True,
               [B("kr", kb // 4), B("krb"), B("qpe", h, t), B("qpeb")], [BPS[b]])
            act(PT[pi][:, n0:512], ps[:, b, n0:512], AF.Exp, [BPS[b]], [B("pt", pi)], scale=sc_attn)
            if u["diag"]:
                tt(PT[pi][:, n0:n0 + 128], PT[pi][:, n0:n0 + 128], TRI, ALU.mult, [B("pt", pi), BTRI], [B("pt", pi)])

        def stage_b(u):
            h, t, kb, pi = u["h"], u["t"], u["kb"], u["pi"]
            hs = h % 2
            n0 = u["r"] * 128
            ob, sb, rcb = 4 + u["acc"] % 2, 6 + u["acc"] % 2, u["acc"] % 2
            mm(ps[:, ob, n0:512], Vb[hs][:, kb, :], PT[pi][:, n0:512], u["first"], u["last"], [B("v", hs, kb // 4), B("pt", pi)], [BPS[ob]])
            role = ("pool", "dve", "pe", "pool")[u["idx"] % 4]
            if role == "pe":
                mm(ps[:, sb, n0:512], ONES, PT[pi][:, n0:512], u["idx"] == 2, False, [BONES, B("pt", pi)], [BPS[sb]])
            else:
                av, ab = ACCV[role], B(ACCB[role])
                if u["idx"] < 2:
                    cp(av, PT[pi], [B("pt", pi)], [ab], eng=role)
                else:
                    tt(av[:, n0:512], av[:, n0:512], PT[pi][:, n0:512], ALU.add, [ab, B("pt", pi)], [ab], eng=role)
            if u["last"]:
                tt(SUMT, ACCV["dve"], ACCV["pool"], ALU.add, [B("accd"), B("accp")], [B("sumt")])
                mm(ps[:, sb, :], ONES32, SUMT, False, True, [B("ones32"), B("sumt")], [BPS[sb]])
                act(RC[rcb], ps[:, sb, :], AF.Ln, [BPS[sb]], [B("rc", rcb)])
                act(RC[rcb], RC[rcb], AF.Exp, [B("rc", rcb)], [B("rc", rcb)], scale=-1.0)
                tt(OH[hs][:, tsl(t)], ps[:, ob, :], RC[rcb], ALU.mult, [BPS[ob], B("rc", rcb)], [B("oh", hs, t)])
                emit_wout(h, t)

        LOOK = 2
        for i in range(len(units_all) + LOOK):
            if i < len(units_all):
                stage_a(units_all[i])
            if i >= LOOK:
                stage_b(units_all[i - LOOK])
        flush()

    if stop not in ("A", "B1", "B2", "B3"):
        fence()
        WMOf = reg(144 * K, 16 * K, BF16)
        WMO = WMOf.rearrange("p (c n) -> p c n", c=8)
        kv_dead = [B("kvn", kt) for kt in range(8)] + [B("kr", kt) for kt in range(8)] + [B("krb")]
        dma("pool", W16f, w_mq_d, [], [B("w16")], "w16")
        dma("pool", WMOf, w_mo_d, [], kv_dead, "wmo")
        HN = [scr(i * 8 * K, 8 * K, BF16).rearrange("p (c t) -> p c t", c=8) for i in range(2)]
        QM = [scr(16 * K + i * 8 * K, 8 * K, BF16).rearrange("p (c t) -> p c t", c=8) for i in range(2)]
        OM = [scr(32 * K + i * 8 * K, 8 * K, BF16).rearrange("p (c t) -> p c t", c=8) for i in range(2)]
        PTm = [scr(48 * K + i * K, K, BF16) for i in range(4)]
        RCm = [scr(52 * K + i * 2 * K, 2 * K) for i in range(2)]
        sc_mem = float(256.0 ** -0.5)

        def xa_norm_g(t):
            hn, hb = HN[t % 2], B("hn", t % 2)
            yield from norm_g([H[:, c, tsl(t)] for c in range(8)], [BH(t, c) for c in range(8)], 512, C_XAT, D,
                              [hn[:, c, :] for c in range(8)], [hb] * 8, 0)

        def xa_q_g(t):
            hn, hb = HN[t % 2], B("hn", t % 2)
            for c in range(8):
                b = pa.alloc()
                for kc in range(8):
                    mm(ps[:, b, :], W16[:, kc, c * 128:(c + 1) * 128], hn[:, kc, :], kc == 0, kc == 7, [B("w16"), hb], [BPS[b]])
                cp(QM[t % 2][:, c, :], ps[:, b, :], [BPS[b]], [B("qm", t % 2, c)], eng=("act" if c % 2 else "dve"))
                pa.release(b)
                yield

        def xa_heads_g(t):
            qm, om = QM[t % 2], OM[t % 2]

            def stage_a(h):
                pis = []
                for mb in range(2):
                    b = pa.alloc()
                    for dc in range(2):
                        mm(ps[:, b, :], KM[:, 2 * h + dc, mb * 128:(mb + 1) * 128], qm[:, 2 * h + dc, :], dc == 0, dc == 1,
                           [B("km"), B("qm", t % 2, 2 * h + dc)], [BPS[b]])
                    pi = (2 * h + mb) % 4
                    pis.append(pi)
                    act(PTm[pi], ps[:, b, :], AF.Exp, [BPS[b]], [B("ptm", pi)], scale=sc_mem)
                    pa.release(b)
                return pis

            pis = {0: stage_a(0)}
            yield
            for h in range(4):
                if h < 3:
                    pis[h + 1] = stage_a(h + 1)
                    yield
                p = pis[h]
                b = pa.alloc()
                for mb in range(2):
                    mm(ps[:, b, :], ONES, PTm[p[mb]], mb == 0, mb == 1, [BONES, B("ptm", p[mb])], [BPS[b]])
                rcb = h % 2
                act(RCm[rcb], ps[:, b, :], AF.Ln, [BPS[b]], [B("rcm", rcb)])
                act(RCm[rcb], RCm[rcb], AF.Exp, [B("rcm", rcb)], [B("rcm", rcb)], scale=-1.0)
                pa.release(b)
                for dvc in range(2):
                    b = pa.alloc()
                    for mb in range(2):
                        mm(ps[:, b, :], VM[:, mb, h * 256 + dvc * 128:h * 256 + (dvc + 1) * 128], PTm[p[mb]], mb == 0, mb == 1,
                           [B("vm"), B("ptm", p[mb])], [BPS[b]])
                    tt(om[:, 2 * h + dvc, :], ps[:, b, :], RCm[rcb], ALU.mult, [BPS[b], B("rcm", rcb)], [B("om", t % 2, 2 * h + dvc)])
                    pa.release(b)
                yield
        def xa_o_g(t):
            om = OM[t % 2]
            for m in range(8):
                b = pa.alloc()
                for kc in range(8):
                    mm(ps[:, b, :], WMO[:, kc, m * 128:(m + 1) * 128], om[:, kc, :], kc == 0, kc == 7,
                       [kv_dead[0], B("om", t % 2, kc)], [BPS[b]])
                tt(H[:, m, tsl(t)], ps[:, b, :], H[:, m, tsl(t)], ALU.add, [BPS[b], BH(t, m)], [BH(t, m)])
                pa.release(b)
                yield

        def chain_g(*gs):
            for g_ in gs:
                if g_ is not None:
                    yield from g_

        interleave(xa_norm_g(0))
        interleave(xa_q_g(0), xa_norm_g(1))
        for t in range(NT):
            interleave(xa_heads_g(t),
                       xa_o_g(t - 1) if t > 0 else None,
                       chain_g(xa_norm_g(t + 2) if t + 2 < NT else None, xa_q_g(t + 1) if t + 1 < NT else None))
        interleave(xa_o_g(NT - 1))

    finals = []
    if stop is None:
        S.add("dve", lambda e: e.engine_nop(), [],
              [B("qnp", h, t) for h in range(4) for t in range(NT)] + [B("wgu", sl) for sl in range(2)] + [B("wd", sl) for sl in range(3)])
        fence()
        emit_ffn(1, C_FFN2)
        fence()
        OUTB = [scr(i * 16 * K, 16 * K).rearrange("p (c t) -> p c t", c=8) for i in range(2)]
        for t in range(NT):
            ob_, obb = OUTB[t % 2], B("outb", t % 2)
            emit_norm([H[:, c, tsl(t)] for c in range(8)], [BH(t, c) for c in range(8)], 512, C_FIN, D,
                      [ob_[:, c, :] for c in range(8)], [obb] * 8)
            finals.append(dma("sp", outT[:, :, tsl(t)], ob_, [obb], [], ("out", t % 2)))

    dump_specs = {
        "H": (H, [128, 8, T], F32),
        "KVN": (KVN, [128, 4096], BF16),
        "KR": (KR[0:65, :], [65, 4096], BF16),
        "QNP": (QNP, [128, 4, T], BF16),
        "QPE": (QPE[0:65], [65, 4, T], BF16),
        "KM": (KM, [128, 8, 256], BF16),
        "VM": (VM, [128, 2, 1024], BF16),
        "ZPH": (ZP[:, :, 0:16], [128, 4, 16], F32),
    }
    for name in dumps:
        ap, shape, dt = dump_specs[name]
        d = nc.dram_tensor("dbg_" + name, shape, dt, kind="ExternalOutput").ap()
        allb = list(bufs.values())
        finals.append(dma("sp", d, ap, allb, [], ("dbg", name)))

    S.emit(nc, final_waits=finals)
    st.close()
    return nc


def _fm(w):
    k, n = w.shape
    return np.ascontiguousarray(w.reshape(k // 128, 128, n).transpose(1, 0, 2)).reshape(128, -1)


def prepare_inputs(inp):
    f32 = np.float32
    x = np.asarray(inp["x"], f32)
    mem = np.asarray(inp["mem"], f32)
    pos = np.asarray(inp["positions"], np.int32)
    g = lambda k: np.asarray(inp[k], f32)

    def ffn_layout(wg, wu, wd):
        wg_r = wg.reshape(8, 128, NF, 128).transpose(2, 1, 0, 3)
        wu_r = wu.reshape(8, 128, NF, 128).transpose(2, 1, 0, 3)
        wgu = np.ascontiguousarray(np.stack([wg_r, wu_r], axis=2)).reshape(NF, 128, 2048)
        wd_r = wd.reshape(NF, 128, 8, 128)
        wdl = np.zeros((4, 8, 128, 6, 128), f32)
        for j, (f0, f1) in enumerate(PASSES):
            wdl[j, :, :, 0:f1 - f0, :] = wd_r[f0:f1].transpose(2, 1, 0, 3)
        return wgu, wdl.reshape(4, 8, 128, 768)

    wgu1, wd1 = ffn_layout(g("ffn1_w_gate")[0], g("ffn1_w_up")[0], g("ffn1_w_down")[0])
    wgu2, wd2 = ffn_layout(g("ffn2_w_gate")[0], g("ffn2_w_up")[0], g("ffn2_w_down")[0])
    w_in = g("w_in")[0]
    w_in_ext = np.concatenate([w_in, w_in[:, 416:448], w_in[:, 384:416]], axis=1)
    wqu = g("w_q_up")[0]
    wq_ext = np.zeros((256, 1024), f32)
    for h in range(4):
        o = h * 192
        wq_ext[:, h * 256:h * 256 + 192] = wqu[:, o:o + 192]
        wq_ext[:, h * 256 + 192:h * 256 + 224] = wqu[:, o + 160:o + 192]
        wq_ext[:, h * 256 + 224:h * 256 + 256] = wqu[:, o + 128:o + 160]
    w_mkv = g("w_mkv")[0]
    w_mkv_l = np.stack([_fm(w_mkv[:, cg * 256:(cg + 1) * 256]) for cg in range(8)])
    tri = (np.arange(128)[None, :] >= np.arange(128)[:, None]).astype(f32)
    freq = (1.0 / (np.float32(10000.0) ** (np.arange(0, 64, 2, dtype=f32) / np.float32(64)))).astype(f32)

    shared = {
        "wgu1": wgu1, "wd1": wd1, "wgu2": wgu2, "wd2": wd2,
        "w_in": _fm(w_in_ext), "wq": _fm(wq_ext), "wkv": np.ascontiguousarray(g("w_kv_up")[0]),
        "poolw": np.ascontiguousarray(g("pool_w")[0].transpose(1, 0, 2)).reshape(128, 512),
        "w_out": _fm(g("w_out")[0]), "w_mq": _fm(g("w_mq")[0]), "w_mo": _fm(g("w_mo")[0]),
        "w_mkv": w_mkv_l, "tri": tri,
    }
    sc0 = np.zeros((128, NSC), f32)
    for col, key in ((C_FFN1, "ffn1_norm"), (C_MIX, "mix_norm"), (C_XAT, "xattn_norm"), (C_FFN2, "ffn2_norm"), (C_MEM, "mem_norm")):
        sc0[:, col:col + 8] = g(key)[0].reshape(8, 128).T
    sc0[:, C_FIN:C_FIN + 8] = g("final_norm").reshape(8, 128).T
    sc0[:, C_QN:C_QN + 2] = g("q_norm")[0].reshape(2, 128).T
    sc0[:, C_KVN] = g("kv_norm")[0]
    sc0[:, C_PSC:C_PSC + 4] = g("pool_scale")[0].reshape(4, 128).T
    sc0[0:32, C_FREQ] = freq
    sc0[32:64, C_FREQ] = freq
    sc0[0:32, C_SGN] = -1.0
    sc0[32:64, C_SGN] = 1.0
    sc0[:, C_S2PI] = sc0[:, C_SGN] * np.float32(2.0 * PI)
    sc0[:, C_SPI] = -sc0[:, C_SGN] * np.float32(PI)

    in_maps = []
    for c in range(8):
        b, half = c // 2, c % 2
        own = slice(half * T, (half + 1) * T)
        oth = slice((1 - half) * T, (2 - half) * T)
        sc = sc0.copy()
        sc[:, C_HALO] = float(half)
        for gi, w in enumerate((2, 4, 8, 16)):
            for i in range(16):
                cntv = min(i + 1, w) if half == 0 else w
                sc[:, C_INVC + gi * 16 + i] = np.float32(1.0) / np.float32(cntv)
        kb = np.zeros((1, 4096), f32)
        if half == 0:
            kb[0, 0:T] = NEG
        m = dict(shared)
        m.update({
            "x_own": np.ascontiguousarray(x[b, own].T).reshape(8, 128, T).transpose(1, 0, 2).copy(),
            "x_oth": np.ascontiguousarray(x[b, oth].T).reshape(8, 128, T).transpose(1, 0, 2).copy(),
            "memT": np.ascontiguousarray(mem[b].T).reshape(8, 128, 256).transpose(1, 0, 2).copy(),
            "pos_own": np.ascontiguousarray(np.broadcast_to(pos[b, own][None, :], (64, T))),
            "pos_oth": np.ascontiguousarray(np.broadcast_to(pos[b, oth][None, :], (64, T))),
            "smallc": sc, "kbias": kb,
        })
        in_maps.append(m)
    return in_maps


_NC_CACHE = {}


def kernel(**inputs):
    in_maps = prepare_inputs(inputs)
    if "nc" not in _NC_CACHE:
        _NC_CACHE["nc"] = build()
    nc = _NC_CACHE["nc"]
    res = run_bass_kernel_spmd(nc, in_maps, core_ids=list(range(8)))
    out = np.empty((4, 4096, D), np.float32)
    for c in range(8):
        b, half = c // 2, c % 2
        o = np.asarray(res.results[c]["outT"], np.float32)
        out[b, half * T:(half + 1) * T, :] = o.transpose(2, 1, 0).reshape(T, D)
    return out
```
